# Optimizing a Trainium2 kernel written in Bass

```python
import math
import jax, jax.numpy as jnp
from jax import lax
import numpy as np

D_MODEL = 1024
BATCH = 8
SEQ = 2048
DEPTH = 2
DEC_BATCH = 32
DEC_SEQ = 4
PAST_LEN = 8192
PAGE_SIZE = 128

A_HEADS = 8
A_KV = 2
A_REP = A_HEADS // A_KV
A_HD = 64
CMP_BLOCK = 32
CMP_STRIDE = 16
SEL_BLOCK = 64
TOP_N = 16
WINDOW = 512
Q_BLOCK = 128
N_BUCKETS = 32
MAX_DIST = 2048
M_HEADS = 4
M_HD = 128
M_CHUNK = 64
G_HEADS = 4
G_HD = 128
G_CHUNK = 64
CONV_W = 4

W_A = A_HEADS * A_HD
W_M = M_HEADS * M_HD
W_G = G_HEADS * G_HD
N_BRANCH = 3
EPS = 1e-6
NEG = -1e30

IN_SPLITS = (
    ('a_q', W_A),
    ('a_kv', 3 * 2 * A_KV * A_HD),
    ('a_gate', A_HEADS * 3),
    ('a_z', W_A),
    ('m_qkv', 3 * W_M),
    ('m_if', 2 * M_HEADS),
    ('m_o', W_M),
    ('m_z', W_M),
    ('g_qkv', 3 * W_G),
    ('g_ab', 2 * G_HEADS),
    ('g_z', W_G),
    ('merge', N_BRANCH * D_MODEL),
)
N_IN = sum(w for _, w in IN_SPLITS)

kernel_name = 'hybrid_nsa_mlstm_gdn_step'


def rmsnorm(x, g):
    xf = x.astype(jnp.float32)
    y = xf * lax.rsqrt(jnp.mean(xf * xf, axis=-1, keepdims=True) + EPS)
    return (y * g.astype(jnp.float32)).astype(x.dtype)


def l2norm(x):
    return x * lax.rsqrt(jnp.sum(x * x, axis=-1, keepdims=True) + EPS)


def split_cols(u):
    parts, off = {}, 0
    for name, width in IN_SPLITS:
        parts[name] = u[..., off:off + width]
        off += width
    return parts


def rel_bucket(dist):
    n = jnp.maximum(dist, 0)
    exact = N_BUCKETS // 2
    nf = jnp.maximum(n, exact).astype(jnp.float32)
    large = exact + (jnp.log(nf / exact) / math.log(MAX_DIST / exact) * (N_BUCKETS - exact)).astype(jnp.int32)
    return jnp.where(n < exact, n, jnp.minimum(large, N_BUCKETS - 1))


def head_bias(rel_bias, dist):
    Q, K = dist.shape
    b = rel_bias[rel_bucket(dist)].astype(jnp.float32)
    return b.reshape(Q, K, A_KV, A_REP).transpose(2, 0, 3, 1)


def masked_softmax(s, valid):
    return jax.nn.softmax(jnp.where(valid, s, NEG), axis=-1) * valid


def gather_pages(pool, page_table):
    g = pool[page_table]
    return g.reshape((page_table.shape[0], -1) + pool.shape[2:])


def to_chunks(a, L):
    B, T = a.shape[:2]
    a = a.reshape((B, T // L, L) + a.shape[2:])
    return jnp.moveaxis(a, (1, 3), (0, 2))


def nsa_compress(x, w):
    B, Tk = x.shape[:2]
    ratio = CMP_BLOCK // CMP_STRIDE
    nch = -(-Tk // CMP_STRIDE)
    xc = jnp.pad(x, ((0, 0), (0, nch * CMP_STRIDE - Tk), (0, 0), (0, 0))).reshape(B, nch, CMP_STRIDE, A_KV, A_HD)
    wr = w.reshape(A_KV, ratio, CMP_STRIDE, A_HD, A_HD)
    nb = nch - ratio + 1
    out = 0
    for m in range(ratio):
        out = out + jnp.einsum('bnsgd,gsde->bnge', xc[:, m:m + nb], wr[:, m])
    return out


def nsa_attention(q, full_cmp, full_sel, win, w_off, gates, cmp_wk, cmp_wv, rel_bias, q_off):
    B, Tq = q.shape[:2]
    Tk = full_cmp.shape[1]
    f32 = jnp.float32
    kcmp = nsa_compress(full_cmp[:, :, 0], cmp_wk)
    vcmp = nsa_compress(full_cmp[:, :, 1], cmp_wv)
    nb = kcmp.shape[1]
    cmp_start = jnp.arange(nb) * CMP_STRIDE
    cmp_end = cmp_start + CMP_BLOCK - 1
    ns = -(-Tk // SEL_BLOCK)
    sel = jnp.pad(full_sel, ((0, 0), (0, ns * SEL_BLOCK - Tk), (0, 0), (0, 0), (0, 0)))
    sel = sel.reshape(B, ns, SEL_BLOCK, 2, A_KV, A_HD).transpose(0, 4, 1, 2, 3, 5)
    sel_start = jnp.arange(ns) * SEL_BLOCK
    cover = ((cmp_start[:, None] < sel_start[None, :] + SEL_BLOCK) & (sel_start[None, :] <= cmp_end[:, None])).astype(f32)
    n_top = min(TOP_N, ns)
    winp = jnp.pad(win, ((0, 0), (WINDOW, 0), (0, 0), (0, 0), (0, 0)))
    QB = math.gcd(Tq, Q_BLOCK)
    nqb = Tq // QB
    n_wk = WINDOW + QB
    qblk = q.reshape(B, nqb, QB, A_KV, A_REP, A_HD).transpose(1, 0, 3, 2, 4, 5)
    gblk = gates.reshape(B, nqb, QB, A_KV, A_REP, 3).transpose(1, 0, 3, 2, 4, 5)
    b_idx = jnp.arange(B)[:, None, None, None]
    g_idx = jnp.arange(A_KV)[None, :, None, None]
    table_g = rel_bias.reshape(N_BUCKETS, A_KV, A_REP).transpose(1, 0, 2)
    j_sel = jnp.arange(ns)

    def block(args):
        qb, gb, bi = args
        t0 = q_off + bi * QB
        t = t0 + jnp.arange(QB)
        dist = t[:, None] - cmp_end[None, :]
        s = jnp.einsum('bgqrd,bngd->bgqrn', qb, kcmp).astype(f32) + head_bias(rel_bias, dist)
        p_c = masked_softmax(s, (dist >= 0)[:, None, :])
        o_c = jnp.einsum('bgqrn,bngd->bgqrd', p_c, vcmp)
        imp = jnp.einsum('bgqrn,ns->bgqs', p_c, cover)
        cur = t // SEL_BLOCK
        forced = (j_sel[None] == 0) | (j_sel[None] == cur[:, None]) | (j_sel[None] == cur[:, None] - 1)
        future = sel_start[None, :] > t[:, None]
        score = jnp.where(future, NEG, jnp.where(forced, -NEG, imp))
        _, idx = lax.top_k(score, n_top)
        kv_sel = sel[b_idx, g_idx, idx]
        kpos = idx[..., None] * SEL_BLOCK + jnp.arange(SEL_BLOCK)
        dist = t[:, None, None] - kpos
        s = jnp.einsum('bgqrd,bgqnjd->bgqrnj', qb, kv_sel[..., 0, :]).astype(f32)
        bias = table_g[g_idx[..., None], rel_bucket(dist)]
        s = (s + jnp.moveaxis(bias, -1, 3).astype(f32)).reshape(B, A_KV, QB, A_REP, n_top * SEL_BLOCK)
        valid = (dist >= 0).reshape(B, A_KV, QB, 1, n_top * SEL_BLOCK)
        p_s = masked_softmax(s, valid)
        o_s = jnp.einsum('bgqrk,bgqkd->bgqrd', p_s, kv_sel[..., 1, :].reshape(B, A_KV, QB, n_top * SEL_BLOCK, A_HD))
        wkv = lax.dynamic_slice_in_dim(winp, t0 - w_off, n_wk, axis=1)
        wpos = t0 - WINDOW + jnp.arange(n_wk)
        dist = t[:, None] - wpos[None, :]
        valid = (dist >= 0) & (dist < WINDOW) & (wpos[None, :] >= w_off)
        s = jnp.einsum('bgqrd,bkgd->bgqrk', qb, wkv[:, :, 0]).astype(f32) + head_bias(rel_bias, dist)
        p_w = masked_softmax(s, valid[:, None, :])
        o_w = jnp.einsum('bgqrk,bkgd->bgqrd', p_w, wkv[:, :, 1])
        return gb[..., 0:1] * o_c + gb[..., 1:2] * o_s + gb[..., 2:3] * o_w

    out = lax.map(block, (qblk, gblk, jnp.arange(nqb)))
    return out.transpose(1, 0, 3, 2, 4, 5).reshape(B, Tq, W_A)


def mlstm_chunked(q, k, v, li, lf, C, n, m):
    B, T = q.shape[:2]
    L = math.gcd(T, M_CHUNK)
    causal = jnp.tril(jnp.ones((L, L), bool))

    def step(carry, xs):
        C, n, m = carry
        qc, kc, vc, lic, lfc = xs
        b = jnp.cumsum(lfc, axis=-1)
        Dm = jnp.where(causal, b[..., :, None] - b[..., None, :] + lic[..., None, :], -jnp.inf)
        a = b + m[..., None]
        mt = jnp.maximum(a, Dm.max(-1))
        S = jnp.einsum('bhtd,bhsd->bhts', qc, kc) * jnp.exp(Dm - mt[..., None])
        inter = jnp.exp(a - mt)
        num = inter[..., None] * jnp.einsum('bhtd,bhde->bhte', qc, C) + jnp.einsum('bhts,bhse->bhte', S, vc)
        den = inter * jnp.einsum('bhtd,bhd->bht', qc, n) + S.sum(-1)
        h = num / jnp.maximum(jnp.abs(den), jnp.exp(-mt))[..., None]
        bL = b[..., -1]
        wlog = bL[..., None] - b + lic
        m_new = jnp.maximum(bL + m, wlog.max(-1))
        ws = jnp.exp(wlog - m_new[..., None])
        dec = jnp.exp(bL + m - m_new)
        C = dec[..., None, None] * C + jnp.einsum('bhs,bhsd,bhse->bhde', ws, kc, vc)
        n = dec[..., None] * n + jnp.einsum('bhs,bhsd->bhd', ws, kc)
        return (C, n, m_new), h

    xs = tuple(to_chunks(a, L) for a in (q, k, v, li, lf))
    (C, n, m), hs = lax.scan(step, (C, n, m), xs)
    h = hs.transpose(1, 0, 3, 2, 4).reshape(B, T, q.shape[2], v.shape[-1])
    return h, C, n, m


def gdn_chunked(q, k, v, beta, g, S):
    B, T = q.shape[:2]
    L = math.gcd(T, G_CHUNK)
    dv = v.shape[-1]
    lower = jnp.tril(jnp.ones((L, L), bool))
    strict = jnp.tril(jnp.ones((L, L), jnp.float32), -1)
    eye = jnp.eye(L, dtype=jnp.float32)

    def step(S, xs):
        qc, kc, vc, bc, gc = xs
        G = jnp.cumsum(gc, axis=-1)
        dmask = jnp.exp(jnp.where(lower, G[..., :, None] - G[..., None, :], -jnp.inf))
        kb = kc * bc[..., None]
        A = jnp.einsum('bhid,bhjd->bhij', kb, kc) * dmask * strict
        rhs = jnp.concatenate([vc * bc[..., None], kb * jnp.exp(G)[..., None]], axis=-1)
        sol = lax.linalg.triangular_solve(eye + A, rhs, left_side=True, lower=True, unit_diagonal=True)
        u, w = sol[..., :dv], sol[..., dv:]
        v_new = u - jnp.einsum('bhid,bhde->bhie', w, S)
        attn = jnp.einsum('bhid,bhjd->bhij', qc, kc) * dmask
        o = jnp.einsum('bhid,bhde->bhie', qc * jnp.exp(G)[..., None], S) + jnp.einsum('bhij,bhje->bhie', attn, v_new)
        GL = G[..., -1]
        S = jnp.exp(GL)[..., None, None] * S + jnp.einsum('bhjd,bhje->bhde', kc * jnp.exp(GL[..., None] - G)[..., None], v_new)
        return S, o

    xs = tuple(to_chunks(a, L) for a in (q, k, v, beta, g))
    S, os_ = lax.scan(step, S, xs)
    o = os_.transpose(1, 0, 3, 2, 4).reshape(B, T, q.shape[2], dv)
    return o, S


def mixer_layer(x, lp, rel_bias, past, q_off):
    B, T, _ = x.shape
    f32 = jnp.float32
    dt = x.dtype
    parts = split_cols(rmsnorm(x, lp['norm_g']) @ lp['w_in'])

    q = rmsnorm(parts['a_q'].reshape(B, T, A_HEADS, A_HD), lp['a_qn']) * (A_HD ** -0.5)
    kv = parts['a_kv'].reshape(B, T, 3, 2, A_KV, A_HD)
    rows = jnp.stack([rmsnorm(kv[:, :, :, 0], lp['a_kn'][:, None, :]), kv[:, :, :, 1]], axis=3)
    new_cmp, new_sel, new_win = rows[:, :, 0], rows[:, :, 1], rows[:, :, 2]
    if past is None:
        full_cmp, full_sel, win, w_off = new_cmp, new_sel, new_win, 0
        C0 = jnp.zeros((B, M_HEADS, M_HD, M_HD), f32)
        n0 = jnp.zeros((B, M_HEADS, M_HD), f32)
        m0 = jnp.zeros((B, M_HEADS), f32)
        S0 = jnp.zeros((B, G_HEADS, G_HD, G_HD), f32)
        buf = jnp.zeros((B, CONV_W - 1, 3 * W_G), dt)
    else:
        full_cmp = jnp.concatenate([past['cmp'].astype(dt), new_cmp], axis=1)
        full_sel = jnp.concatenate([past['sel'].astype(dt), new_sel], axis=1)
        win = jnp.concatenate([past['win'].astype(dt), new_win], axis=1)
        w_off = q_off - past['win'].shape[1]
        C0 = past['mC'].astype(f32)
        n0 = past['mn'].astype(f32)
        m0 = past['mm'].astype(f32)
        S0 = past['gS'].astype(f32)
        buf = past['gconv'].astype(dt)
    gates = jax.nn.sigmoid(parts['a_gate'].reshape(B, T, A_HEADS, 3).astype(f32))
    o_a = nsa_attention(q, full_cmp, full_sel, win, w_off, gates, lp['a_cmp_wk'], lp['a_cmp_wv'], rel_bias, q_off)
    o_a = o_a * jax.nn.silu(parts['a_z'].astype(f32))

    mq = parts['m_qkv'].reshape(B, T, 3, M_HEADS, M_HD).astype(f32)
    mif = parts['m_if'].reshape(B, T, 2, M_HEADS).astype(f32)
    li = mif[:, :, 0] + lp['m_bi'].astype(f32)
    lf = jax.nn.log_sigmoid(mif[:, :, 1] + lp['m_bf'].astype(f32))
    hm, Cn, nn_, mn = mlstm_chunked(mq[:, :, 0], mq[:, :, 1] * (M_HD ** -0.5), mq[:, :, 2], li, lf, C0, n0, m0)
    o_gate = jax.nn.sigmoid(parts['m_o'].reshape(B, T, M_HEADS, M_HD).astype(f32))
    o_m = (rmsnorm(hm, lp['m_hn']) * o_gate).reshape(B, T, W_M) * jax.nn.silu(parts['m_z'].astype(f32))

    full = jnp.concatenate([buf, parts['g_qkv']], axis=1)
    conv = sum(full[:, j:j + T] * lp['g_conv'][j] for j in range(CONV_W))
    conv = jax.nn.silu(conv.astype(f32)).reshape(B, T, 3, G_HEADS, G_HD)
    gq = l2norm(conv[:, :, 0]) * (G_HD ** -0.5)
    gk = l2norm(conv[:, :, 1])
    gab = parts['g_ab'].reshape(B, T, 2, G_HEADS).astype(f32)
    decay = -jnp.exp(lp['g_A_log'].astype(f32)) * jax.nn.softplus(gab[:, :, 0] + lp['g_dt_bias'].astype(f32))
    beta = jax.nn.sigmoid(gab[:, :, 1])
    og, Sn = gdn_chunked(gq, gk, conv[:, :, 2], beta, decay, S0)
    o_g = rmsnorm(og, lp['g_hn']).reshape(B, T, W_G) * jax.nn.silu(parts['g_z'].astype(f32))

    br = jnp.stack([o_a, o_m, o_g], axis=2)
    proj = jnp.einsum('btie,ied->btid', br, lp['w_branch'].astype(f32))
    mg = jax.nn.sigmoid(parts['merge'].reshape(B, T, N_BRANCH, D_MODEL).astype(f32))
    y = (mg * proj).sum(axis=2).astype(dt)
    x = x + y @ lp['w_out']
    new_state = {
        'cmp': new_cmp, 'sel': new_sel, 'win': win[:, -min(WINDOW, win.shape[1]):],
        'mC': Cn.astype(dt), 'mn': nn_.astype(dt), 'mm': mn.astype(dt),
        'gS': Sn.astype(dt), 'gconv': full[:, -(CONV_W - 1):],
    }
    return x, new_state


def setup_inputs(seed: int = 0) -> dict:
    key = jax.random.key(seed)
    keys = iter(jax.random.split(key, 40))

    def nrm(shape, scale=1.0):
        return scale * jax.random.normal(next(keys), shape, jnp.float32)

    n_pages = PAST_LEN // PAGE_SIZE
    n_pool = (DEC_BATCH * n_pages * 5) // 4
    w_buf = min(WINDOW, PAST_LEN)
    page_table = jax.random.permutation(next(keys), n_pool)[:DEC_BATCH * n_pages].reshape(DEC_BATCH, n_pages).astype(jnp.int32)
    dt_init = jnp.exp(jax.random.uniform(next(keys), (DEPTH, G_HEADS), jnp.float32, math.log(1e-3), math.log(1e-1)))
    a_log = jnp.log(jax.random.uniform(next(keys), (DEPTH, G_HEADS), jnp.float32, 1.0, 16.0))
    return {
        'x_prompt': nrm((BATCH, SEQ, D_MODEL)),
        'x_sample': nrm((DEC_BATCH, DEC_SEQ, D_MODEL)),
        'cache_cmp_kv': nrm((DEPTH, n_pool, PAGE_SIZE, 2, A_KV, A_HD)),
        'cache_sel_kv': nrm((DEPTH, n_pool, PAGE_SIZE, 2, A_KV, A_HD)),
        'cache_win_kv': nrm((DEPTH, DEC_BATCH, w_buf, 2, A_KV, A_HD)),
        'state_mlstm_C': nrm((DEPTH, DEC_BATCH, M_HEADS, M_HD, M_HD), 0.1),
        'state_mlstm_n': nrm((DEPTH, DEC_BATCH, M_HEADS, M_HD), 0.5),
        'state_mlstm_m': nrm((DEPTH, DEC_BATCH, M_HEADS)),
        'state_gdn_S': nrm((DEPTH, DEC_BATCH, G_HEADS, G_HD, G_HD), 0.1),
        'state_gdn_conv': nrm((DEPTH, DEC_BATCH, CONV_W - 1, 3 * W_G)),
        'page_table': page_table,
        'norm_g': 1.0 + nrm((DEPTH, D_MODEL), 0.05),
        'w_in': nrm((DEPTH, D_MODEL, N_IN), D_MODEL ** -0.5),
        'a_qn': 1.0 + nrm((DEPTH, A_HD), 0.05),
        'a_kn': 1.0 + nrm((DEPTH, 3, A_HD), 0.05),
        'a_cmp_wk': nrm((DEPTH, A_KV, CMP_BLOCK, A_HD, A_HD), (CMP_BLOCK * A_HD) ** -0.5 * 4.0),
        'a_cmp_wv': nrm((DEPTH, A_KV, CMP_BLOCK, A_HD, A_HD), (CMP_BLOCK * A_HD) ** -0.5 * 4.0),
        'rel_bias': nrm((N_BUCKETS, A_HEADS), 0.5),
        'm_bi': nrm((DEPTH, M_HEADS), 0.1),
        'm_bf': jnp.linspace(3.0, 6.0, M_HEADS)[None, :] + nrm((DEPTH, M_HEADS), 0.1),
        'm_hn': 1.0 + nrm((DEPTH, M_HD), 0.05),
        'g_conv': nrm((DEPTH, CONV_W, 3 * W_G), CONV_W ** -0.5),
        'g_A_log': a_log,
        'g_dt_bias': dt_init + jnp.log(-jnp.expm1(-dt_init)),
        'g_hn': 1.0 + nrm((DEPTH, G_HD), 0.05),
        'w_branch': nrm((DEPTH, N_BRANCH, W_A, D_MODEL), W_A ** -0.5),
        'w_out': nrm((DEPTH, D_MODEL, D_MODEL), D_MODEL ** -0.5),
    }


def reference(x_prompt, x_sample, cache_cmp_kv, cache_sel_kv, cache_win_kv, state_mlstm_C, state_mlstm_n,
              state_mlstm_m, state_gdn_S, state_gdn_conv, page_table, norm_g, w_in, a_qn, a_kn, a_cmp_wk,
              a_cmp_wv, rel_bias, m_bi, m_bf, m_hn, g_conv, g_A_log, g_dt_bias, g_hn, w_branch, w_out):
    layer_w = {
        'norm_g': norm_g, 'w_in': w_in, 'a_qn': a_qn, 'a_kn': a_kn, 'a_cmp_wk': a_cmp_wk, 'a_cmp_wv': a_cmp_wv,
        'm_bi': m_bi, 'm_bf': m_bf, 'm_hn': m_hn, 'g_conv': g_conv, 'g_A_log': g_A_log,
        'g_dt_bias': g_dt_bias, 'g_hn': g_hn, 'w_branch': w_branch, 'w_out': w_out,
    }
    names = ('cmp', 'sel', 'win', 'mC', 'mn', 'mm', 'gS', 'gconv')
    st_p = {k: [] for k in names}
    st_s = {k: [] for k in names}
    y_prompt, y_sample = x_prompt, x_sample
    for l in range(DEPTH):
        lp = {name: arr[l] for name, arr in layer_w.items()}
        y_prompt, new_p = mixer_layer(y_prompt, lp, rel_bias, None, 0)
        past = {
            'cmp': gather_pages(cache_cmp_kv[l], page_table),
            'sel': gather_pages(cache_sel_kv[l], page_table),
            'win': cache_win_kv[l], 'mC': state_mlstm_C[l], 'mn': state_mlstm_n[l], 'mm': state_mlstm_m[l],
            'gS': state_gdn_S[l], 'gconv': state_gdn_conv[l],
        }
        y_sample, new_s = mixer_layer(y_sample, lp, rel_bias, past, PAST_LEN)
        for k in names:
            st_p[k].append(new_p[k])
            st_s[k].append(new_s[k])
    P = {k: jnp.stack(v) for k, v in st_p.items()}
    S = {k: jnp.stack(v) for k, v in st_s.items()}
    return (y_prompt, y_sample, P['cmp'], S['cmp'], P['sel'], S['sel'], P['win'], S['win'],
            P['mC'], S['mC'], P['mn'], S['mn'], P['mm'], S['mm'], P['gS'], S['gS'], P['gconv'], S['gconv'])
```

```python
import math
import numpy as np
from contextlib import ExitStack
import concourse.bass as bass
import concourse.mybir as mybir
from concourse.bass_utils import run_bass_kernel_spmd

F32 = mybir.dt.float32
BF16 = mybir.dt.bfloat16
I32 = mybir.dt.int32
ALU = mybir.AluOpType
AF = mybir.ActivationFunctionType

NCORES = 8
T = 2048
TS = 16
TT = T + TS
D = 1024
EPS = 1e-6

O_AQ = 0
O_AKV = O_AQ + 512
O_AG = O_AKV + 768
O_AZ = O_AG + 24
O_MQKV = O_AZ + 512
O_MIF = O_MQKV + 1536
O_MO = O_MIF + 8
O_MZ = O_MO + 512
O_GQKV = O_MZ + 512
O_GAB = O_GQKV + 1536
O_GZ = O_GAB + 8
O_MERGE = O_GZ + 512
N_IN = O_MERGE + 3072


def _fm_chunks():
    ch = []
    r = np.arange
    def headpair(base, c):
        return np.concatenate([base + 64 * c + r(64), base + 256 + 64 * c + r(64)])
    for c in range(4):
        ch.append(("aq", c, headpair(O_AQ, c)))
    for b in range(3):
        ch.append(("ak", b, O_AKV + b * 256 + r(128)))
    ch.append(("av", 0, O_AKV + 128 + r(128)))
    for j in range(4):
        ch.append(("az", j, O_AZ + j * 128 + r(128)))
    for h in range(4):
        ch.append(("mq", h, O_MQKV + h * 128 + r(128)))
    for h in range(4):
        ch.append(("mk", h, O_MQKV + 512 + h * 128 + r(128)))
    for h in range(4):
        ch.append(("mo", h, O_MO + h * 128 + r(128)))
    for h in range(4):
        ch.append(("mz", h, O_MZ + h * 128 + r(128)))
    for j in range(12):
        ch.append(("gqkv", j, O_GQKV + j * 128 + r(128)))
    for h in range(4):
        ch.append(("gz", h, O_GZ + h * 128 + r(128)))
    for j in range(24):
        ch.append(("merge", j, O_MERGE + j * 128 + r(128)))
    return ch


FM = _fm_chunks()
NFM = len(FM)
FMI = {(n, i): k for k, (n, i, _) in enumerate(FM)}
TMA_COLS = np.concatenate([O_AKV + b * 256 + 128 + np.arange(128) for b in range(3)] + [O_AG + np.arange(24)])
NTMA = len(TMA_COLS)
TMM_COLS = [np.concatenate([O_MQKV + 512 + h * 128 + np.arange(128), O_MQKV + 1024 + h * 128 + np.arange(128)]) for h in range(4)]
SM_COLS = np.concatenate([O_MIF + np.arange(4), O_MIF + 4 + np.arange(4), O_GAB + np.arange(4), O_GAB + 4 + np.arange(4)])
NT64 = 36


class View:
    __slots__ = ("buf", "ap")

    def __init__(self, buf, ap):
        self.buf = buf
        self.ap = ap

    def re(self, pat, **kw):
        return View(self.buf, self.ap.rearrange(pat, **kw))

    def __getitem__(self, idx):
        return View(self.buf, self.ap[idx])


class Buf:
    __slots__ = ("t", "name", "w", "r", "dsem", "dcnt", "excl")

    def __init__(self, t, name, excl=False):
        self.t = t
        self.name = name
        self.w = None
        self.r = []
        self.dsem = None
        self.dcnt = 0
        self.excl = excl

    def __getitem__(self, idx):
        return View(self, self.t[idx])


def _ap(x):
    return x.ap if isinstance(x, View) else x


class Scope:
    def __init__(self, em):
        self.em = em
        self.stack = ExitStack()
        self.bufs = []
        self.nbytes = 0

    def __enter__(self):
        self.stack.__enter__()
        return self

    def __exit__(self, *a):
        if a[0] is None:
            self.em.barrier()
            for b in self.bufs:
                if b.dsem is not None:
                    self.em.sem_pool.append((b.dsem, b.dcnt))
                    b.dsem = None
            ids = {id(b) for b in self.bufs}
            self.em.allbufs = [b for b in self.em.allbufs if id(b) not in ids]
        self.em.cur_bytes -= self.nbytes
        return self.stack.__exit__(*a)


class Em:
    ROLL = 30000

    def __init__(self, nc, stack):
        self.nc = nc
        self.stack = stack
        self.eng = {"pe": nc.tensor, "act": nc.scalar, "dve": nc.vector, "pool": nc.gpsimd, "sp": nc.sync}
        self.sem = {}
        self.cnt = {}
        self.nsem = 0
        self.seen = {k: {} for k in self.eng}
        for k in self.eng:
            self._newsem(k)
        self.store_tokens = []
        self.nbuf = 0
        self.nwaits = 0
        self.ninst = 0
        self.allbufs = []
        self.cur_bytes = 0
        self.peak_bytes = 0
        self.sem_pool = []
        self.nsem_d = 0
        self.selfsync = {"pe": False, "act": True, "dve": True, "pool": True, "sp": False}

    def _newsem(self, k):
        self.nsem += 1
        self.sem[k] = self.stack.enter_context(self.nc.semaphore(f"s_{k}_{self.nsem}"))
        self.cnt[k] = 0

    def sb(self, shape, dt, name=None):
        self.nbuf += 1
        name = name or f"sb{self.nbuf}"
        t = self.stack.enter_context(self.nc.sbuf_tensor("t_" + name, list(shape), dt))
        self.cur_bytes += int(np.prod(shape[1:])) * (2 if dt == BF16 else 4)
        self.peak_bytes = max(self.peak_bytes, self.cur_bytes)
        b = Buf(t, name)
        self.allbufs.append(b)
        return b

    def ps(self, shape, dt, name=None):
        self.nbuf += 1
        name = name or f"ps{self.nbuf}"
        t = self.stack.enter_context(self.nc.psum_tensor("t_" + name, list(shape), dt))
        return Buf(t, name, excl=True)

    def _waits(self, e, reads, writes):
        need = {}

        def add(tok):
            if tok is None:
                return
            s, v = tok
            k = id(s)
            if k not in need or need[k][1] < v:
                need[k] = (s, v)
        for b in reads:
            add(b.w)
            if b.excl:
                for t in b.r:
                    add(t)
        for b in writes:
            add(b.w)
            for t in b.r:
                add(t)
        eng = self.eng[e]
        seen = self.seen[e]
        own = self.sem[e]
        for k, (s, v) in need.items():
            if s is own and not self.selfsync[e]:
                continue
            if seen.get(k, 0) >= v:
                continue
            seen[k] = v
            eng.wait_ge(s, v)
            self.nwaits += 1

    def _commit(self, tok, reads, writes):
        for b in reads:
            if len(b.r) > 24:
                last = {}
                for (s, v) in b.r:
                    if id(s) not in last or last[id(s)][1] < v:
                        last[id(s)] = (s, v)
                b.r = list(last.values())
            b.r.append(tok)
        for b in writes:
            b.w = tok
            b.r = []

    def _signal(self, e, ins):
        if self.cnt[e] >= self.ROLL:
            self._newsem(e)
        self.cnt[e] += 1
        ins.then_inc(self.sem[e], 1)
        return (self.sem[e], self.cnt[e])

    def op(self, e, fn, ins=(), outs=()):
        reads = [v.buf for v in ins if isinstance(v, View)]
        writes = [v.buf for v in outs if isinstance(v, View)]
        self._waits(e, reads, writes)
        i = fn(self.eng[e])
        tok = self._signal(e, i)
        self._commit(tok, reads, writes)
        self.ninst += 1

    def mm(self, out, pairs, extra_reads=(), transpose=False):
        reads = []
        for a, b in pairs:
            reads += [a.buf, b.buf]
        writes = [out.buf]
        self._waits("pe", reads, writes)
        n = len(pairs)
        i = None
        for j, (a, b) in enumerate(pairs):
            if transpose:
                i = self.nc.tensor.transpose(out.ap, a.ap, b.ap)
            else:
                i = self.nc.tensor.matmul(out.ap, lhsT=a.ap, rhs=b.ap, start=(j == 0), stop=(j == n - 1))
            self.ninst += 1
        tok = self._signal("pe", i)
        self._commit(tok, reads, writes)

    def mm_multi(self, groups):
        reads, writes = [], []
        for out, pairs in groups:
            writes.append(out.buf)
            for a, b in pairs:
                reads += [a.buf, b.buf]
        self._waits("pe", reads, writes)
        i = None
        for out, pairs in groups:
            n = len(pairs)
            for j, (a, b) in enumerate(pairs):
                i = self.nc.tensor.matmul(out.ap, lhsT=a.ap, rhs=b.ap, start=(j == 0), stop=(j == n - 1))
                self.ninst += 1
        tok = self._signal("pe", i)
        self._commit(tok, reads, writes)

    def transposes(self, items):
        reads, writes = [], []
        for o, a, idn in items:
            writes.append(o.buf)
            reads += [a.buf, idn.buf]
        self._waits("pe", reads, writes)
        i = None
        for o, a, idn in items:
            i = self.nc.tensor.transpose(o.ap, a.ap, idn.ap)
            self.ninst += 1
        tok = self._signal("pe", i)
        self._commit(tok, reads, writes)

    def act(self, out, in_, func, bias=None, scale=None, accum_out=None, e="act"):
        kw = {}
        ins = [in_]
        outs = [out]
        if bias is not None:
            kw["bias"] = _ap(bias)
            if isinstance(bias, View):
                ins.append(bias)
        if scale is not None:
            kw["scale"] = _ap(scale)
            if isinstance(scale, View):
                ins.append(scale)
        if accum_out is not None:
            kw["accum_out"] = _ap(accum_out)
            outs.append(accum_out)
        self.op("act", lambda g: g.activation(out=_ap(out), in_=_ap(in_), func=func, **kw), ins, outs)

    def tt(self, out, in0, in1, op, e="dve"):
        self.op(e, lambda g: g.tensor_tensor(out=_ap(out), in0=_ap(in0), in1=_ap(in1), op=op), [in0, in1], [out])

    def ts(self, out, in0, s1, s2=None, op0=ALU.mult, op1=None, e="dve", accum_out=None):
        kw = {}
        outs = [out]
        if op1 is not None:
            kw["op1"] = op1
        if accum_out is not None:
            kw["accum_out"] = _ap(accum_out)
            outs.append(accum_out)
        self.op(e, lambda g: g.tensor_scalar(out=_ap(out), in0=_ap(in0), scalar1=_ap(s1), scalar2=_ap(s2), op0=op0, **kw),
                [in0, s1, s2], outs)

    def stt(self, out, in0, scalar, in1, op0, op1):
        self.op("dve", lambda g: g.scalar_tensor_tensor(out=_ap(out), in0=_ap(in0), scalar=_ap(scalar), in1=_ap(in1),
                                                        op0=op0, op1=op1), [in0, scalar, in1], [out])

    def copy(self, out, in_, e="dve"):
        if e == "act":
            self.op("act", lambda g: g.copy(out=_ap(out), in_=_ap(in_)), [in_], [out])
        else:
            self.op(e, lambda g: g.tensor_copy(out=_ap(out), in_=_ap(in_)), [in_], [out])

    def memset(self, out, val, e="pool"):
        self.op(e, lambda g: g.memset(_ap(out), val), [], [out])

    def recip(self, out, in_):
        self.op("dve", lambda g: g.reciprocal(out=_ap(out), in_=_ap(in_)), [in_], [out])

    def dma(self, q, out, in_, key, final=False, **kw):
        reads = [in_.buf] if isinstance(in_, View) else []
        writes = [out.buf] if isinstance(out, View) else []
        self._waits(q, reads, writes)
        if key.dsem is None or key.dcnt >= 28000:
            key.dsem, key.dcnt = self._getsem()
        i = self.eng[q].dma_start(out=_ap(out), in_=_ap(in_), **kw)
        key.dcnt += 16
        i.then_inc(key.dsem, 16)
        tok = (key.dsem, key.dcnt)
        self._commit(tok, reads, writes)
        if final:
            self.store_tokens.append(tok)
        self.ninst += 1

    def dram(self, ap, name):
        self.nbuf += 1
        b = Buf(ap, name)
        self.allbufs.append(b)
        return b

    def sbs(self, scope, shape, dt, name):
        self.nbuf += 1
        t = scope.stack.enter_context(self.nc.sbuf_tensor(f"t{self.nbuf}_" + name, list(shape), dt))
        b = Buf(t, f"{name}{self.nbuf}")
        nb_ = int(np.prod(shape[1:])) * (2 if dt == BF16 else 4)
        self.cur_bytes += nb_
        scope.nbytes += nb_
        if self.cur_bytes > self.peak_bytes:
            self.peak_bytes = self.cur_bytes
            self.peak_at = name
        self.allbufs.append(b)
        scope.bufs.append(b)
        return b

    def idma(self, out, rows_ap, idxv, key):
        self._waits("pool", [idxv.buf], [out.buf])
        if key.dsem is None or key.dcnt >= 28000:
            key.dsem, key.dcnt = self._getsem()
        i = self.nc.gpsimd.indirect_dma_start(out=out.ap, out_offset=None, in_=rows_ap,
                                              in_offset=bass.IndirectOffsetOnAxis(ap=idxv.ap, axis=0))
        key.dcnt += 16
        i.then_inc(key.dsem, 16)
        tok = (key.dsem, key.dcnt)
        self._commit(tok, [idxv.buf], [out.buf])
        self.ninst += 1

    def barrier(self):
        toks = []
        for k in self.eng:
            if self.cnt[k] > 0:
                toks.append((self.sem[k], self.cnt[k]))
        for b in self.allbufs:
            if b.dsem is not None and b.dcnt > 0:
                toks.append((b.dsem, b.dcnt))
        for e in self.eng:
            seen = self.seen[e]
            for s_, v in toks:
                if s_ is self.sem[e]:
                    continue
                if seen.get(id(s_), 0) >= v:
                    continue
                seen[id(s_)] = v
                self.eng[e].wait_ge(s_, v)

    def reduce(self, out, in_, op=ALU.add, e="dve"):
        self.op(e, lambda g: g.tensor_reduce(out=_ap(out), in_=_ap(in_), axis=mybir.AxisListType.X, op=op), [in_], [out])

    def scan(self, out, d0, d1, initial, op0, op1):
        ins = [d0, d1] + ([initial] if isinstance(initial, View) else [])
        self.op("dve", lambda g: g.tensor_tensor_scan(out=_ap(out), data0=_ap(d0), data1=_ap(d1), initial=_ap(initial),
                                                      op0=op0, op1=op1), ins, [out])

    def top8(self, out, in_):
        self.op("dve", lambda g: g.max(out=_ap(out), in_=_ap(in_)), [in_], [out])

    def mrep(self, out, m8, vals, imm):
        self.op("dve", lambda g: g.match_replace(out=_ap(out), in_to_replace=_ap(m8), in_values=_ap(vals), imm_value=imm),
                [m8, vals], [out])

    def mm1(self, out, a, b, start, stop):
        reads = [a.buf, b.buf]
        writes = [out.buf]
        self._waits("pe", reads, writes)
        i = self.nc.tensor.matmul(out.ap, lhsT=a.ap, rhs=b.ap, start=start, stop=stop)
        self.ninst += 1
        tok = self._signal("pe", i)
        self._commit(tok, reads, writes)

    def _getsem(self):
        while self.sem_pool:
            s_, c = self.sem_pool.pop()
            if c < 20000:
                return s_, c
        self.nsem_d += 1
        return self.stack.enter_context(self.nc.semaphore(f"d{self.nsem_d}")), 0

    def scope(self):
        return Scope(self)

    def finish(self, e="sp"):
        eng = self.eng[e]
        last = {}
        for s, v in self.store_tokens:
            if id(s) not in last or last[id(s)][1] < v:
                last[id(s)] = (s, v)
        for s, v in last.values():
            eng.wait_ge(s, v)
        for k in self.eng:
            if k != e and self.cnt[k] > 0:
                eng.wait_ge(self.sem[k], self.cnt[k])


BLOCKS = [(i * 512, 512) for i in range(4)] + [(T, TS)]
NEGM = -30000.0


def t64(i):
    return (64 * i, 64) if i < 32 else (T + 4 * (i - 32), 4)


def bc(view, shape):
    return View(view.buf, view.ap.unsqueeze(len(view.ap.shape)).broadcast_to(list(shape)))


def build_program(stages=("B1", "M", "G", "ATT", "SAMP", "OUT"), layers=2):
    nc = bass.Bass("TRN2", target_bir_lowering=False)
    S = set(stages)

    in_names = []

    def din(name, shape, dt=F32):
        in_names.append(name)
        return nc.dram_tensor(name, list(shape), dt, kind="ExternalInput").ap()

    def dout(name, shape, dt=F32):
        return nc.dram_tensor(name, list(shape), dt, kind="ExternalOutput").ap()

    xp = din("xp", [T, D])
    xs = din("xs", [TS, D])
    wfm = din("wfm", [2, NFM, 128, 8 * 128])
    wtma = din("wtma", [2, 128, 8 * NTMA])
    wtmm = din("wtmm", [2, 4, 128, 8 * 256])
    wsm = din("wsm", [2, 128, 8 * 16])
    wbr_in = din("wbr", [2, 128, 12 * 1024])
    wout_in = din("wout", [2, 128, 8 * 1024])
    normg = din("normg", [2, 128, D])
    aqn = din("aqn", [2, 128, 1])
    akn = din("akn", [2, 3, 128, 1])
    mhn_in = din("mhn", [2, 64, 128])
    ghn_in = din("ghn", [2, 64, 128])
    gcol_in = din("gcols", [2, 4, 4])
    gcw_in = din("gcw", [2, 12, 128, 4])
    ident_in = din("ident", [128, 128])
    bd64_in = din("bd64", [128, 128])
    onehot_in = din("onehot", [4, 4 * 128])
    masku_in = din("masku", [64, 64])
    maskb_in = din("maskb", [64, 64])
    strict_in = din("strict", [64, 64])
    rmask_in = din("rmask", [4, TT])
    ttinit_in = din("ttinit", [128, 64 * 16])
    ttinit4_in = din("ttinit4", [4, 16])
    cwin = din("cwin", [2, 4, 512, 256])
    if "SAMP" in S:
        cmp_pool = [din(f"cmp_pool{i}", [2560 * 128, 256]) for i in range(2)]
        sel_pool = [din(f"sel_pool{i}", [2560 * 128, 256]) for i in range(2)]
        ptab_in = din("ptab", [1, 256], I32)
        iota_in = din("iota", [128, 1])
        wag_in = din("wag", [2, 128, 8 * 24])
        selg_in = din("selg", [24, 12 * 128])
        mcs_in = din("mcs", [128, 4 * 32])
        covers_in = din("covers", [128, 4 * 129])
        msels_in = din("msels", [128, 64 * 32])
        mwins_in = din("mwins", [128, 4 * 32])
        msnew_in = din("msnew", [16, 4 * 32])
        keeps_in = din("keeps", [4, 129])
        adds_in = din("adds", [4, 129])
        qsel_in = din("qsel", [16, 4])
        maskc_in = din("maskc", [16, 4])
        lsel_in = din("lsel", [4, 256])
        if "ATT" not in S:
            wcmp_in = din("wcmp", [2, 2, 128, 32 * 128])
    if "ATT" in S:
        msel_in = din("msel", [8, 128, 2432])
        mwin_in = din("mwin", [8, 128, 1408])
        mc_in = din("mc", [8, 128, 2048])
        cover_in = din("cover", [128, 32])
        keep_in = din("keep", [128, 16 * 32])
        addm_in = din("addm", [128, 16 * 32])
        wex_in = din("wex", [32, 2048])
        wcmp_in = din("wcmp", [2, 2, 128, 32 * 128])
    mC0 = din("mC0", [2, 4, 4, 128, 128])
    mn0 = din("mn0", [2, 4, 4, 128])
    mm0 = din("mm0", [2, 4, 4])
    gS0 = din("gS0", [2, 4, 4, 128, 128])
    gcv0 = din("gcv0", [2, 4, 12, 128, 3])

    o_y_p = dout("y_p", [T, D])
    o_y_s = dout("y_s", [TS, D])
    o_cmp_p = dout("cmp_p", [2, T, 256])
    o_cmp_s = dout("cmp_s", [2, TS, 256])
    o_sel_p = dout("sel_p", [2, T, 256])
    o_sel_s = dout("sel_s", [2, TS, 256])
    o_win_p = dout("win_p", [2, 512, 256])
    o_win_s = dout("win_s", [2, 4, 512, 256])
    o_mC_p = dout("mC_p", [2, 4, 128, 128])
    o_mC_s = dout("mC_s", [2, 4, 4, 128, 128])
    o_mn_p = dout("mn_p", [2, 4, 128])
    o_mn_s = dout("mn_s", [2, 4, 4, 128])
    o_mm_p = dout("mm_p", [2, 4])
    o_mm_s = dout("mm_s", [2, 4, 4])
    o_gS_p = dout("gS_p", [2, 4, 128, 128])
    o_gS_s = dout("gS_s", [2, 4, 4, 128, 128])
    o_gc_p = dout("gc_p", [2, 12, 128, 3])
    o_gc_s = dout("gc_s", [2, 4, 12, 128, 3])
    br_d = dout("br_scr", [12, 128, TT], BF16)
    sA_d = dout("sA_scr", [32, 64 * 64])
    sT_d = dout("sT_scr", [32, 64, 64])
    sA4_d = dout("sA4_scr", [4, 16])
    sT4_d = dout("sT4_scr", [4, 16])
    kv_out_p = [o_cmp_p, o_sel_p]
    kv_out_s = [o_cmp_s, o_sel_s]

    with ExitStack() as st:
        em = Em(nc, st)
        YP = em.dram(o_y_p, "YP")
        YS = em.dram(o_y_s, "YS")
        BR = em.dram(br_d, "BR")
        SA = em.dram(sA_d, "SA")
        STT = em.dram(sT_d, "STT")
        SA4 = em.dram(sA4_d, "SA4")
        ST4 = em.dram(sT4_d, "ST4")
        OUTB = em.dram(None, "OUTB")
        ident = em.sb([128, 128], F32, "ident")
        identb = em.sb([128, 128], BF16, "identb")
        bd64 = em.sb([128, 128], F32, "bd64")
        onehot = em.sb([4, 4, 128], F32, "onehot")
        masku = em.sb([64, 64], F32, "masku")
        maskb = em.sb([64, 64], F32, "maskb")
        strict = em.sb([64, 64], F32, "strict")
        ones128 = em.sb([128, 128], F32, "ones128")
        em.dma("sp", ident[:], ident_in[:, :], key=ident)
        em.dma("sp", bd64[:], bd64_in[:, :], key=bd64)
        em.dma("sp", onehot[:], onehot_in.rearrange("p (h c) -> p h c", h=4), key=onehot)
        em.dma("sp", masku[:], masku_in[:, :], key=masku)
        em.dma("sp", maskb[:], maskb_in[:, :], key=maskb)
        em.dma("sp", strict[:], strict_in[:, :], key=strict)
        em.copy(identb[:], ident[:])
        em.memset(ones128[:], 1.0)
        PS = [em.ps([128, 512], F32, f"bank{i}") for i in range(8)]
        hT = em.sb([128, 8, TT], BF16, "hT")
        gcol = em.sb([128, 2, 4], F32, "gcol")
        for l in range(2):
            em.dma("sp", gcol[:, l, 0:1], aqn[l], key=gcol)
            for b in range(3):
                em.dma("sp", gcol[:, l, 1 + b:2 + b], akn[l, b], key=gcol)
        for l in range(2):
            em.ts(gcol[:, l, 0:1], gcol[:, l, 0:1], 0.125, None, op0=ALU.mult)
        wb = [em.sb([128, 8, 128], BF16, f"wb{i}") for i in range(4)]
        wi = [0]

        def load_w(l, name, idx):
            wbuf = wb[wi[0] % 4]
            wi[0] += 1
            em.dma("pool", wbuf[:, :, :], wfm[l, FMI[(name, idx)]].rearrange("p (a b) -> p a b", a=8), key=wbuf)
            return wbuf

        def fm_proj(wbuf, bi, t0, n):
            pb = PS[bi % 2]
            em.mm(pb[:, 0:n], [(wbuf[:, kc, :], hT[:, kc, t0:t0 + n]) for kc in range(8)])
            return pb

        def store(q, dst, src, key=None):
            em.dma(q, dst, src, key=key or src.buf, final=True)

        for l in range(layers):
            with em.scope() as ph:
                xt = [em.sbs(ph, [128, D], F32, f"xt{i}") for i in range(2)]
                hn = [em.sbs(ph, [128, D], F32, f"hn{i}") for i in range(2)]
                sqj = em.sbs(ph, [128, D], F32, "sqj")
                ssq = [em.sbs(ph, [128, 1], F32, f"ssq{i}") for i in range(2)]
                rstd = [em.sbs(ph, [128, 1], F32, f"rstd{i}") for i in range(2)]
                gbc = em.sbs(ph, [128, D], F32, "gbc")
                em.dma("sp", gbc[:], normg[l], key=gbc)
                for ti in range(17):
                    n = 128 if ti < 16 else TS
                    tok0 = ti * 128 if ti < 16 else T
                    xb = xt[ti % 2]
                    hb = hn[ti % 2]
                    if l == 0:
                        src = xp[ti * 128:(ti + 1) * 128, :] if ti < 16 else xs[:, :]
                    else:
                        src = YP[ti * 128:(ti + 1) * 128, :] if ti < 16 else YS[:, :]
                    em.dma("sp", xb[0:n, :], src, key=xb)
                    s_ = ssq[ti % 2]
                    r_ = rstd[ti % 2]
                    em.act(sqj[0:n, :], xb[0:n, :], AF.Square, accum_out=s_[0:n, :])
                    em.act(r_[0:n, :], s_[0:n, :], AF.Sqrt, scale=1.0 / D, bias=EPS)
                    em.recip(r_[0:n, :], r_[0:n, :])
                    em.stt(hb[0:n, :], xb[0:n, :], r_[0:n, :], gbc[0:n, :], ALU.mult, ALU.mult)
                    for half in range(2):
                        pb = PS[6 + half]
                        items = []
                        for j in range(4):
                            kc = half * 4 + j
                            items.append((pb[:, j * 128:j * 128 + n], hb[0:n, kc * 128:(kc + 1) * 128], ident[0:n, 0:n]))
                        em.transposes(items)
                        em.copy(hT[:, half * 4:half * 4 + 4, tok0:tok0 + n],
                                pb[:, :].re("p (j t) -> p j t", j=4)[:, :, 0:n], e="act" if half else "dve")
                em.barrier()

            if "B1" in S or "ATT" in S:
              with em.scope() as pa:
                qT = em.sbs(pa, [128, 4, TT], BF16, "qT")
                kT = em.sbs(pa, [128, 3, TT], BF16, "kT")
                vcT = em.sbs(pa, [128, TT], BF16, "vcT")
                vaug = em.sbs(pa, [128, 17, 6, 65], BF16, "vaug")
                gates = em.sbs(pa, [128, 17, 24], F32, "gates")
                with em.scope() as ph:
                    wtmb = em.sbs(ph, [128, 8, NTMA], BF16, "wtmab")
                    sq = [em.sbs(ph, [128, 512], F32, f"sq{i}") for i in range(2)]
                    rs = [em.sbs(ph, [128, 512], F32, f"rs{i}") for i in range(2)]
                    kf = [em.sbs(ph, [128, 512], F32, f"kf{i}") for i in range(2)]
                    stg = em.sbs(ph, [128, 4, 3, 256], F32, "stg")
                    em.dma("pool", wtmb[:, :, :], wtma[l].rearrange("p (a b) -> p a b", a=8), key=wtmb)
                    em.memset(vaug[:, :, :, 64:65], 1.0, e="dve")

                    def headnorm(pb, n, bi, gain, out_views):
                        s_ = sq[bi % 2]
                        r_ = rs[bi % 2]
                        em.act(s_[:, 0:n], pb[:, 0:n], AF.Square)
                        p2 = PS[2 + bi % 2]
                        em.mm(p2[:, 0:n], [(bd64[:, :], s_[:, 0:n])])
                        em.act(r_[:, 0:n], p2[:, 0:n], AF.Sqrt, scale=1.0 / 64.0, bias=EPS)
                        em.recip(r_[:, 0:n], r_[:, 0:n])
                        for ov in out_views:
                            em.stt(ov, pb[:, 0:n], gain, r_[:, 0:n], ALU.mult, ALU.mult)

                    wk = [load_w(l, "ak", b_) for b_ in range(3)]
                    for bi, (t0, n) in enumerate(BLOCKS):
                        sg = stg
                        nt = (n + 127) // 128
                        for j in range(nt):
                            nn = min(128, n - j * 128)
                            ti = (t0 // 128 + j) if bi < 4 else 16
                            pb = PS[4 + j % 2]
                            em.mm(pb[0:nn, 0:NTMA], [(hT[:, kc, t0 + j * 128:t0 + j * 128 + nn], wtmb[:, kc, :]) for kc in range(8)])
                            em.copy(vaug[0:nn, ti, :, 0:64], pb[0:nn, 0:384].re("p (b c) -> p b c", c=64), e="act")
                            em.copy(sg[0:nn, j, :, 128:256], pb[0:nn, 0:384].re("p (b c) -> p b c", b=3), e="dve")
                            em.act(gates[0:nn, ti, :], pb[0:nn, 384:408], AF.Sigmoid)
                        for b_ in range(3):
                            pb = PS[b_ % 2]
                            em.mm(pb[:, 0:n], [(wk[b_][:, kc, :], hT[:, kc, t0:t0 + n]) for kc in range(8)])
                            k_ = kf[b_ % 2]
                            headnorm(pb, n, b_, gcol[:, l, 1 + b_:2 + b_], [k_[:, 0:n]])
                            em.copy(kT[:, b_, t0:t0 + n], k_[:, 0:n], e="pool")
                            for j in range(nt):
                                nn = min(128, n - j * 128)
                                p3 = PS[6 + j % 2]
                                em.transposes([(p3[0:nn, 0:128], k_[:, j * 128:j * 128 + nn], ident[:, :])])
                                em.copy(sg[0:nn, j, b_, 0:128], p3[0:nn, 0:128], e="act")
                        for b_ in range(3):
                            if b_ < 2:
                                if bi < 4:
                                    store("sp", kv_out_p[b_][l, t0:t0 + 512, :].rearrange("(j p) c -> p j c", p=128), sg[:, :, b_, :])
                                else:
                                    store("sp", kv_out_s[b_][l, :, :], sg[0:TS, 0, b_, :])
                            else:
                                if bi == 3:
                                    store("sp", o_win_p[l, :, :].rearrange("(j p) c -> p j c", p=128), sg[:, :, b_, :])
                                elif bi == 4:
                                    for s_ in range(4):
                                        store("sp", o_win_s[l, s_, 508:512, :], sg[4 * s_:4 * s_ + 4, 0, b_, :])
                    for s_ in range(4):
                        em.dma("pool", o_win_s[l, s_, 0:508, :], cwin[l, s_, 4:512, :], key=OUTB, final=True)
                    for c in range(4):
                        wbuf = load_w(l, "aq", c)
                        for bi, (t0, n) in enumerate(BLOCKS):
                            pb = PS[bi % 2]
                            em.mm(pb[:, 0:n], [(wbuf[:, kc, :], hT[:, kc, t0:t0 + n]) for kc in range(8)])
                            headnorm(pb, n, bi, gcol[:, l, 0:1], [qT[:, c, t0:t0 + n]])
                    wv_ = load_w(l, "av", 0)
                    for bi, (t0, n) in enumerate(BLOCKS):
                        pb = fm_proj(wv_, bi, t0, n)
                        em.copy(vcT[:, t0:t0 + n], pb[:, 0:n], e="act")
                    em.barrier()
                if "ATT" in S:
                  with em.scope() as ph:
                    kcmpT = em.sbs(ph, [128, 128], BF16, "kcmpT")
                    rhsc = em.sbs(ph, [128, 2, 97], BF16, "rhsc")
                    coverf = em.sbs(ph, [128, 32], F32, "coverf")
                    keep = em.sbs(ph, [128, 16, 32], F32, "keep")
                    addm = em.sbs(ph, [128, 16, 32], F32, "addm")
                    wex = em.sbs(ph, [32, 2048], BF16, "wex")
                    phc = em.scope()
                    phc.__enter__()
                    wck = em.sbs(phc, [128, 32, 128], BF16, "wck")
                    wcv = em.sbs(phc, [128, 32, 128], BF16, "wcv")
                    em.dma("pool", wck[:, :, :], wcmp_in[l, 0].rearrange("p (a b) -> p a b", a=32), key=wck)
                    em.dma("pool", wcv[:, :, :], wcmp_in[l, 1].rearrange("p (a b) -> p a b", a=32), key=wcv)
                    em.dma("sp", coverf[:], cover_in[:, :], key=coverf)
                    em.dma("sp", keep[:], keep_in.rearrange("p (a b) -> p a b", a=16), key=keep)
                    em.dma("sp", addm[:], addm_in.rearrange("p (a b) -> p a b", a=16), key=addm)
                    em.dma("pool", wex[:, :], wex_in[:, :], key=wex)
                    kv16 = kT[:, 0, 0:T].re("p (n s) -> p n s", s=16)
                    vv16 = vcT[:, 0:T].re("p (n s) -> p n s", s=16)
                    em.mm(PS[0][:, 0:127], [(wck[:, ms, :], kv16[:, ms // 16:ms // 16 + 127, ms % 16]) for ms in range(32)])
                    em.copy(kcmpT[:, 0:127], PS[0][:, 0:127], e="act")
                    em.mm(PS[1][0:127, 0:128], [(vv16[:, ms // 16:ms // 16 + 127, ms % 16], wcv[:, ms, :]) for ms in range(32)])
                    em.copy(rhsc[0:127, :, 0:64], PS[1][0:127, 0:128].re("p (g e) -> p g e", g=2), e="act")
                    em.memset(rhsc[:, :, 64:65], 1.0, e="dve")
                    for g in range(2):
                        em.copy(rhsc[:, g, 65:97], coverf[:, :])
                    phc.__exit__(None, None, None)
                    oa = em.sbs(ph, [128, 4, 512], F32, "oa")
                    imp = em.sbs(ph, [128, 4, 2, 32], F32, "imp")
                    penT = em.sbs(ph, [32, 2, 512], BF16, "penT")
                    msel2 = [em.sbs(ph, [128, 2432], BF16, f"msel{i}") for i in range(2)]
                    mwin2 = [em.sbs(ph, [128, 1408], BF16, f"mwin{i}") for i in range(2)]
                    mcb2 = [em.sbs(ph, [128, 512], F32, f"mcb{i}") for i in range(2)]
                    sbt = [em.sbs(ph, [128, 512], F32, f"sbt{i}") for i in range(2)]
                    Pt = [em.sbs(ph, [128, 512], BF16, f"Pt{i}") for i in range(2)]
                    rden = em.sbs(ph, [128, 4], F32, "rden")
                    coef = em.sbs(ph, [128, 4], F32, "coef")
                    tmpo = em.sbs(ph, [128, 4, 64], F32, "tmpo")
                    tmpi = em.sbs(ph, [128, 4, 32], F32, "tmpi")
                    sc = em.sbs(ph, [128, 32], F32, "sc")
                    sc2 = em.sbs(ph, [128, 32], F32, "sc2")
                    m8 = em.sbs(ph, [128, 8], F32, "m8")
                    azs = em.sbs(ph, [128, 4, 512], F32, "azs")
                    brA = em.sbs(ph, [128, 4, 512], BF16, "brA")
                    tcount = [0]
                    for qb in range(4):
                        q0 = 512 * qb
                        for j in range(4):
                            wz_ = load_w(l, "az", j)
                            pb = fm_proj(wz_, j, q0, 512)
                            em.act(azs[:, j, :], pb[:, 0:512], AF.Silu)
                        for h in range(8):
                            g, r = h // 4, h % 4
                            hs = slice(64 * g, 64 * g + 64)
                            mcb = mcb2[h % 2]
                            em.dma("sp", mcb[:, :], mc_in[h, :, q0:q0 + 512], key=mcb)
                            k = tcount[0] % 2
                            tcount[0] += 1
                            ps = PS[k]
                            em.mm(ps[0:127, 0:512], [(kcmpT[hs, 0:127], qT[hs, r, q0:q0 + 512])])
                            em.tt(sbt[k][0:127, :], ps[0:127, 0:512], mcb[0:127, :], ALU.add)
                            em.act(Pt[k][0:127, :], sbt[k][0:127, :], AF.Exp)
                            po = PS[2 + k]
                            pov = po[:, 0:388].re("p (s c) -> p s c", c=97)
                            for sub in range(4):
                                em.mm(pov[:, sub, :], [(Pt[k][0:127, sub * 128:(sub + 1) * 128], rhsc[0:127, g, :])])
                            em.ts(rden[:, :], pov[:, :, 64], 1e-30, None, op0=ALU.max)
                            em.recip(rden[:, :], rden[:, :])
                            em.tt(coef[:, :], rden[:, :], gates[:, 4 * qb:4 * qb + 4, 3 * h], ALU.mult)
                            em.tt(oa[:, :, 64 * h:64 * h + 64], pov[:, :, 0:64], bc(coef[:, :], [128, 4, 64]), ALU.mult)
                            if r == 0:
                                em.tt(imp[:, :, g, :], pov[:, :, 65:97], bc(rden[:, :], [128, 4, 32]), ALU.mult)
                            else:
                                em.tt(tmpi[:, :, :], pov[:, :, 65:97], bc(rden[:, :], [128, 4, 32]), ALU.mult)
                                em.tt(imp[:, :, g, :], imp[:, :, g, :], tmpi[:, :, :], ALU.add, e="pool")
                        for sub in range(4):
                            for g in range(2):
                                em.tt(sc[:, :], imp[:, sub, g, :], keep[:, 4 * qb + sub, :], ALU.mult)
                                em.tt(sc[:, :], sc[:, :], addm[:, 4 * qb + sub, :], ALU.add)
                                em.top8(m8[:, :], sc[:, :])
                                em.mrep(sc2[:, :], m8[:, :], sc[:, :], -3.0e38)
                                em.top8(m8[:, :], sc2[:, :])
                                em.ts(sc2[:, :], sc[:, :], m8[:, 7:8], -NEGM, op0=ALU.is_ge, op1=ALU.mult)
                                em.ts(sc2[:, :], sc2[:, :], NEGM, None, op0=ALU.add)
                                pt = PS[4 + (2 * sub + g) % 2]
                                em.mm(pt[0:32, 0:128], [(sc2[:, :], ident[:, :])])
                                em.copy(penT[:, g, sub * 128:(sub + 1) * 128], pt[0:32, 0:128], e="act")
                        for h in range(8):
                            g, r = h // 4, h % 4
                            hs = slice(64 * g, 64 * g + 64)
                            wsel = 512 * qb + 896
                            msel, mwin = msel2[h % 2], mwin2[h % 2]
                            em.dma("pool", msel[:, 0:wsel], msel_in[h, :, 0:wsel], key=msel)
                            em.dma("pool", mwin[:, :], mwin_in[h, :, :], key=mwin)
                            for br_, b_, tab, kts in ((1, 1, msel, range(0, 4 * qb + 4)), (2, 2, mwin, range(max(0, 4 * qb - 4), 4 * qb + 4))):
                                kts = list(kts)
                                for kt in kts:
                                    dl = 512 * qb - 128 * kt
                                    k = tcount[0] % 2
                                    tcount[0] += 1
                                    ps = PS[k]
                                    pairs = [(kT[hs, b_, kt * 128:(kt + 1) * 128], qT[hs, r, q0:q0 + 512])]
                                    if br_ == 1:
                                        pairs.append((wex[:, kt * 128:(kt + 1) * 128], penT[:, g, :]))
                                    em.mm(ps[:, 0:512], pairs)
                                    em.tt(sbt[k][:, :], ps[:, 0:512], tab[:, dl + 384:dl + 896], ALU.add)
                                    em.act(Pt[k][:, :], sbt[k][:, :], AF.Exp)
                                    for sub in range(4):
                                        em.mm1(PS[4 + sub][:, 0:65], Pt[k][:, sub * 128:(sub + 1) * 128], vaug[:, kt, 2 * b_ + g, :],
                                               start=(kt == kts[0]), stop=(kt == kts[-1]))
                                for sub in range(4):
                                    em.recip(rden[:, sub:sub + 1], PS[4 + sub][:, 64:65])
                                em.tt(coef[:, :], rden[:, :], gates[:, 4 * qb:4 * qb + 4, 3 * h + br_], ALU.mult)
                                for sub in range(4):
                                    em.ts(tmpo[:, sub, :], PS[4 + sub][:, 0:64], coef[:, sub:sub + 1], None, op0=ALU.mult)
                                em.tt(oa[:, :, 64 * h:64 * h + 64], oa[:, :, 64 * h:64 * h + 64], tmpo[:, :, :], ALU.add, e="pool")
                        for sub in range(4):
                            for j in range(4):
                                pt = PS[4 + j % 2]
                                em.mm(pt[:, 0:128], [(oa[:, sub, 128 * j:128 * j + 128], ident[:, :])])
                                em.tt(brA[:, j, sub * 128:(sub + 1) * 128], pt[:, 0:128], azs[:, j, sub * 128:(sub + 1) * 128], ALU.mult)
                        em.dma("sp", BR[0:4, :, q0:q0 + 512].re("a p t -> p a t"), brA[:, :, :], key=BR)
                    em.barrier()
                if "SAMP" in S:
                  with em.scope() as ph:
                    F_, B_ = F32, BF16
                    wag = em.sbs(ph, [128, 8, 24], B_, "wag")
                    selg = em.sbs(ph, [24, 4, 3, 128], F_, "selg")
                    gsm = em.sbs(ph, [24, 16], F_, "gsm")
                    gfm = em.sbs(ph, [128, 4, 3, 16], F_, "gfm")
                    qs = em.sbs(ph, [128, 4, 16], B_, "qs")
                    oaT = em.sbs(ph, [128, 4, 16], F_, "oaT")
                    mcs = em.sbs(ph, [128, 4, 2, 16], F_, "mcs")
                    coversf = em.sbs(ph, [128, 4, 129], F_, "coversf")
                    rhscs = em.sbs(ph, [128, 4, 2, 194], B_, "rhscs")
                    msels = em.sbs(ph, [128, 64, 2, 16], F_, "msels")
                    mwins = em.sbs(ph, [128, 4, 2, 16], F_, "mwins")
                    msnew = em.sbs(ph, [16, 4, 2, 16], F_, "msnew")
                    keeps = em.sbs(ph, [4, 129], F_, "keeps")
                    adds = em.sbs(ph, [4, 129], F_, "adds")
                    qsel = em.sbs(ph, [16, 4], F_, "qsel")
                    maskc = em.sbs(ph, [16, 2, 2], F_, "maskc")
                    lsel = em.sbs(ph, [4, 2, 128], F_, "lsel")
                    iota = em.sbs(ph, [128, 1], F_, "iota")
                    pti = em.sbs(ph, [128, 256], I32, "pti")
                    ptf = em.sbs(ph, [128, 256], F_, "ptf")
                    idx = em.sbs(ph, [128, 4, 64], I32, "idx")
                    X = em.sbs(ph, [128, 8208], B_, "X")
                    wc = em.sbs(ph, [128, 32, 128], B_, "wc")
                    kcs = em.sbs(ph, [128, 512], B_, "kcs")
                    pgb = [em.sbs(ph, [128, 4, 256], B_, f"pgb{i}") for i in range(2)]
                    KsT = [em.sbs(ph, [128, 512], B_, f"KsT{i}") for i in range(2)]
                    Es = [em.sbs(ph, [128, 4, 2, 16], B_, f"Es{i}") for i in range(2)]
                    sb_ = [em.sbs(ph, [128, 4, 2, 16], F_, f"ssb{i}") for i in range(2)]
                    sbn = em.sbs(ph, [16, 2, 16], F_, "sbn")
                    En = em.sbs(ph, [16, 2, 16], B_, "En")
                    penTf = em.sbs(ph, [128, 2, 64, 4], F_, "penTf")
                    sc = em.sbs(ph, [4, 129], F_, "ssc")
                    sc2 = em.sbs(ph, [4, 129], F_, "ssc2")
                    m8 = em.sbs(ph, [4, 8], F_, "sm8")
                    Xe = em.sbs(ph, [4, 64, 4], F_, "Xe")
                    Xo = em.sbs(ph, [4, 64, 4], F_, "Xo")
                    on = em.sbs(ph, [16, 64], F_, "on")
                    Om = em.sbs(ph, [16, 2, 2, 64], F_, "Om")
                    rd16 = em.sbs(ph, [16, 1], F_, "rd16")
                    impn = em.sbs(ph, [16, 129], F_, "impn")
                    tmp4 = em.sbs(ph, [128, 4], F_, "tmp4")
                    onesb = em.sbs(ph, [128, 1], B_, "onesb")
                    azsS = em.sbs(ph, [128, 4, 16], F_, "azsS")
                    brS = em.sbs(ph, [128, 4, 16], B_, "brS")
                    em.dma("pool", wag[:, :, :], wag_in[l].rearrange("p (a b) -> p a b", a=8), key=wag)
                    em.dma("sp", selg[:], selg_in.rearrange("p (a b c) -> p a b c", a=4, b=3), key=selg)
                    em.dma("sp", mcs[:], mcs_in.rearrange("p (a b c) -> p a b c", a=4, b=2), key=mcs)
                    em.dma("sp", coversf[:], covers_in.rearrange("p (a b) -> p a b", a=4), key=coversf)
                    em.dma("sp", msels[:], msels_in.rearrange("p (a b c) -> p a b c", a=64, b=2), key=msels)
                    em.dma("sp", mwins[:], mwins_in.rearrange("p (a b c) -> p a b c", a=4, b=2), key=mwins)
                    em.dma("sp", msnew[:], msnew_in.rearrange("p (a b c) -> p a b c", a=4, b=2), key=msnew)
                    em.dma("sp", keeps[:], keeps_in[:, :], key=keeps)
                    em.dma("sp", adds[:], adds_in[:, :], key=adds)
                    em.dma("sp", qsel[:], qsel_in[:, :], key=qsel)
                    em.dma("sp", maskc[:], maskc_in.rearrange("p (a b) -> p a b", a=2), key=maskc)
                    em.dma("sp", lsel[:], lsel_in.rearrange("p (a b) -> p a b", a=2), key=lsel)
                    em.dma("sp", iota[:], iota_in[:, :], key=iota)
                    em.dma("sp", pti[:, :], ptab_in[0:1, :].broadcast_to([128, 256]), key=pti)
                    em.copy(ptf[:, :], pti[:, :])
                    em.ts(ptf[:, :], ptf[:, :], 128.0, iota[:, 0:1], op0=ALU.mult, op1=ALU.add)
                    em.copy(idx[:, :, :].re("p s i -> p (s i)"), ptf[:, :])
                    em.memset(onesb[:, :], 1.0, e="dve")
                    em.memset(rhscs[:, :, :, 64:65], 1.0, e="dve")
                    for g in range(2):
                        em.copy(rhscs[:, :, g, 65:194], coversf[:, :, :])
                    em.mm(PS[0][0:24, 0:16], [(wag[:, kc, :], hT[:, kc, T:TT]) for kc in range(8)])
                    em.act(gsm[:, :], PS[0][0:24, 0:16], AF.Sigmoid)
                    for j in range(4):
                        for br_ in range(3):
                            em.mm(PS[1][:, (3 * j + br_) * 16:(3 * j + br_) * 16 + 16], [(selg[:, j, br_, :], gsm[:, :])])
                    em.copy(gfm[:, :, :, :].re("p a b c -> p (a b c)"), PS[1][:, 0:192])
                    em.copy(qs[:, :, :].re("p s (c q) -> p s c q", q=4), qT[:, :, T:TT].re("p c (s q) -> p s c q", q=4))
                    first = {}

                    def branch_out(pnum, pden, g, br_, s):
                        em.ts(rd16[:, :], pden, 1e-30, None, op0=ALU.max)
                        em.recip(rd16[:, :], rd16[:, :])
                        em.ts(on[:, :], pnum, rd16[:, 0:1], None, op0=ALU.mult)
                        onv = View(on, on.t[:, :].unsqueeze(1).unsqueeze(1).broadcast_to([16, 2, 2, 64]))
                        em.tt(Om[:, :, :, :], onv, bc(maskc[:, :, :], [16, 2, 2, 64]), ALU.mult)
                        for jj in range(2):
                            j = 2 * g + jj
                            pj = PS[jj]
                            em.mm(pj[:, 0:4], [(Om[:, jj, :, :].re("p e d -> p (e d)"), qsel[:, :])])
                            if (s, j) not in first:
                                first[(s, j)] = 1
                                em.tt(oaT[:, j, 4 * s:4 * s + 4], pj[:, 0:4], gfm[:, j, br_, 4 * s:4 * s + 4], ALU.mult)
                            else:
                                em.tt(tmp4[:, :], pj[:, 0:4], gfm[:, j, br_, 4 * s:4 * s + 4], ALU.mult)
                                em.tt(oaT[:, j, 4 * s:4 * s + 4], oaT[:, j, 4 * s:4 * s + 4], tmp4[:, :], ALU.add)

                    def gather_group(pool_ap, s, pg, kcol, dst):
                        pb_ = pgb[pg % 2]
                        for pi in range(4):
                            em.idma(pb_[:, pi, :], pool_ap, idx[:, s, 4 * pg + pi:4 * pg + pi + 1], key=pb_)
                        pt = PS[pg % 2]
                        for pi in range(4):
                            em.mm(pt[:, pi * 128:(pi + 1) * 128], [(pb_[:, pi, kcol:kcol + 128], identb[:, :])])
                        em.copy(dst, pt[:, 0:512], e="act" if pg % 2 else "dve")
                        return pb_

                    def score_group(KT_, qsl, tab4, pen4, k):
                        for g in range(2):
                            hs = slice(64 * g, 64 * g + 64)
                            for pi in range(4):
                                em.mm(PS[2 + g][:, pi * 16:pi * 16 + 16], [(KT_[hs, pi * 128:(pi + 1) * 128], qsl[hs, :])])
                        for g in range(2):
                            em.tt(sb_[k][:, :, g, :], PS[2 + g][:, 0:64].re("p (t x) -> p t x", x=16), tab4[:, :, g, :], ALU.add)
                        fl = lambda v: v.re("p a b c -> p (a b c)")
                        if pen4 is not None:
                            for g in range(2):
                                v_ = sb_[k][:, :, g, :].re("p t (c q) -> p t c q", q=4)
                                p4 = pen4(g)
                                em.tt(v_, v_, View(p4.buf, p4.ap.unsqueeze(2).broadcast_to([128, 4, 4, 4])), ALU.add)
                        em.act(fl(Es[k][:, :, :, :]), fl(sb_[k][:, :, :, :]), AF.Exp)

                    import os
                    CUT = float(os.environ.get("SAMP_CUT", "99"))

                    class StopSamp(Exception):
                        pass

                    def cut(n):
                        if CUT <= n:
                            raise StopSamp()
                    for s in range(4 if CUT >= 99 else 1):
                      try:
                        cut(1)
                        qsl = qs[:, s, :]
                        for kv_ in range(2):
                            em.dma("pool", wc[:, :, :], wcmp_in[l, kv_].rearrange("p (a b) -> p a b", a=32), key=wc)
                            em.memset(X[:, 8192:8208], 0.0, e="dve")
                            em.copy(X[:, 8192:8196], kT[:, 0, T + 4 * s:T + 4 * s + 4] if kv_ == 0 else vcT[:, T + 4 * s:T + 4 * s + 4])
                            for pg in range(16):
                                gather_group(cmp_pool[l], s, pg, 128 * kv_, X[:, 512 * pg:512 * pg + 512])
                                cut(2)
                            X16 = X[:, :].re("p (n s) -> p n s", s=16)
                            if kv_ == 0:
                                em.mm(PS[2][:, 0:512], [(wc[:, ms, :], X16[:, ms // 16:ms // 16 + 512, ms % 16]) for ms in range(32)])
                                em.copy(kcs[:, :], PS[2][:, 0:512], e="act")
                            else:
                                for nt in range(4):
                                    pz = PS[2 + nt % 2]
                                    em.mm(pz[:, 0:128], [(X16[:, ms // 16 + 128 * nt:ms // 16 + 128 * nt + 128, ms % 16], wc[:, ms, :]) for ms in range(32)])
                                    em.copy(rhscs[:, nt, :, 0:64], pz[:, 0:128].re("p (g e) -> p g e", g=2), e="act")
                        cut(3)
                        score_group(kcs, qsl, mcs[:, :, :, :], None, 0)
                        cut(3.1)
                        for g in range(2):
                            pc = PS[4 + g]
                            em.mm(pc[0:16, 0:194], [(Es[0][:, nt, g, :], rhscs[:, nt, g, :]) for nt in range(4)])
                            cut(3.2)
                            branch_out(pc[0:16, 0:64], pc[0:16, 64:65], g, 0, s)
                            cut(3.3)
                            em.ts(impn[:, :], pc[0:16, 65:194], rd16[:, 0:1], None, op0=ALU.mult)
                            pi_ = PS[3]
                            em.mm(pi_[0:4, 0:129], [(qsel[:, :], impn[:, :])])
                            em.tt(sc[:, :], pi_[0:4, 0:129], keeps[:, :], ALU.mult)
                            em.tt(sc[:, :], sc[:, :], adds[:, :], ALU.add)
                            cut(3.4)
                            em.top8(m8[:, :], sc[:, :])
                            em.mrep(sc2[:, :], m8[:, :], sc[:, :], -3.0e38)
                            em.top8(m8[:, :], sc2[:, :])
                            em.ts(sc2[:, :], sc[:, :], m8[:, 7:8], -NEGM, op0=ALU.is_ge, op1=ALU.mult)
                            em.ts(sc2[:, :], sc2[:, :], NEGM, None, op0=ALU.add)
                            cut(3.5)
                            pe2 = sc2[:, 0:128].re("p (i two) -> p i two", two=2)
                            i4 = View(ident, ident.t[0:4, 0:4].unsqueeze(1).broadcast_to([4, 64, 4]))
                            em.tt(Xe[:, :, :], bc(pe2[:, :, 0], [4, 64, 4]), i4, ALU.mult)
                            em.tt(Xo[:, :, :], bc(pe2[:, :, 1], [4, 64, 4]), i4, ALU.mult)
                            cut(3.6)
                            pp = PS[3]
                            em.mm(pp[:, 256:512], [(lsel[:, 0, :], Xe[:, :, :].re("p i q -> p (i q)")), (lsel[:, 1, :], Xo[:, :, :].re("p i q -> p (i q)"))])
                            em.copy(penTf[:, g, :, :].re("p i q -> p (i q)"), pp[:, 256:512], e="act")
                        cut(4)
                        for (br_, bidx) in ((1, 1), (2, 2)):
                            ngrp = 16 if br_ == 1 else 1
                            for pg in range(ngrp):
                                k = pg % 2
                                if br_ == 1:
                                    pb_ = gather_group(sel_pool[l], s, pg, 0, KsT[k][:, :])
                                    tab4 = msels[:, 4 * pg:4 * pg + 4, :, :]
                                    pen4 = (lambda g, pg=pg: penTf[:, g, 4 * pg:4 * pg + 4, :])
                                else:
                                    pb_ = pgb[0]
                                    em.dma("pool", pb_[:, :, :], cwin[l, s].rearrange("(t p) c -> p t c", p=128), key=pb_)
                                    pt = PS[0]
                                    for pi in range(4):
                                        em.mm(pt[:, pi * 128:(pi + 1) * 128], [(pb_[:, pi, 0:128], identb[:, :])])
                                    em.copy(KsT[k][:, :], pt[:, 0:512])
                                    tab4 = mwins[:, :, :, :]
                                    pen4 = None
                                score_group(KsT[k], qsl, tab4, pen4, k)
                                for pi in range(4):
                                    for g in range(2):
                                        st_ = (pg == 0 and pi == 0)
                                        em.mm1(PS[4 + g][0:16, 0:64], Es[k][:, pi, g, :], pb_[:, pi, 128 + 64 * g:128 + 64 * g + 64], start=st_, stop=False)
                                        em.mm1(PS[6 + g][0:16, 0:1], Es[k][:, pi, g, :], onesb[:, 0:1], start=st_, stop=False)
                            for g in range(2):
                                hs = slice(64 * g, 64 * g + 64)
                                em.mm(PS[2 + g][0:16, 0:16], [(kT[hs, bidx, T:TT], qsl[hs, :])])
                            for g in range(2):
                                em.tt(sbn[:, g, :], PS[2 + g][0:16, 0:16], msnew[:, s, g, :], ALU.add)
                            em.act(En[:, :, :], sbn[:, :, :], AF.Exp)
                            for g in range(2):
                                em.mm1(PS[4 + g][0:16, 0:64], En[:, g, :], vaug[0:16, 16, 2 * bidx + g, 0:64], start=False, stop=True)
                                em.mm1(PS[6 + g][0:16, 0:1], En[:, g, :], vaug[0:16, 16, 2 * bidx + g, 64:65], start=False, stop=True)
                            for g in range(2):
                                branch_out(PS[4 + g][0:16, 0:64], PS[6 + g][0:16, 0:1], g, br_, s)
                            cut(4 + br_)
                      except StopSamp:
                        pass
                    for j in range(4):
                        wz_ = load_w(l, "az", j)
                        pb = fm_proj(wz_, j, T, TS)
                        em.act(azsS[:, j, :], pb[:, 0:TS], AF.Silu)
                    em.tt(brS[:, :, :], oaT[:, :, :], azsS[:, :, :], ALU.mult)
                    em.dma("sp", BR[0:4, :, T:TT].re("a p t -> p a t"), brS[:, :, :], key=BR)
                em.barrier()

            if "M" in S:
                with em.scope() as ph:
                    cols = em.sbs(ph, [64, NT64, 8], F32, "mcols")
                    decb = em.sbs(ph, [128, 4, NT64], F32, "decb")
                    mhn = em.sbs(ph, [64, 128], F32, "mhn")
                    wsmb = em.sbs(ph, [128, 8, 16], BF16, "wsmb")
                    em.dma("sp", mhn[:], mhn_in[l], key=mhn)
                    em.dma("pool", wsmb[:, :, :], wsm[l].rearrange("p (a b) -> p a b", a=8), key=wsmb)
                    with em.scope() as ph2:
                        rA = em.sbs(ph2, [4, TT], F32, "rA")
                        rB = em.sbs(ph2, [4, TT], F32, "rB")
                        rC = em.sbs(ph2, [4, TT], F32, "rC")
                        rD = em.sbs(ph2, [4, TT], F32, "rD")
                        hc = em.sbs(ph2, [4, 4], F32, "hc")
                        nbf = em.sbs(ph2, [4, 1], F32, "nbf")
                        m0c = em.sbs(ph2, [4, 4], F32, "m0c")
                        nm0 = em.sbs(ph2, [4, 4], F32, "nm0")
                        Mp = em.sbs(ph2, [4, 32], F32, "Mp")
                        dec = em.sbs(ph2, [4, NT64], F32, "dec")
                        mfin = em.sbs(ph2, [4, 5], F32, "mfin")
                        em.dma("sp", hc[:], gcol_in[l], key=hc)
                        em.dma("sp", m0c[:], mm0[l], key=m0c)
                        em.ts(nbf[:], hc[:, 1:2], -1.0, None, op0=ALU.mult)
                        em.ts(nm0[:], m0c[:], -1.0, None, op0=ALU.mult)
                        for bi, (t0, n) in enumerate(BLOCKS):
                            for gi, dst in ((0, rA), (1, rB)):
                                pb = PS[(2 * bi + gi) % 4]
                                em.mm(pb[0:4, 0:n], [(wsmb[:, kc, 4 * gi:4 * gi + 4], hT[:, kc, t0:t0 + n]) for kc in range(8)])
                                if gi == 0:
                                    em.ts(dst[:, t0:t0 + n], pb[0:4, 0:n], hc[:, 0:1], None, op0=ALU.add)
                                else:
                                    em.act(dst[:, t0:t0 + n], pb[0:4, 0:n], AF.Softplus, scale=-1.0, bias=nbf[:, 0:1])
                        em.ts(rB[:], rB[:], -1.0, None, op0=ALU.mult)
                        em.scan(rD[:, 0:T], rB[:, 0:T], rB[:, 0:T], 0.0, ALU.add, ALU.bypass)
                        em.tt(rA[:, 0:T], rA[:, 0:T], rD[:, 0:T], ALU.subtract)
                        em.scan(rC[:, 0:T], rA[:, 0:T], rA[:, 0:T], 0.0, ALU.max, ALU.max)
                        em.memset(Mp[:, 0:1], 0.0, e="dve")
                        Rv = rC[:, 0:T].re("p (c t) -> p c t", t=64)
                        em.copy(Mp[:, 1:32], Rv[:, 0:31, 63])
                        em.tt(dec[:, 0:32], Mp[:, 0:32], Rv[:, :, 63], ALU.subtract)
                        em.tt(mfin[:, 0:1], rC[:, T - 1:T], rD[:, T - 1:T], ALU.add)
                        Mpb = bc(Mp[:, 0:32], [4, 32, 64])
                        em.tt(rA[:, 0:T].re("p (c t) -> p c t", t=64), rA[:, 0:T].re("p (c t) -> p c t", t=64), Mpb, ALU.subtract)
                        em.tt(rD[:, 0:T].re("p (c t) -> p c t", t=64), rD[:, 0:T].re("p (c t) -> p c t", t=64), Mpb, ALU.add)
                        for s in range(4):
                            c0 = T + 4 * s
                            em.scan(rD[:, c0:c0 + 4], rB[:, c0:c0 + 4], rB[:, c0:c0 + 4], 0.0, ALU.add, ALU.bypass)
                            em.tt(rA[:, c0:c0 + 4], rA[:, c0:c0 + 4], rD[:, c0:c0 + 4], ALU.subtract)
                            em.scan(rC[:, c0:c0 + 4], rA[:, c0:c0 + 4], rA[:, c0:c0 + 4], m0c[:, s:s + 1], ALU.max, ALU.max)
                            em.tt(dec[:, 32 + s:33 + s], m0c[:, s:s + 1], rC[:, c0 + 3:c0 + 4], ALU.subtract)
                            em.tt(mfin[:, 1 + s:2 + s], rC[:, c0 + 3:c0 + 4], rD[:, c0 + 3:c0 + 4], ALU.add)
                            em.ts(rA[:, c0:c0 + 4], rA[:, c0:c0 + 4], m0c[:, s:s + 1], None, op0=ALU.subtract)
                            em.ts(rD[:, c0:c0 + 4], rD[:, c0:c0 + 4], m0c[:, s:s + 1], None, op0=ALU.add)
                        em.act(rA[:], rA[:], AF.Exp)
                        em.act(rD[:], rD[:], AF.Exp, scale=-1.0)
                        em.act(dec[:], dec[:], AF.Exp)
                        store("sp", o_mm_p[l].rearrange("(h o) -> h o", o=1), mfin[:, 0:1])
                        store("sp", o_mm_s[l], mfin[:, 1:5])
                        pc = PS[4]
                        for i in range(NT64):
                            t0, n = t64(i)
                            em.mm(pc[0:n, 8 * i:8 * i + 4], [(rA[:, t0:t0 + n], ident[0:4, 0:4])])
                            em.mm(pc[0:n, 8 * i + 4:8 * i + 8], [(rD[:, t0:t0 + n], ident[0:4, 0:4])])
                        em.copy(cols[:, 0:32, :], pc[0:64, 0:256].re("p (i c) -> p i c", c=8))
                        em.copy(cols[0:4, 32:36, :], pc[0:4, 256:288].re("p (i c) -> p i c", c=8))
                        pd = PS[5]
                        for h in range(4):
                            em.mm(pd[:, h * NT64:(h + 1) * NT64], [(onehot[:, h, :], dec[:, :])])
                        em.copy(decb[:], pd[:, 0:4 * NT64].re("p (h c) -> p h c", h=4))
                        em.barrier()
                    mqT = em.sbs(ph, [128, TT], BF16, "mqT")
                    mkT = em.sbs(ph, [128, TT], BF16, "mkT")
                    gmT = em.sbs(ph, [128, TT], F32, "gmT")
                    gtmp = em.sbs(ph, [128, 512], F32, "gtmp")
                    mktm = em.sbs(ph, [64, NT64, 128], BF16, "mktm")
                    vb = em.sbs(ph, [64, NT64, 129], BF16, "vb")
                    wtmb = em.sbs(ph, [128, 8, 256], BF16, "wtmmb")
                    Cst = em.sbs(ph, [128, 129], F32, "Cst")
                    Cb = em.sbs(ph, [128, 129], BF16, "Cb")
                    At = [em.sbs(ph, [64, 64], BF16, f"At{i}") for i in range(2)]
                    dn = [em.sbs(ph, [64, 2], F32, f"dn{i}") for i in range(2)]
                    hm = [em.sbs(ph, [64, 128], F32, f"hm{i}") for i in range(2)]
                    hsq = em.sbs(ph, [64, 128], F32, "hsq")
                    hnb = [em.sbs(ph, [64, 128], BF16, f"hnb{i}") for i in range(2)]
                    brh = em.sbs(ph, [128, TT], BF16, "brh")
                    for h in range(4):
                        em.dma("pool", wtmb[:, :, :], wtmm[l, h].rearrange("p (a b) -> p a b", a=8), key=wtmb)
                        wq_, wk_, wo_, wz_ = (load_w(l, nm, h) for nm in ("mq", "mk", "mo", "mz"))
                        for bi, (t0, n) in enumerate(BLOCKS):
                            pb = fm_proj(wq_, 0, t0, n)
                            em.copy(mqT[:, t0:t0 + n], pb[:, 0:n], e="act")
                            pb = fm_proj(wk_, 1, t0, n)
                            em.ts(mkT[:, t0:t0 + n], pb[:, 0:n], 128.0 ** -0.5, None, op0=ALU.mult)
                            pb = fm_proj(wo_, 0, t0, n)
                            em.act(gmT[:, t0:t0 + n], pb[:, 0:n], AF.Sigmoid)
                            pb = fm_proj(wz_, 1, t0, n)
                            em.act(gtmp[:, 0:n], pb[:, 0:n], AF.Silu)
                            em.tt(gmT[:, t0:t0 + n], gmT[:, t0:t0 + n], gtmp[:, 0:n], ALU.mult)
                        for i in range(NT64):
                            t0, n = t64(i)
                            pb = PS[2 + i % 2]
                            em.mm(pb[0:n, 0:256], [(hT[:, kc, t0:t0 + n], wtmb[:, kc, :]) for kc in range(8)])
                            em.ts(mktm[0:n, i, :], pb[0:n, 0:128], 128.0 ** -0.5, None, op0=ALU.mult)
                            em.ts(vb[0:n, i, 0:128], pb[0:n, 128:256], cols[0:n, i, h:h + 1], None, op0=ALU.mult)
                            em.copy(vb[0:n, i, 128:129], cols[0:n, i, h:h + 1], e="act")

                        pend = []

                        def flush():
                            while pend:
                                pend.pop(0)()

                        def chunk(i):
                            t0, n = t64(i)
                            k = i % 2
                            pa = PS[4 + k]
                            em.mm(pa[0:n, 0:n], [(mkT[:, t0:t0 + n], mqT[:, t0:t0 + n])])
                            em.tt(At[k][0:n, 0:n], pa[0:n, 0:n], masku[0:n, 0:n], ALU.mult)
                            po = PS[6 + k]
                            em.mm(po[0:n, 0:129], [(mqT[:, t0:t0 + n], Cb[:, :]), (At[k][0:n, 0:n], vb[0:n, i, :])])
                            pk = PS[k]
                            em.mm(pk[:, 0:129], [(mktm[0:n, i, :], vb[0:n, i, :])])
                            flush()
                            em.ts(Cst[:, :], Cst[:, :], decb[:, h, i:i + 1], None, op0=ALU.mult)
                            em.stt(Cst[:, :], pk[:, 0:129], decb[:, h, i:i + 1], Cst[:, :], ALU.mult, ALU.add)
                            em.copy(Cb[:, :], Cst[:, :], e="act")
                            em.act(dn[k][0:n, 0:1], po[0:n, 128:129], AF.Abs)
                            em.tt(dn[k][0:n, 0:1], dn[k][0:n, 0:1], cols[0:n, i, 4 + h:5 + h], ALU.max)
                            em.recip(dn[k][0:n, 0:1], dn[k][0:n, 0:1])
                            em.ts(hm[k][0:n, :], po[0:n, 0:128], dn[k][0:n, 0:1], None, op0=ALU.mult)

                            def tail():
                                em.act(hsq[0:n, :], hm[k][0:n, :], AF.Square, accum_out=dn[k][0:n, 1:2])
                                em.act(dn[k][0:n, 1:2], dn[k][0:n, 1:2], AF.Sqrt, scale=1.0 / 128, bias=EPS)
                                em.recip(dn[k][0:n, 1:2], dn[k][0:n, 1:2])
                                em.stt(hnb[k][0:n, :], hm[k][0:n, :], dn[k][0:n, 1:2], mhn[0:n, :], ALU.mult, ALU.mult)
                                pt = PS[2 + k]
                                em.mm(pt[:, 0:n], [(hnb[k][0:n, :], identb[0:n, 0:n])])
                                em.tt(brh[:, t0:t0 + n], pt[:, 0:n], gmT[:, t0:t0 + n], ALU.mult)
                            pend.append(tail)

                        em.memset(Cst[:, :], 0.0, e="dve")
                        em.memset(Cb[:, :], 0.0, e="dve")
                        for i in range(32):
                            chunk(i)
                        flush()
                        store("sp", o_mC_p[l, h], Cst[:, 0:128])
                        store("sp", o_mn_p[l, h].rearrange("(p o) -> p o", o=1), Cst[:, 128:129])
                        for s in range(4):
                            em.dma("sp", Cst[:, 0:128], mC0[l, s, h], key=Cst)
                            em.dma("sp", Cst[:, 128:129], mn0[l, s, h].rearrange("(p o) -> p o", o=1), key=Cst)
                            em.copy(Cb[:, :], Cst[:, :], e="act")
                            chunk(32 + s)
                            flush()
                            store("sp", o_mC_s[l, s, h], Cst[:, 0:128])
                            store("sp", o_mn_s[l, s, h].rearrange("(p o) -> p o", o=1), Cst[:, 128:129])
                        em.dma("sp", BR[4 + h, :, :], brh[:, :], key=BR)
                    em.barrier()

            if "G" in S:
                with em.scope() as ph:
                    ghn = em.sbs(ph, [64, 128], F32, "ghn")
                    wsmb = em.sbs(ph, [128, 8, 16], BF16, "wsmb")
                    gcw = em.sbs(ph, [128, 12, 4], F32, "gcw")
                    Gr = em.sbs(ph, [4, TT], F32, "Gr")
                    Bg = em.sbs(ph, [4, TT], F32, "Bg")
                    gct = em.sbs(ph, [64, NT64, 16], F32, "gct")
                    eGLb = em.sbs(ph, [128, 4, NT64], F32, "eGLb")
                    em.dma("sp", ghn[:], ghn_in[l], key=ghn)
                    em.dma("sp", gcw[:], gcw_in[l].rearrange("j p t -> p j t"), key=gcw)
                    em.dma("pool", wsmb[:, :, :], wsm[l].rearrange("p (a b) -> p a b", a=8), key=wsmb)
                    with em.scope() as ph2:
                        rA = em.sbs(ph2, [4, TT], F32, "grA")
                        rK = em.sbs(ph2, [4, TT], F32, "grK")
                        rDf = em.sbs(ph2, [4, TT], F32, "grD")
                        rm = em.sbs(ph2, [4, TT], F32, "grm")
                        hc = em.sbs(ph2, [4, 4], F32, "ghc")
                        nA = em.sbs(ph2, [4, 1], F32, "gnA")
                        GL = em.sbs(ph2, [4, NT64], F32, "GL")
                        em.dma("sp", hc[:], gcol_in[l], key=hc)
                        em.dma("sp", rm[:], rmask_in[:, :], key=rm)
                        em.act(nA[:], hc[:, 2:3], AF.Exp)
                        em.ts(nA[:], nA[:], -1.0, None, op0=ALU.mult)
                        for bi, (t0, n) in enumerate(BLOCKS):
                            for gi in (2, 3):
                                pb = PS[(2 * bi + gi) % 4]
                                em.mm(pb[0:4, 0:n], [(wsmb[:, kc, 4 * gi:4 * gi + 4], hT[:, kc, t0:t0 + n]) for kc in range(8)])
                                if gi == 2:
                                    em.act(rA[:, t0:t0 + n], pb[0:4, 0:n], AF.Softplus, bias=hc[:, 3:4])
                                else:
                                    em.act(Bg[:, t0:t0 + n], pb[0:4, 0:n], AF.Sigmoid)
                        em.ts(rA[:], rA[:], nA[:, 0:1], None, op0=ALU.mult)
                        em.scan(Gr[:], rm[:], rA[:], 0.0, ALU.mult, ALU.add)
                        em.act(rK[:], Gr[:], AF.Exp)
                        em.tt(rK[:], rK[:], Bg[:], ALU.mult)
                        Gp = Gr[:, 0:T].re("p (c t) -> p c t", t=64)
                        Gs = Gr[:, T:TT].re("p (c t) -> p c t", t=4)
                        em.copy(GL[:, 0:32], Gp[:, :, 63])
                        em.copy(GL[:, 32:36], Gs[:, :, 3])
                        em.tt(rDf[:, 0:T].re("p (c t) -> p c t", t=64), bc(GL[:, 0:32], [4, 32, 64]), Gp, ALU.subtract)
                        em.tt(rDf[:, T:TT].re("p (c t) -> p c t", t=4), bc(GL[:, 32:36], [4, 4, 4]), Gs, ALU.subtract)
                        em.act(rDf[:], rDf[:], AF.Exp)
                        em.act(GL[:], GL[:], AF.Exp)
                        em.ts(rA[:], Gr[:], -1.0, None, op0=ALU.mult)
                        for i in range(NT64):
                            t0, n = t64(i)
                            pc = PS[4 + (i // 18)]
                            o = 16 * (i % 18)
                            for q, row in enumerate((Bg, rK, rDf, rA)):
                                em.mm(pc[0:n, o + 4 * q:o + 4 * q + 4], [(row[:, t0:t0 + n], ident[0:4, 0:4])])
                        em.copy(gct[:, 0:18, :], PS[4][0:64, 0:288].re("p (i c) -> p i c", c=16))
                        em.copy(gct[:, 18:32, :], PS[5][0:64, 0:224].re("p (i c) -> p i c", c=16))
                        em.copy(gct[0:4, 32:36, :], PS[5][0:4, 224:288].re("p (i c) -> p i c", c=16))
                        pd = PS[6]
                        for h in range(4):
                            em.mm(pd[:, h * NT64:(h + 1) * NT64], [(onehot[:, h, :], GL[:, :])])
                        em.copy(eGLb[:], pd[:, 0:4 * NT64].re("p (h c) -> p h c", h=4))
                        em.barrier()
                    gqT = em.sbs(ph, [128, TT], BF16, "gqT")
                    gkT = em.sbs(ph, [128, TT], BF16, "gkT")
                    gkbT = em.sbs(ph, [128, TT], BF16, "gkbT")
                    gqGT = em.sbs(ph, [128, TT], BF16, "gqGT")
                    kbG = em.sbs(ph, [64, NT64, 128], BF16, "kbG")
                    kdec = em.sbs(ph, [64, NT64, 128], BF16, "kdec")
                    vbt = em.sbs(ph, [64, NT64, 128], BF16, "vbt")
                    attnT = em.sbs(ph, [64, NT64, 64], BF16, "attnT")
                    gg = em.sbs(ph, [128, TT], BF16, "gg")
                    brh = em.sbs(ph, [128, TT], BF16, "gbrh")
                    Ttb = em.sbs(ph, [64, 32, 64], BF16, "Ttb")
                    Ttb4 = em.sbs(ph, [4, 4, 4], BF16, "Ttb4")
                    Sst = em.sbs(ph, [128, 128], F32, "Sst")
                    Sb = em.sbs(ph, [128, 128], BF16, "Sb")
                    SP0 = 2051
                    for h in range(4):
                        with em.scope() as ph2:
                            wkA = em.sbs(ph2, [128, 2080], F32, "wkA")
                            wkB = em.sbs(ph2, [128, TT], F32, "wkB")
                            sqb = em.sbs(ph2, [128, 512], F32, "sqb")
                            qf = em.sbs(ph2, [128, 512], F32, "qf")
                            dtm = [em.sbs(ph2, [64, 64], F32, f"dtm{i}") for i in range(2)]
                            atf = [em.sbs(ph2, [64, 64], F32, f"atf{i}") for i in range(2)]
                            wz_ = load_w(l, "gz", h)
                            for bi, (t0, n) in enumerate(BLOCKS):
                                pb = fm_proj(wz_, bi, t0, n)
                                em.act(gg[:, t0:t0 + n], pb[:, 0:n], AF.Silu)
                            sview = wkA[:, SP0:SP0 + 28].re("p (s c) -> p s c", c=7)
                            for comp in range(3):
                                j = 4 * comp + h
                                w_ = load_w(l, "gqkv", j)
                                em.memset(wkA[:, 0:3], 0.0, e="dve")
                                for s_ in range(4):
                                    em.dma("sp", wkA[:, SP0 + 7 * s_:SP0 + 7 * s_ + 3], gcv0[l, s_, j], key=wkA)
                                for bi, (t0, n) in enumerate(BLOCKS):
                                    pb = fm_proj(w_, bi, t0, n)
                                    if bi < 4:
                                        em.copy(wkA[:, 3 + t0:3 + t0 + n], pb[:, 0:n], e="act")
                                    else:
                                        em.copy(sview[:, :, 3:7], pb[:, 0:16].re("p (s c) -> p s c", c=4), e="act")
                                store("sp", o_gc_p[l, j], wkA[:, T:T + 3])
                                for s_ in range(4):
                                    store("sp", o_gc_s[l, s_, j], wkA[:, SP0 + 7 * s_ + 4:SP0 + 7 * s_ + 7])
                                for (src_, dst_) in ((lambda k: wkA[:, k:k + T], wkB[:, 0:T]),
                                                     (lambda k: sview[:, :, k:k + 4], wkB[:, T:TT].re("p (s c) -> p s c", c=4))):
                                    em.ts(dst_, src_(0), gcw[:, j, 0:1], None, op0=ALU.mult)
                                    for k in range(1, 4):
                                        em.stt(dst_, src_(k), gcw[:, j, k:k + 1], dst_, ALU.mult, ALU.add)
                                em.act(wkB[:], wkB[:], AF.Silu)
                                if comp < 2:
                                    for bi, (t0, n) in enumerate(BLOCKS):
                                        em.act(sqb[:, 0:n], wkB[:, t0:t0 + n], AF.Square)
                                        pss = PS[2 + bi % 2]
                                        em.mm(pss[:, 0:n], [(ones128[:, :], sqb[:, 0:n])])
                                        em.act(sqb[:, 0:n], pss[:, 0:n], AF.Sqrt, bias=EPS)
                                        em.recip(sqb[:, 0:n], sqb[:, 0:n])
                                        pbc = PS[4 + bi % 2]
                                        if comp == 0:
                                            em.stt(qf[:, 0:n], wkB[:, t0:t0 + n], 128.0 ** -0.5, sqb[:, 0:n], ALU.mult, ALU.mult)
                                            em.copy(gqT[:, t0:t0 + n], qf[:, 0:n], e="act")
                                            em.mm(pbc[:, 0:n], [(onehot[:, h, :], Gr[:, t0:t0 + n])])
                                            em.act(sqb[:, 0:n], pbc[:, 0:n], AF.Exp)
                                            em.tt(gqGT[:, t0:t0 + n], qf[:, 0:n], sqb[:, 0:n], ALU.mult)
                                        else:
                                            em.tt(qf[:, 0:n], wkB[:, t0:t0 + n], sqb[:, 0:n], ALU.mult)
                                            em.copy(gkT[:, t0:t0 + n], qf[:, 0:n], e="act")
                                            em.mm(pbc[:, 0:n], [(onehot[:, h, :], Bg[:, t0:t0 + n])])
                                            em.tt(gkbT[:, t0:t0 + n], qf[:, 0:n], pbc[:, 0:n], ALU.mult)
                                    if comp == 1:
                                        for i in range(NT64):
                                            t0, n = t64(i)
                                            pt = PS[6 + i % 2]
                                            em.mm(pt[0:n, 0:128], [(gkT[:, t0:t0 + n], identb[:, :])])
                                            em.ts(kbG[0:n, i, :], pt[0:n, 0:128], gct[0:n, i, 4 + h:5 + h], None, op0=ALU.mult)
                                            em.ts(kdec[0:n, i, :], pt[0:n, 0:128], gct[0:n, i, 8 + h:9 + h], None, op0=ALU.mult)
                                else:
                                    for i in range(NT64):
                                        t0, n = t64(i)
                                        pt = PS[6 + i % 2]
                                        em.mm(pt[0:n, 0:128], [(wkB[:, t0:t0 + n], ident[:, :])])
                                        em.ts(vbt[0:n, i, :], pt[0:n, 0:128], gct[0:n, i, h:h + 1], None, op0=ALU.mult)
                            for i in range(NT64):
                                t0, n = t64(i)
                                k = i % 2
                                pg = PS[k]
                                em.mm(pg[0:n, 0:n], [(onehot[:, h, 0:n], Gr[:, t0:t0 + n])])
                                em.stt(dtm[k][0:n, 0:n], pg[0:n, 0:n], gct[0:n, i, 12 + h:13 + h], maskb[0:n, 0:n], ALU.add, ALU.add)
                                em.act(dtm[k][0:n, 0:n], dtm[k][0:n, 0:n], AF.Exp)
                                pkk = PS[2 + k]
                                em.mm(pkk[0:n, 0:n], [(gkT[:, t0:t0 + n], gkbT[:, t0:t0 + n])])
                                em.tt(atf[k][0:n, 0:n], pkk[0:n, 0:n], dtm[k][0:n, 0:n], ALU.mult)
                                em.tt(atf[k][0:n, 0:n], atf[k][0:n, 0:n], strict[0:n, 0:n], ALU.mult)
                                if i < 32:
                                    em.dma("sp", SA[i, :].re("(j x) -> j x", x=64), atf[k][0:64, 0:64], key=SA)
                                else:
                                    em.dma("sp", SA4[i - 32, :].re("(j x) -> j x", x=4), atf[k][0:4, 0:4], key=SA4)
                                pkq = PS[4 + k]
                                em.mm(pkq[0:n, 0:n], [(gkT[:, t0:t0 + n], gqT[:, t0:t0 + n])])
                                em.tt(attnT[0:n, i, 0:n], pkq[0:n, 0:n], dtm[k][0:n, 0:n], ALU.mult)
                            em.barrier()
                        with em.scope() as ph2:
                            Atp = em.sbs(ph2, [128, 4096], F32, "Atp")
                            Tt = em.sbs(ph2, [128, 64, 16], F32, "Tt")
                            tmp = em.sbs(ph2, [128, 16, 64], F32, "tmpS")
                            red = em.sbs(ph2, [128, 16], F32, "red")
                            At4 = em.sbs(ph2, [4, 16], F32, "At4")
                            Tt4 = em.sbs(ph2, [4, 4, 4], F32, "Tt4")
                            for rb in range(4):
                                em.dma("sp", Atp[32 * rb:32 * rb + 32, :], SA[:, :], key=Atp)
                            em.dma("sp", Tt[:, :, :], ttinit_in.rearrange("p (j r) -> p j r", r=16), key=Tt)
                            em.dma("sp", At4[:, :], SA4[:, :], key=At4)
                            em.dma("sp", Tt4[:, :, :], ttinit4_in.rearrange("p (j r) -> p j r", r=4), key=Tt4)
                            for (A_, T_, L_, R_, np_) in ((Atp, Tt, 64, 16, 128), (At4, Tt4, 4, 4, 4)):
                                for j in range(L_ - 2, -1, -1):
                                    cnt = L_ - 1 - j
                                    in0 = T_[0:np_, j + 1:L_, :].re("p i r -> p r i")
                                    a1 = A_[0:np_, j * L_ + j + 1:j * L_ + L_]
                                    in1 = View(a1.buf, a1.ap.unsqueeze(1).broadcast_to([np_, R_, cnt]))
                                    em.tt(tmp[0:np_, 0:R_, 0:cnt], in0, in1, ALU.mult)
                                    em.reduce(red[0:np_, 0:R_], tmp[0:np_, 0:R_, 0:cnt])
                                    em.tt(T_[0:np_, j, :], T_[0:np_, j, :], red[0:np_, 0:R_], ALU.subtract)
                            for rb in range(4):
                                em.dma("sp", STT[:, :, 16 * rb:16 * rb + 16], Tt[32 * rb:32 * rb + 32, :, :], key=STT)
                            em.dma("sp", ST4[:, :], Tt4[:, :, :].re("p j r -> p (j r)"), key=ST4)
                            em.dma("pool", Ttb[:, :, :], STT[:, :, :].re("c j r -> j c r"), key=Ttb)
                            em.dma("pool", Ttb4[:, :, :], ST4[:, :].re("s (j r) -> j s r", r=4), key=Ttb4)
                            em.barrier()
                        with em.scope() as ph2:
                            usb = [em.sbs(ph2, [64, 128], F32, f"usb{i}") for i in range(2)]
                            wT = [em.sbs(ph2, [128, 64], BF16, f"wT{i}") for i in range(2)]
                            vnew = [em.sbs(ph2, [64, 128], BF16, f"vnew{i}") for i in range(2)]
                            og = [em.sbs(ph2, [64, 128], F32, f"og{i}") for i in range(2)]
                            osq = em.sbs(ph2, [64, 128], F32, "osq")
                            onb = [em.sbs(ph2, [64, 128], BF16, f"onb{i}") for i in range(2)]
                            rs2 = [em.sbs(ph2, [64, 1], F32, f"rs2{i}") for i in range(2)]

                            gpend = []

                            def gflush():
                                while gpend:
                                    gpend.pop(0)()

                            def gindep(i):
                                t0, n = t64(i)
                                k = i % 2
                                Ttv = Ttb[0:n, i, 0:n] if i < 32 else Ttb4[0:n, i - 32, 0:n]
                                pu = PS[k]
                                em.mm(pu[0:n, 0:128], [(Ttv, vbt[0:n, i, :])])
                                em.copy(usb[k][0:n, :], pu[0:n, 0:128], e="act")
                                pw = PS[2 + k]
                                em.mm(pw[:, 0:n], [(kbG[0:n, i, :], Ttv)])
                                em.copy(wT[k][:, 0:n], pw[:, 0:n], e="act")

                            def gchunk(i, nxt=None):
                                t0, n = t64(i)
                                k = i % 2
                                pws = PS[4 + k]
                                em.mm(pws[0:n, 0:128], [(wT[k][:, 0:n], Sb[:, :])])
                                em.tt(vnew[k][0:n, :], usb[k][0:n, :], pws[0:n, 0:128], ALU.subtract)
                                po = PS[6 + k]
                                em.mm(po[0:n, 0:128], [(gqGT[:, t0:t0 + n], Sb[:, :]), (attnT[0:n, i, 0:n], vnew[k][0:n, :])])
                                pS_ = PS[k]
                                em.mm(pS_[:, 0:128], [(kdec[0:n, i, :], vnew[k][0:n, :])])
                                em.stt(Sst[:, :], Sst[:, :], eGLb[:, h, i:i + 1], pS_[:, 0:128], ALU.mult, ALU.add)
                                em.copy(Sb[:, :], Sst[:, :], e="act")
                                if nxt is not None:
                                    gindep(nxt)
                                gflush()
                                em.copy(og[k][0:n, :], po[0:n, 0:128])

                                def tail():
                                    em.act(osq[0:n, :], og[k][0:n, :], AF.Square, accum_out=rs2[k][0:n, :])
                                    em.act(rs2[k][0:n, :], rs2[k][0:n, :], AF.Sqrt, scale=1.0 / 128, bias=EPS)
                                    em.recip(rs2[k][0:n, :], rs2[k][0:n, :])
                                    em.stt(onb[k][0:n, :], og[k][0:n, :], rs2[k][0:n, :], ghn[0:n, :], ALU.mult, ALU.mult)
                                    pt = PS[4 + k]
                                    em.mm(pt[:, 0:n], [(onb[k][0:n, :], identb[0:n, 0:n])])
                                    em.tt(brh[:, t0:t0 + n], pt[:, 0:n], gg[:, t0:t0 + n], ALU.mult)
                                gpend.append(tail)

                            em.memset(Sst[:, :], 0.0, e="dve")
                            em.memset(Sb[:, :], 0.0, e="dve")
                            gindep(0)
                            for i in range(32):
                                gchunk(i, i + 1 if i < 31 else None)
                            gflush()
                            store("sp", o_gS_p[l, h], Sst[:, :])
                            for s_ in range(4):
                                em.dma("sp", Sst[:, :], gS0[l, s_, h], key=Sst)
                                em.copy(Sb[:, :], Sst[:, :], e="act")
                                gindep(32 + s_)
                                gchunk(32 + s_)
                                gflush()
                                store("sp", o_gS_s[l, s_, h], Sst[:, :])
                            em.dma("sp", BR[8 + h, :, :], brh[:, :], key=BR)
                            em.barrier()
                    em.barrier()
            if "OUT" in S:
                with em.scope() as ph:
                    wbrb = em.sbs(ph, [128, 12, 1024], BF16, "wbrb")
                    woutb = em.sbs(ph, [128, 8, 1024], BF16, "woutb")
                    brb = em.sbs(ph, [128, 12, 512], BF16, "brb")
                    yT = em.sbs(ph, [128, 8, 512], BF16, "yT")
                    sg = [em.sbs(ph, [128, 512], F32, f"sg{i}") for i in range(3)]
                    acc = em.sbs(ph, [128, 512], F32, "macc")
                    tmp = em.sbs(ph, [128, 512], F32, "mtmp")
                    xt = [em.sbs(ph, [128, D], F32, f"oxt{i}") for i in range(2)]
                    for a in range(12):
                        em.dma("pool", wbrb[:, a, :], wbr_in[l, :, a * 1024:(a + 1) * 1024], key=wbrb)
                    for a in range(8):
                        em.dma("pool", woutb[:, a, :], wout_in[l, :, a * 1024:(a + 1) * 1024], key=woutb)
                    xcnt = 0
                    for bi, (t0, n) in enumerate(BLOCKS):
                        em.dma("sp", brb[:, :, 0:n], BR[:, :, t0:t0 + n].re("a p t -> p a t"), key=brb)
                        for dmc in range(8):
                            for i in range(3):
                                em.mm(PS[i][:, 0:n], [(wbrb[:, 4 * i + e, dmc * 128:(dmc + 1) * 128], brb[:, 4 * i + e, 0:n]) for e in range(4)])
                                w_ = load_w(l, "merge", i * 8 + dmc)
                                em.mm(PS[3 + i][:, 0:n], [(w_[:, kc, :], hT[:, kc, t0:t0 + n]) for kc in range(8)])
                                em.act(sg[i][:, 0:n], PS[3 + i][:, 0:n], AF.Sigmoid)
                            em.tt(acc[:, 0:n], sg[0][:, 0:n], PS[0][:, 0:n], ALU.mult)
                            em.tt(tmp[:, 0:n], sg[1][:, 0:n], PS[1][:, 0:n], ALU.mult)
                            em.tt(acc[:, 0:n], acc[:, 0:n], tmp[:, 0:n], ALU.add, e="pool")
                            em.tt(tmp[:, 0:n], sg[2][:, 0:n], PS[2][:, 0:n], ALU.mult)
                            em.tt(yT[:, dmc, 0:n], acc[:, 0:n], tmp[:, 0:n], ALU.add)
                        for sub in range((n + 127) // 128):
                            nn = min(128, n - 128 * sub)
                            r0 = t0 + 128 * sub
                            xb = xt[xcnt % 2]
                            xcnt += 1
                            if bi < 4:
                                src = xp[r0:r0 + nn, :] if l == 0 else YP[r0:r0 + nn, :]
                                dst = YP[r0:r0 + nn, :]
                            else:
                                src = xs[:, :] if l == 0 else YS[:, :]
                                dst = YS[:, :]
                            em.dma("sp", xb[0:nn, :], src, key=xb)
                            for half in range(2):
                                po = PS[6 + half]
                                em.mm(po[0:nn, 0:512], [(yT[:, dmc, 128 * sub:128 * sub + nn], woutb[:, dmc, 512 * half:512 * half + 512]) for dmc in range(8)])
                                em.tt(xb[0:nn, 512 * half:512 * half + 512], xb[0:nn, 512 * half:512 * half + 512], po[0:nn, 0:512], ALU.add)
                            em.dma("sp", dst, xb[0:nn, :], key=xb, final=True)
                    em.barrier()
        em.barrier()
        em.finish()
        stats = dict(peak_sbuf=em.peak_bytes, peak_at=getattr(em, 'peak_at', ''), ninst=em.ninst, nwaits=em.nwaits, nsem=em.nsem, cnt=dict(em.cnt))
    return nc, stats, in_names


def _prep_inputs(inp, stages):
    f = np.float32
    w_in = np.asarray(inp["w_in"], f)

    def kcl(W):
        C = W.shape[1]
        return np.ascontiguousarray(W.reshape(8, 128, C).transpose(1, 0, 2).reshape(128, 8 * C))
    wfm = np.stack([np.stack([kcl(w_in[l][:, idx]) for (_, _, idx) in FM]) for l in range(2)])
    wtma = np.stack([kcl(w_in[l][:, TMA_COLS]) for l in range(2)])
    wtmm = np.stack([np.stack([kcl(w_in[l][:, TMM_COLS[h]]) for h in range(4)]) for l in range(2)])
    wsm = np.stack([kcl(w_in[l][:, SM_COLS]) for l in range(2)])
    wbr = np.asarray(inp["w_branch"], f).reshape(2, 3, 4, 128, 1024).transpose(0, 3, 1, 2, 4).reshape(2, 128, 12 * 1024)
    wout = np.asarray(inp["w_out"], f).reshape(2, 8, 128, 1024).transpose(0, 2, 1, 3).reshape(2, 128, 8 * 1024)
    gcols = np.stack([np.asarray(inp[k], f) for k in ("m_bi", "m_bf", "g_A_log", "g_dt_bias")], axis=2)
    gcw = np.asarray(inp["g_conv"], f).reshape(2, 4, 12, 128).transpose(0, 2, 3, 1)
    onehot = np.zeros((4, 4, 128), f)
    for h in range(4):
        onehot[h, h, :] = 1.0
    ii = np.arange(64)
    masku = (ii[None, :] >= ii[:, None]).astype(f)
    maskb = np.where(ii[:, None] <= ii[None, :], 0.0, NEGM).astype(f)
    strict = (ii[:, None] < ii[None, :]).astype(f)
    rmask = np.ones((4, TT), f)
    rmask[:, 0:T:64] = 0.0
    rmask[:, T::4] = 0.0
    ttinit = np.zeros((128, 64, 16), f)
    for p in range(128):
        rb = p // 32
        for r_ in range(16):
            ttinit[p, 16 * rb + r_, r_] = 1.0
    shared = {
        "wfm": wfm, "wtma": wtma, "wtmm": wtmm, "wsm": wsm,
        "wbr": np.ascontiguousarray(wbr), "wout": np.ascontiguousarray(wout),
        "normg": np.ascontiguousarray(np.broadcast_to(np.asarray(inp["norm_g"], f)[:, None, :], (2, 128, D))),
        "aqn": np.ascontiguousarray(np.asarray(inp["a_qn"], f)[:, np.arange(128) % 64][:, :, None]),
        "akn": np.ascontiguousarray(np.asarray(inp["a_kn"], f)[:, :, np.arange(128) % 64][:, :, :, None]),
        "mhn": np.ascontiguousarray(np.broadcast_to(np.asarray(inp["m_hn"], f)[:, None, :], (2, 64, 128))),
        "ghn": np.ascontiguousarray(np.broadcast_to(np.asarray(inp["g_hn"], f)[:, None, :], (2, 64, 128))),
        "gcols": np.ascontiguousarray(gcols), "gcw": np.ascontiguousarray(gcw),
        "ident": np.eye(128, dtype=f),
        "bd64": np.kron(np.eye(2, dtype=f), np.ones((64, 64), f)),
        "onehot": onehot.reshape(4, 512), "masku": masku, "maskb": maskb, "strict": strict, "rmask": rmask,
        "ttinit": ttinit.reshape(128, 1024), "ttinit4": np.tile(np.eye(4, dtype=f).reshape(1, 16), (4, 1)),
    }
    if "ATT" in stages:
        rb_ = np.asarray(inp["rel_bias"], f)

        def bucket(dist):
            n = np.maximum(dist, 0)
            nf = np.maximum(n, 16).astype(np.float32)
            large = 16 + (np.log(nf / np.float32(16)) / np.float32(math.log(2048 / 16)) * np.float32(16)).astype(np.int32)
            return np.where(n < 16, n, np.minimum(large, 31))

        def table(dist, valid):
            bk = bucket(dist)
            out = np.empty((8,) + dist.shape, f)
            for hh in range(8):
                out[hh] = np.where(valid, rb_[bk, hh], np.float32(NEGM))
            return out
        kk = np.arange(128)[:, None]
        d1 = np.arange(2432)[None, :] - kk - 384
        shared["msel"] = table(d1, d1 >= 0)
        d2 = np.arange(1408)[None, :] - kk - 384
        shared["mwin"] = table(d2, (d2 >= 0) & (d2 < 512))
        d3 = np.arange(2048)[None, :] - 16 * kk - 31
        shared["mc"] = table(d3, d3 >= 0)
        nn_ = np.arange(128)[:, None]
        jj = np.arange(32)[None, :]
        cover = ((16 * nn_ < 64 * jj + 64) & (64 * jj <= 16 * nn_ + 31)).astype(f)
        cover[127] = 0.0
        shared["cover"] = cover
        tt_ = (np.arange(16)[None, :, None] * 128 + np.arange(128)[:, None, None])
        j3 = np.arange(32)[None, None, :]
        cur = tt_ // 64
        future = 64 * j3 > tt_
        forced = ((j3 == 0) | (j3 == cur) | (j3 == cur - 1)) & ~future
        big = (np.float32(1e30) * (1.0 + j3 / 64.0)).astype(f) * np.ones_like(tt_, dtype=f)
        shared["keep"] = np.ascontiguousarray((~future & ~forced).astype(f).reshape(128, 512))
        shared["addm"] = np.ascontiguousarray(np.where(future, -big, np.where(forced, big, 0.0)).astype(f).reshape(128, 512))
        shared["wex"] = (np.arange(2048)[None, :] // 64 == np.arange(32)[:, None]).astype(f)
        wc = np.zeros((2, 2, 128, 32, 128), f)
        for kv_, nm in enumerate(("a_cmp_wk", "a_cmp_wv")):
            w = np.asarray(inp[nm], f)
            for g in range(2):
                wc[:, kv_, 64 * g:64 * g + 64, :, 64 * g:64 * g + 64] = w[:, g].transpose(0, 2, 1, 3)
        shared["wcmp"] = wc.reshape(2, 2, 128, 32 * 128)
    if "SAMP" in stages:
        if "ATT" not in stages:
            raise ValueError("SAMP needs ATT")
        P0 = 8192
        cq = np.arange(16)
        c_, q_ = cq // 4, cq % 4

        def table_s(dist_q, valid_q):
            bk = bucket(dist_q)
            out = np.empty(dist_q.shape[:-1] + (2, 16), f)
            for g in range(2):
                for x in range(16):
                    hh = 4 * g + c_[x]
                    out[..., g, x] = np.where(valid_q[..., q_[x]], rb_[bk[..., q_[x]], hh], np.float32(NEGM))
            return out
        n_abs = (np.arange(4)[None, :, None] * 128 + np.arange(128)[:, None, None])
        qq = np.arange(4)[None, None, :]
        d = P0 + qq - 16 * n_abs - 31
        shared["mcs"] = table_s(d, d >= 0).reshape(128, 128)
        j129 = np.arange(129)[None, None, :]
        covers = ((16 * n_abs < 64 * j129 + 64) & (64 * j129 <= 16 * n_abs + 31)).astype(f)
        shared["covers"] = np.ascontiguousarray(covers.reshape(128, 4 * 129))
        kpos = (np.arange(64)[None, :, None] * 128 + np.arange(128)[:, None, None])
        d = P0 + qq - kpos
        shared["msels"] = table_s(d, d >= 0).reshape(128, 64 * 32)
        wpos = P0 - 512 + n_abs
        d = P0 + qq - wpos
        shared["mwins"] = table_s(d, (d >= 0) & (d < 512)).reshape(128, 128)
        k16 = np.arange(16)[:, None, None]
        s4 = np.arange(4)[None, :, None]
        d = qq - (k16 % 4) + 0 * s4
        valid = ((k16 // 4) == s4) & (d >= 0)
        shared["msnew"] = table_s(d, valid).reshape(16, 128)
        forced = np.zeros(129, bool)
        forced[[0, 127, 128]] = True
        bigs = (np.float32(1e30) * (1.0 + np.arange(129) / 256.0)).astype(f)
        shared["keeps"] = np.tile((~forced).astype(f)[None, :], (4, 1))
        shared["adds"] = np.tile(np.where(forced, bigs, 0.0).astype(f)[None, :], (4, 1))
        shared["qsel"] = (q_[:, None] == np.arange(4)[None, :]).astype(f)
        shared["maskc"] = (c_[:, None, None] == (2 * np.arange(2)[None, :, None] + np.arange(2)[None, None, :])).astype(f).reshape(16, 4)
        ls = np.zeros((4, 2, 128), f)
        ls[:, 0, 0:64] = 1.0
        ls[:, 1, 64:128] = 1.0
        shared["lsel"] = ls.reshape(4, 256)
        sg_ = np.zeros((24, 4, 3, 128), f)
        for j in range(4):
            for br_ in range(3):
                for m_ in range(128):
                    sg_[3 * (2 * j + m_ // 64) + br_, j, br_, m_] = 1.0
        shared["selg"] = sg_.reshape(24, 12 * 128)
        shared["iota"] = np.arange(128, dtype=f)[:, None]
        shared["wag"] = np.stack([kcl(w_in[l][:, O_AG:O_AG + 24]) for l in range(2)])
        cp_ = np.asarray(inp["cache_cmp_kv"], f).reshape(2, 2560 * 128, 256)
        sp_ = np.asarray(inp["cache_sel_kv"], f).reshape(2, 2560 * 128, 256)
        for i in range(2):
            shared[f"cmp_pool{i}"] = cp_[i]
            shared[f"sel_pool{i}"] = sp_[i]
    maps = []
    xp = np.asarray(inp["x_prompt"], f)
    xs = np.asarray(inp["x_sample"], f)
    cw = np.asarray(inp["cache_win_kv"], f).reshape(2, 32, 512, 256)
    mC = np.asarray(inp["state_mlstm_C"], f)
    mn = np.asarray(inp["state_mlstm_n"], f)
    mm = np.asarray(inp["state_mlstm_m"], f)
    gS = np.asarray(inp["state_gdn_S"], f)
    gc = np.asarray(inp["state_gdn_conv"], f)
    for c in range(NCORES):
        m = dict(shared)
        sl = slice(4 * c, 4 * c + 4)
        m["xp"] = np.ascontiguousarray(xp[c])
        m["xs"] = np.ascontiguousarray(xs[sl].reshape(TS, D))
        m["cwin"] = np.ascontiguousarray(cw[:, sl])
        m["mC0"] = np.ascontiguousarray(mC[:, sl])
        m["mn0"] = np.ascontiguousarray(mn[:, sl])
        m["mm0"] = np.ascontiguousarray(mm[:, sl].transpose(0, 2, 1))
        m["gS0"] = np.ascontiguousarray(gS[:, sl])
        if "SAMP" in stages:
            m["ptab"] = np.ascontiguousarray(np.asarray(inp["page_table"], np.int32)[sl].reshape(1, 256))
        m["gcv0"] = np.ascontiguousarray(gc[:, sl].reshape(2, 4, 3, 12, 128).transpose(0, 1, 3, 4, 2))
        maps.append(m)
    return maps


def _assemble(res):
    R = res

    def cat(name):
        return [np.asarray(r[name]) for r in R]
    y_p = np.stack(cat("y_p"))
    y_s = np.concatenate([a.reshape(4, 4, D) for a in cat("y_s")])

    def kvp(name, tlen):
        return np.stack(cat(name), axis=1).reshape(2, 8, tlen, 2, 2, 64)

    def kvs(name):
        return np.concatenate([a.reshape(2, 4, 4, 256) for a in cat(name)], axis=1).reshape(2, 32, 4, 2, 2, 64)
    gc_p = np.stack([a.transpose(0, 3, 1, 2).reshape(2, 3, 1536) for a in cat("gc_p")], axis=1)
    gc_s = np.concatenate([a.transpose(0, 1, 4, 2, 3).reshape(2, 4, 3, 1536) for a in cat("gc_s")], axis=1)
    outs = [
        y_p, y_s,
        kvp("cmp_p", T), kvs("cmp_s"), kvp("sel_p", T), kvs("sel_s"),
        kvp("win_p", 512), np.concatenate(cat("win_s"), axis=1).reshape(2, 32, 512, 2, 2, 64),
        np.stack(cat("mC_p"), axis=1), np.concatenate(cat("mC_s"), axis=1),
        np.stack(cat("mn_p"), axis=1), np.concatenate(cat("mn_s"), axis=1),
        np.stack(cat("mm_p"), axis=1), np.concatenate([a.transpose(0, 2, 1) for a in cat("mm_s")], axis=1),
        np.stack(cat("gS_p"), axis=1), np.concatenate(cat("gS_s"), axis=1),
        gc_p, gc_s,
    ]
    return tuple(np.ascontiguousarray(o, dtype=np.float32) for o in outs)


STAGES = ("B1", "M", "G", "ATT", "SAMP", "OUT")
LAST = {}


def kernel(**inputs):
    nc, stats, in_names = build_program(STAGES)
    maps = _prep_inputs(inputs, STAGES)
    names = set(in_names)
    maps = [{k: v for k, v in m.items() if k in names} for m in maps]
    res = run_bass_kernel_spmd(nc, maps, core_ids=list(range(NCORES)))
    LAST["res"] = res.results
    return _assemble(res.results)
```

```python
import math
import numpy as np
from contextlib import ExitStack
import concourse.bass as bass
import concourse.mybir as mybir
from concourse.bass_utils import run_bass_kernel_spmd

F32 = mybir.dt.float32
BF16 = mybir.dt.bfloat16
I32 = mybir.dt.int32
ALU = mybir.AluOpType
AF = mybir.ActivationFunctionType

NCORES = 8
T = 2048
TS = 16
TT = T + TS
D = 1024
EPS = 1e-6

O_AQ = 0
O_AKV = O_AQ + 512
O_AG = O_AKV + 768
O_AZ = O_AG + 24
O_MQKV = O_AZ + 512
O_MIF = O_MQKV + 1536
O_MO = O_MIF + 8
O_MZ = O_MO + 512
O_GQKV = O_MZ + 512
O_GAB = O_GQKV + 1536
O_GZ = O_GAB + 8
O_MERGE = O_GZ + 512
N_IN = O_MERGE + 3072


def _fm_chunks():
    ch = []
    r = np.arange
    def headpair(base, c):
        return np.concatenate([base + 64 * c + r(64), base + 256 + 64 * c + r(64)])
    for c in range(4):
        ch.append(("aq", c, headpair(O_AQ, c)))
    for b in range(3):
        ch.append(("ak", b, O_AKV + b * 256 + r(128)))
    ch.append(("av", 0, O_AKV + 128 + r(128)))
    for j in range(4):
        ch.append(("az", j, O_AZ + j * 128 + r(128)))
    for h in range(4):
        ch.append(("mq", h, O_MQKV + h * 128 + r(128)))
    for h in range(4):
        ch.append(("mk", h, O_MQKV + 512 + h * 128 + r(128)))
    for h in range(4):
        ch.append(("mo", h, O_MO + h * 128 + r(128)))
    for h in range(4):
        ch.append(("mz", h, O_MZ + h * 128 + r(128)))
    for j in range(12):
        ch.append(("gqkv", j, O_GQKV + j * 128 + r(128)))
    for h in range(4):
        ch.append(("gz", h, O_GZ + h * 128 + r(128)))
    for j in range(24):
        ch.append(("merge", j, O_MERGE + j * 128 + r(128)))
    return ch


FM = _fm_chunks()
NFM = len(FM)
FMI = {(n, i): k for k, (n, i, _) in enumerate(FM)}
TMA_COLS = np.concatenate([O_AKV + b * 256 + 128 + np.arange(128) for b in range(3)] + [O_AG + np.arange(24)])
NTMA = len(TMA_COLS)
TMM_COLS = [np.concatenate([O_MQKV + 512 + h * 128 + np.arange(128), O_MQKV + 1024 + h * 128 + np.arange(128)]) for h in range(4)]
SM_COLS = np.concatenate([O_MIF + np.arange(4), O_MIF + 4 + np.arange(4), O_GAB + np.arange(4), O_GAB + 4 + np.arange(4)])
NT64 = 36


class View:
    __slots__ = ("buf", "ap")

    def __init__(self, buf, ap):
        self.buf = buf
        self.ap = ap

    def re(self, pat, **kw):
        return View(self.buf, self.ap.rearrange(pat, **kw))

    def __getitem__(self, idx):
        return View(self.buf, self.ap[idx])


class Buf:
    __slots__ = ("t", "name", "w", "r", "dsem", "dcnt", "excl")

    def __init__(self, t, name, excl=False):
        self.t = t
        self.name = name
        self.w = None
        self.r = []
        self.dsem = None
        self.dcnt = 0
        self.excl = excl

    def __getitem__(self, idx):
        return View(self, self.t[idx])


def _ap(x):
    return x.ap if isinstance(x, View) else x


class Scope:
    def __init__(self, em):
        self.em = em
        self.stack = ExitStack()
        self.bufs = []
        self.nbytes = 0

    def __enter__(self):
        self.stack.__enter__()
        return self

    def __exit__(self, *a):
        if a[0] is None:
            self.em.barrier()
            for b in self.bufs:
                if b.dsem is not None:
                    self.em.sem_pool.append((b.dsem, b.dcnt))
                    b.dsem = None
            ids = {id(b) for b in self.bufs}
            self.em.allbufs = [b for b in self.em.allbufs if id(b) not in ids]
        self.em.cur_bytes -= self.nbytes
        return self.stack.__exit__(*a)


class Em:
    ROLL = 30000

    def __init__(self, nc, stack):
        self.nc = nc
        self.stack = stack
        self.eng = {"pe": nc.tensor, "act": nc.scalar, "dve": nc.vector, "pool": nc.gpsimd, "sp": nc.sync}
        self.sem = {}
        self.cnt = {}
        self.nsem = 0
        self.seen = {k: {} for k in self.eng}
        for k in self.eng:
            self._newsem(k)
        self.store_tokens = []
        self.nbuf = 0
        self.nwaits = 0
        self.ninst = 0
        self.allbufs = []
        self.cur_bytes = 0
        self.peak_bytes = 0
        self.sem_pool = []
        self.nsem_d = 0
        self.selfsync = {"pe": False, "act": True, "dve": True, "pool": True, "sp": False}

    def _newsem(self, k):
        self.nsem += 1
        self.sem[k] = self.stack.enter_context(self.nc.semaphore(f"s_{k}_{self.nsem}"))
        self.cnt[k] = 0

    def sb(self, shape, dt, name=None):
        self.nbuf += 1
        name = name or f"sb{self.nbuf}"
        t = self.stack.enter_context(self.nc.sbuf_tensor("t_" + name, list(shape), dt))
        self.cur_bytes += int(np.prod(shape[1:])) * (2 if dt == BF16 else 4)
        self.peak_bytes = max(self.peak_bytes, self.cur_bytes)
        b = Buf(t, name)
        self.allbufs.append(b)
        return b

    def ps(self, shape, dt, name=None):
        self.nbuf += 1
        name = name or f"ps{self.nbuf}"
        t = self.stack.enter_context(self.nc.psum_tensor("t_" + name, list(shape), dt))
        return Buf(t, name, excl=True)

    def _waits(self, e, reads, writes):
        need = {}

        def add(tok):
            if tok is None:
                return
            s, v = tok
            k = id(s)
            if k not in need or need[k][1] < v:
                need[k] = (s, v)
        for b in reads:
            add(b.w)
            if b.excl:
                for t in b.r:
                    add(t)
        for b in writes:
            add(b.w)
            for t in b.r:
                add(t)
        eng = self.eng[e]
        seen = self.seen[e]
        own = self.sem[e]
        for k, (s, v) in need.items():
            if s is own and not self.selfsync[e]:
                continue
            if seen.get(k, 0) >= v:
                continue
            seen[k] = v
            eng.wait_ge(s, v)
            self.nwaits += 1

    def _commit(self, tok, reads, writes):
        for b in reads:
            if len(b.r) > 24:
                last = {}
                for (s, v) in b.r:
                    if id(s) not in last or last[id(s)][1] < v:
                        last[id(s)] = (s, v)
                b.r = list(last.values())
            b.r.append(tok)
        for b in writes:
            b.w = tok
            b.r = []

    def _signal(self, e, ins):
        if self.cnt[e] >= self.ROLL:
            self._newsem(e)
        self.cnt[e] += 1
        ins.then_inc(self.sem[e], 1)
        return (self.sem[e], self.cnt[e])

    def op(self, e, fn, ins=(), outs=()):
        reads = [v.buf for v in ins if isinstance(v, View)]
        writes = [v.buf for v in outs if isinstance(v, View)]
        self._waits(e, reads, writes)
        i = fn(self.eng[e])
        tok = self._signal(e, i)
        self._commit(tok, reads, writes)
        self.ninst += 1

    def mm(self, out, pairs, extra_reads=(), transpose=False):
        reads = []
        for a, b in pairs:
            reads += [a.buf, b.buf]
        writes = [out.buf]
        self._waits("pe", reads, writes)
        n = len(pairs)
        i = None
        for j, (a, b) in enumerate(pairs):
            if transpose:
                i = self.nc.tensor.transpose(out.ap, a.ap, b.ap)
            else:
                i = self.nc.tensor.matmul(out.ap, lhsT=a.ap, rhs=b.ap, start=(j == 0), stop=(j == n - 1))
            self.ninst += 1
        tok = self._signal("pe", i)
        self._commit(tok, reads, writes)

    def mm_multi(self, groups):
        reads, writes = [], []
        for out, pairs in groups:
            writes.append(out.buf)
            for a, b in pairs:
                reads += [a.buf, b.buf]
        self._waits("pe", reads, writes)
        i = None
        for out, pairs in groups:
            n = len(pairs)
            for j, (a, b) in enumerate(pairs):
                i = self.nc.tensor.matmul(out.ap, lhsT=a.ap, rhs=b.ap, start=(j == 0), stop=(j == n - 1))
                self.ninst += 1
        tok = self._signal("pe", i)
        self._commit(tok, reads, writes)

    def transposes(self, items):
        reads, writes = [], []
        for o, a, idn in items:
            writes.append(o.buf)
            reads += [a.buf, idn.buf]
        self._waits("pe", reads, writes)
        i = None
        for o, a, idn in items:
            i = self.nc.tensor.transpose(o.ap, a.ap, idn.ap)
            self.ninst += 1
        tok = self._signal("pe", i)
        self._commit(tok, reads, writes)

    def act(self, out, in_, func, bias=None, scale=None, accum_out=None, e="act"):
        kw = {}
        ins = [in_]
        outs = [out]
        if bias is not None:
            kw["bias"] = _ap(bias)
            if isinstance(bias, View):
                ins.append(bias)
        if scale is not None:
            kw["scale"] = _ap(scale)
            if isinstance(scale, View):
                ins.append(scale)
        if accum_out is not None:
            kw["accum_out"] = _ap(accum_out)
            outs.append(accum_out)
        self.op("act", lambda g: g.activation(out=_ap(out), in_=_ap(in_), func=func, **kw), ins, outs)

    def tt(self, out, in0, in1, op, e="dve"):
        self.op(e, lambda g: g.tensor_tensor(out=_ap(out), in0=_ap(in0), in1=_ap(in1), op=op), [in0, in1], [out])

    def ts(self, out, in0, s1, s2=None, op0=ALU.mult, op1=None, e="dve", accum_out=None):
        kw = {}
        outs = [out]
        if op1 is not None:
            kw["op1"] = op1
        if accum_out is not None:
            kw["accum_out"] = _ap(accum_out)
            outs.append(accum_out)
        self.op(e, lambda g: g.tensor_scalar(out=_ap(out), in0=_ap(in0), scalar1=_ap(s1), scalar2=_ap(s2), op0=op0, **kw),
                [in0, s1, s2], outs)

    def stt(self, out, in0, scalar, in1, op0, op1):
        self.op("dve", lambda g: g.scalar_tensor_tensor(out=_ap(out), in0=_ap(in0), scalar=_ap(scalar), in1=_ap(in1),
                                                        op0=op0, op1=op1), [in0, scalar, in1], [out])

    def copy(self, out, in_, e="dve"):
        if e == "act":
            self.op("act", lambda g: g.copy(out=_ap(out), in_=_ap(in_)), [in_], [out])
        else:
            self.op(e, lambda g: g.tensor_copy(out=_ap(out), in_=_ap(in_)), [in_], [out])

    def memset(self, out, val, e="pool"):
        self.op(e, lambda g: g.memset(_ap(out), val), [], [out])

    def recip(self, out, in_):
        self.op("dve", lambda g: g.reciprocal(out=_ap(out), in_=_ap(in_)), [in_], [out])

    def dma(self, q, out, in_, key, final=False, **kw):
        reads = [in_.buf] if isinstance(in_, View) else []
        writes = [out.buf] if isinstance(out, View) else []
        self._waits(q, reads, writes)
        if key.dsem is None or key.dcnt >= 28000:
            key.dsem, key.dcnt = self._getsem()
        i = self.eng[q].dma_start(out=_ap(out), in_=_ap(in_), **kw)
        key.dcnt += 16
        i.then_inc(key.dsem, 16)
        tok = (key.dsem, key.dcnt)
        self._commit(tok, reads, writes)
        if final:
            self.store_tokens.append(tok)
        self.ninst += 1

    def dram(self, ap, name):
        self.nbuf += 1
        b = Buf(ap, name)
        self.allbufs.append(b)
        return b

    def sbs(self, scope, shape, dt, name):
        self.nbuf += 1
        t = scope.stack.enter_context(self.nc.sbuf_tensor(f"t{self.nbuf}_" + name, list(shape), dt))
        b = Buf(t, f"{name}{self.nbuf}")
        nb_ = int(np.prod(shape[1:])) * (2 if dt == BF16 else 4)
        self.cur_bytes += nb_
        scope.nbytes += nb_
        if self.cur_bytes > self.peak_bytes:
            self.peak_bytes = self.cur_bytes
            self.peak_at = name
        self.allbufs.append(b)
        scope.bufs.append(b)
        return b

    def idma(self, out, rows_ap, idxv, key):
        self._waits("pool", [idxv.buf], [out.buf])
        if key.dsem is None or key.dcnt >= 28000:
            key.dsem, key.dcnt = self._getsem()
        i = self.nc.gpsimd.indirect_dma_start(out=out.ap, out_offset=None, in_=rows_ap,
                                              in_offset=bass.IndirectOffsetOnAxis(ap=idxv.ap, axis=0))
        key.dcnt += 16
        i.then_inc(key.dsem, 16)
        tok = (key.dsem, key.dcnt)
        self._commit(tok, [idxv.buf], [out.buf])
        self.ninst += 1

    def barrier(self):
        toks = []
        for k in self.eng:
            if self.cnt[k] > 0:
                toks.append((self.sem[k], self.cnt[k]))
        for b in self.allbufs:
            if b.dsem is not None and b.dcnt > 0:
                toks.append((b.dsem, b.dcnt))
        for e in self.eng:
            seen = self.seen[e]
            for s_, v in toks:
                if s_ is self.sem[e]:
                    continue
                if seen.get(id(s_), 0) >= v:
                    continue
                seen[id(s_)] = v
                self.eng[e].wait_ge(s_, v)

    def reduce(self, out, in_, op=ALU.add, e="dve"):
        self.op(e, lambda g: g.tensor_reduce(out=_ap(out), in_=_ap(in_), axis=mybir.AxisListType.X, op=op), [in_], [out])

    def scan(self, out, d0, d1, initial, op0, op1):
        ins = [d0, d1] + ([initial] if isinstance(initial, View) else [])
        self.op("dve", lambda g: g.tensor_tensor_scan(out=_ap(out), data0=_ap(d0), data1=_ap(d1), initial=_ap(initial),
                                                      op0=op0, op1=op1), ins, [out])

    def top8(self, out, in_):
        self.op("dve", lambda g: g.max(out=_ap(out), in_=_ap(in_)), [in_], [out])

    def mrep(self, out, m8, vals, imm):
        self.op("dve", lambda g: g.match_replace(out=_ap(out), in_to_replace=_ap(m8), in_values=_ap(vals), imm_value=imm),
                [m8, vals], [out])

    def mm1(self, out, a, b, start, stop):
        reads = [a.buf, b.buf]
        writes = [out.buf]
        self._waits("pe", reads, writes)
        i = self.nc.tensor.matmul(out.ap, lhsT=a.ap, rhs=b.ap, start=start, stop=stop)
        self.ninst += 1
        tok = self._signal("pe", i)
        self._commit(tok, reads, writes)

    def _getsem(self):
        while self.sem_pool:
            s_, c = self.sem_pool.pop()
            if c < 20000:
                return s_, c
        self.nsem_d += 1
        return self.stack.enter_context(self.nc.semaphore(f"d{self.nsem_d}")), 0

    def scope(self):
        return Scope(self)

    def finish(self, e="sp"):
        eng = self.eng[e]
        last = {}
        for s, v in self.store_tokens:
            if id(s) not in last or last[id(s)][1] < v:
                last[id(s)] = (s, v)
        for s, v in last.values():
            eng.wait_ge(s, v)
        for k in self.eng:
            if k != e and self.cnt[k] > 0:
                eng.wait_ge(self.sem[k], self.cnt[k])


BLOCKS = [(i * 512, 512) for i in range(4)] + [(T, TS)]
NEGM = -30000.0


def t64(i):
    return (64 * i, 64) if i < 32 else (T + 4 * (i - 32), 4)


def bc(view, shape):
    return View(view.buf, view.ap.unsqueeze(len(view.ap.shape)).broadcast_to(list(shape)))


def build_program(stages=("B1", "M", "G", "ATT", "SAMP", "OUT"), layers=2):
    nc = bass.Bass("TRN2", target_bir_lowering=False)
    S = set(stages)

    in_names = []

    def din(name, shape, dt=F32):
        in_names.append(name)
        return nc.dram_tensor(name, list(shape), dt, kind="ExternalInput").ap()

    def dout(name, shape, dt=F32):
        return nc.dram_tensor(name, list(shape), dt, kind="ExternalOutput").ap()

    xp = din("xp", [T, D])
    xs = din("xs", [TS, D])
    wfm = din("wfm", [2, NFM, 128, 8 * 128])
    wtma = din("wtma", [2, 128, 8 * NTMA])
    wtmm = din("wtmm", [2, 4, 128, 8 * 256])
    wsm = din("wsm", [2, 128, 8 * 16])
    wbr_in = din("wbr", [2, 128, 12 * 1024])
    wout_in = din("wout", [2, 128, 8 * 1024])
    normg = din("normg", [2, 128, D])
    aqn = din("aqn", [2, 128, 1])
    akn = din("akn", [2, 3, 128, 1])
    mhn_in = din("mhn", [2, 64, 128])
    ghn_in = din("ghn", [2, 64, 128])
    gcol_in = din("gcols", [2, 4, 4])
    gcw_in = din("gcw", [2, 12, 128, 4])
    ident_in = din("ident", [128, 128])
    bd64_in = din("bd64", [128, 128])
    onehot_in = din("onehot", [4, 4 * 128])
    masku_in = din("masku", [64, 64])
    maskb_in = din("maskb", [64, 64])
    strict_in = din("strict", [64, 64])
    rmask_in = din("rmask", [4, TT])
    ttinit_in = din("ttinit", [128, 64 * 16])
    ttinit4_in = din("ttinit4", [4, 16])
    cwin = din("cwin", [2, 4, 512, 256])
    if "SAMP" in S:
        cmp_pool = [din(f"cmp_pool{i}", [2560 * 128, 256]) for i in range(2)]
        sel_pool = [din(f"sel_pool{i}", [2560 * 128, 256]) for i in range(2)]
        ptab_in = din("ptab", [1, 256], I32)
        iota_in = din("iota", [128, 1])
        wag_in = din("wag", [2, 128, 8 * 24])
        selg_in = din("selg", [24, 12 * 128])
        mcs_in = din("mcs", [128, 4 * 32])
        covers_in = din("covers", [128, 4 * 129])
        msels_in = din("msels", [128, 64 * 32])
        mwins_in = din("mwins", [128, 4 * 32])
        msnew_in = din("msnew", [16, 4 * 32])
        keeps_in = din("keeps", [4, 129])
        adds_in = din("adds", [4, 129])
        qsel_in = din("qsel", [16, 4])
        maskc_in = din("maskc", [16, 4])
        lsel_in = din("lsel", [4, 256])
        if "ATT" not in S:
            wcmp_in = din("wcmp", [2, 2, 128, 32 * 128])
    if "ATT" in S:
        msel_in = din("msel", [8, 128, 2432])
        mwin_in = din("mwin", [8, 128, 1408])
        mc_in = din("mc", [8, 128, 2048])
        cover_in = din("cover", [128, 32])
        keep_in = din("keep", [128, 16 * 32])
        addm_in = din("addm", [128, 16 * 32])
        wex_in = din("wex", [32, 2048])
        wcmp_in = din("wcmp", [2, 2, 128, 32 * 128])
    mC0 = din("mC0", [2, 4, 4, 128, 128])
    mn0 = din("mn0", [2, 4, 4, 128])
    mm0 = din("mm0", [2, 4, 4])
    gS0 = din("gS0", [2, 4, 4, 128, 128])
    gcv0 = din("gcv0", [2, 4, 12, 128, 3])

    o_y_p = dout("y_p", [T, D])
    o_y_s = dout("y_s", [TS, D])
    o_cmp_p = dout("cmp_p", [2, T, 256])
    o_cmp_s = dout("cmp_s", [2, TS, 256])
    o_sel_p = dout("sel_p", [2, T, 256])
    o_sel_s = dout("sel_s", [2, TS, 256])
    o_win_p = dout("win_p", [2, 512, 256])
    o_win_s = dout("win_s", [2, 4, 512, 256])
    o_mC_p = dout("mC_p", [2, 4, 128, 128])
    o_mC_s = dout("mC_s", [2, 4, 4, 128, 128])
    o_mn_p = dout("mn_p", [2, 4, 128])
    o_mn_s = dout("mn_s", [2, 4, 4, 128])
    o_mm_p = dout("mm_p", [2, 4])
    o_mm_s = dout("mm_s", [2, 4, 4])
    o_gS_p = dout("gS_p", [2, 4, 128, 128])
    o_gS_s = dout("gS_s", [2, 4, 4, 128, 128])
    o_gc_p = dout("gc_p", [2, 12, 128, 3])
    o_gc_s = dout("gc_s", [2, 4, 12, 128, 3])
    br_d = dout("br_scr", [12, 128, TT], BF16)
    sA_d = dout("sA_scr", [32, 64 * 64])
    sT_d = dout("sT_scr", [32, 64, 64])
    sA4_d = dout("sA4_scr", [4, 16])
    sT4_d = dout("sT4_scr", [4, 16])
    kv_out_p = [o_cmp_p, o_sel_p]
    kv_out_s = [o_cmp_s, o_sel_s]

    with ExitStack() as st:
        em = Em(nc, st)
        YP = em.dram(o_y_p, "YP")
        YS = em.dram(o_y_s, "YS")
        BR = em.dram(br_d, "BR")
        SA = em.dram(sA_d, "SA")
        STT = em.dram(sT_d, "STT")
        SA4 = em.dram(sA4_d, "SA4")
        ST4 = em.dram(sT4_d, "ST4")
        OUTB = em.dram(None, "OUTB")
        ident = em.sb([128, 128], F32, "ident")
        identb = em.sb([128, 128], BF16, "identb")
        bd64 = em.sb([128, 128], F32, "bd64")
        onehot = em.sb([4, 4, 128], F32, "onehot")
        masku = em.sb([64, 64], F32, "masku")
        maskb = em.sb([64, 64], F32, "maskb")
        strict = em.sb([64, 64], F32, "strict")
        ones128 = em.sb([128, 128], F32, "ones128")
        em.dma("sp", ident[:], ident_in[:, :], key=ident)
        em.dma("sp", bd64[:], bd64_in[:, :], key=bd64)
        em.dma("sp", onehot[:], onehot_in.rearrange("p (h c) -> p h c", h=4), key=onehot)
        em.dma("sp", masku[:], masku_in[:, :], key=masku)
        em.dma("sp", maskb[:], maskb_in[:, :], key=maskb)
        em.dma("sp", strict[:], strict_in[:, :], key=strict)
        em.copy(identb[:], ident[:])
        em.memset(ones128[:], 1.0)
        PS = [em.ps([128, 512], F32, f"bank{i}") for i in range(8)]
        hT = em.sb([128, 8, TT], BF16, "hT")
        gcol = em.sb([128, 2, 4], F32, "gcol")
        for l in range(2):
            em.dma("sp", gcol[:, l, 0:1], aqn[l], key=gcol)
            for b in range(3):
                em.dma("sp", gcol[:, l, 1 + b:2 + b], akn[l, b], key=gcol)
        for l in range(2):
            em.ts(gcol[:, l, 0:1], gcol[:, l, 0:1], 0.125, None, op0=ALU.mult)
        wb = [em.sb([128, 8, 128], BF16, f"wb{i}") for i in range(4)]
        wi = [0]

        def load_w(l, name, idx):
            wbuf = wb[wi[0] % 4]
            wi[0] += 1
            em.dma("pool", wbuf[:, :, :], wfm[l, FMI[(name, idx)]].rearrange("p (a b) -> p a b", a=8), key=wbuf)
            return wbuf

        def fm_proj(wbuf, bi, t0, n):
            pb = PS[bi % 2]
            em.mm(pb[:, 0:n], [(wbuf[:, kc, :], hT[:, kc, t0:t0 + n]) for kc in range(8)])
            return pb

        def store(q, dst, src, key=None):
            em.dma(q, dst, src, key=key or src.buf, final=True)

        for l in range(layers):
            with em.scope() as ph:
                xt = [em.sbs(ph, [128, D], F32, f"xt{i}") for i in range(2)]
                hn = [em.sbs(ph, [128, D], F32, f"hn{i}") for i in range(2)]
                sqj = em.sbs(ph, [128, D], F32, "sqj")
                ssq = [em.sbs(ph, [128, 1], F32, f"ssq{i}") for i in range(2)]
                rstd = [em.sbs(ph, [128, 1], F32, f"rstd{i}") for i in range(2)]
                gbc = em.sbs(ph, [128, D], F32, "gbc")
                em.dma("sp", gbc[:], normg[l], key=gbc)
                for ti in range(17):
                    n = 128 if ti < 16 else TS
                    tok0 = ti * 128 if ti < 16 else T
                    xb = xt[ti % 2]
                    hb = hn[ti % 2]
                    if l == 0:
                        src = xp[ti * 128:(ti + 1) * 128, :] if ti < 16 else xs[:, :]
                    else:
                        src = YP[ti * 128:(ti + 1) * 128, :] if ti < 16 else YS[:, :]
                    em.dma("sp", xb[0:n, :], src, key=xb)
                    s_ = ssq[ti % 2]
                    r_ = rstd[ti % 2]
                    em.act(sqj[0:n, :], xb[0:n, :], AF.Square, accum_out=s_[0:n, :])
                    em.act(r_[0:n, :], s_[0:n, :], AF.Sqrt, scale=1.0 / D, bias=EPS)
                    em.recip(r_[0:n, :], r_[0:n, :])
                    em.stt(hb[0:n, :], xb[0:n, :], r_[0:n, :], gbc[0:n, :], ALU.mult, ALU.mult)
                    for half in range(2):
                        pb = PS[6 + half]
                        items = []
                        for j in range(4):
                            kc = half * 4 + j
                            items.append((pb[:, j * 128:j * 128 + n], hb[0:n, kc * 128:(kc + 1) * 128], ident[0:n, 0:n]))
                        em.transposes(items)
                        em.copy(hT[:, half * 4:half * 4 + 4, tok0:tok0 + n],
                                pb[:, :].re("p (j t) -> p j t", j=4)[:, :, 0:n], e="act" if half else "dve")
                em.barrier()

            if "B1" in S or "ATT" in S:
              with em.scope() as pa:
                qT = em.sbs(pa, [128, 4, TT], BF16, "qT")
                kT = em.sbs(pa, [128, 3, TT], BF16, "kT")
                vcT = em.sbs(pa, [128, TT], BF16, "vcT")
                vaug = em.sbs(pa, [128, 17, 6, 65], BF16, "vaug")
                gates = em.sbs(pa, [128, 17, 24], F32, "gates")
                with em.scope() as ph:
                    wtmb = em.sbs(ph, [128, 8, NTMA], BF16, "wtmab")
                    sq = [em.sbs(ph, [128, 512], F32, f"sq{i}") for i in range(2)]
                    rs = [em.sbs(ph, [128, 512], F32, f"rs{i}") for i in range(2)]
                    kf = [em.sbs(ph, [128, 512], F32, f"kf{i}") for i in range(2)]
                    stg = em.sbs(ph, [128, 4, 3, 256], F32, "stg")
                    em.dma("pool", wtmb[:, :, :], wtma[l].rearrange("p (a b) -> p a b", a=8), key=wtmb)
                    em.memset(vaug[:, :, :, 64:65], 1.0, e="dve")

                    def headnorm(pb, n, bi, gain, out_views):
                        s_ = sq[bi % 2]
                        r_ = rs[bi % 2]
                        em.act(s_[:, 0:n], pb[:, 0:n], AF.Square)
                        p2 = PS[2 + bi % 2]
                        em.mm(p2[:, 0:n], [(bd64[:, :], s_[:, 0:n])])
                        em.act(r_[:, 0:n], p2[:, 0:n], AF.Sqrt, scale=1.0 / 64.0, bias=EPS)
                        em.recip(r_[:, 0:n], r_[:, 0:n])
                        for ov in out_views:
                            em.stt(ov, pb[:, 0:n], gain, r_[:, 0:n], ALU.mult, ALU.mult)

                    wk = [load_w(l, "ak", b_) for b_ in range(3)]
                    for bi, (t0, n) in enumerate(BLOCKS):
                        sg = stg
                        nt = (n + 127) // 128
                        for j in range(nt):
                            nn = min(128, n - j * 128)
                            ti = (t0 // 128 + j) if bi < 4 else 16
                            pb = PS[4 + j % 2]
                            em.mm(pb[0:nn, 0:NTMA], [(hT[:, kc, t0 + j * 128:t0 + j * 128 + nn], wtmb[:, kc, :]) for kc in range(8)])
                            em.copy(vaug[0:nn, ti, :, 0:64], pb[0:nn, 0:384].re("p (b c) -> p b c", c=64), e="act")
                            em.copy(sg[0:nn, j, :, 128:256], pb[0:nn, 0:384].re("p (b c) -> p b c", b=3), e="dve")
                            em.act(gates[0:nn, ti, :], pb[0:nn, 384:408], AF.Sigmoid)
                        for b_ in range(3):
                            pb = PS[b_ % 2]
                            em.mm(pb[:, 0:n], [(wk[b_][:, kc, :], hT[:, kc, t0:t0 + n]) for kc in range(8)])
                            k_ = kf[b_ % 2]
                            headnorm(pb, n, b_, gcol[:, l, 1 + b_:2 + b_], [k_[:, 0:n]])
                            em.copy(kT[:, b_, t0:t0 + n], k_[:, 0:n], e="pool")
                            for j in range(nt):
                                nn = min(128, n - j * 128)
                                p3 = PS[6 + j % 2]
                                em.transposes([(p3[0:nn, 0:128], k_[:, j * 128:j * 128 + nn], ident[:, :])])
                                em.copy(sg[0:nn, j, b_, 0:128], p3[0:nn, 0:128], e="act")
                        for b_ in range(3):
                            if b_ < 2:
                                if bi < 4:
                                    store("sp", kv_out_p[b_][l, t0:t0 + 512, :].rearrange("(j p) c -> p j c", p=128), sg[:, :, b_, :])
                                else:
                                    store("sp", kv_out_s[b_][l, :, :], sg[0:TS, 0, b_, :])
                            else:
                                if bi == 3:
                                    store("sp", o_win_p[l, :, :].rearrange("(j p) c -> p j c", p=128), sg[:, :, b_, :])
                                elif bi == 4:
                                    for s_ in range(4):
                                        store("sp", o_win_s[l, s_, 508:512, :], sg[4 * s_:4 * s_ + 4, 0, b_, :])
                    for s_ in range(4):
                        em.dma("pool", o_win_s[l, s_, 0:508, :], cwin[l, s_, 4:512, :], key=OUTB, final=True)
                    for c in range(4):
                        wbuf = load_w(l, "aq", c)
                        for bi, (t0, n) in enumerate(BLOCKS):
                            pb = PS[bi % 2]
                            em.mm(pb[:, 0:n], [(wbuf[:, kc, :], hT[:, kc, t0:t0 + n]) for kc in range(8)])
                            headnorm(pb, n, bi, gcol[:, l, 0:1], [qT[:, c, t0:t0 + n]])
                    wv_ = load_w(l, "av", 0)
                    for bi, (t0, n) in enumerate(BLOCKS):
                        pb = fm_proj(wv_, bi, t0, n)
                        em.copy(vcT[:, t0:t0 + n], pb[:, 0:n], e="act")
                    em.barrier()
                if "ATT" in S:
                  with em.scope() as ph:
                    kcmpT = em.sbs(ph, [128, 128], BF16, "kcmpT")
                    rhsc = em.sbs(ph, [128, 2, 97], BF16, "rhsc")
                    coverf = em.sbs(ph, [128, 32], F32, "coverf")
                    keep = em.sbs(ph, [128, 16, 32], F32, "keep")
                    addm = em.sbs(ph, [128, 16, 32], F32, "addm")
                    wex = em.sbs(ph, [32, 2048], BF16, "wex")
                    phc = em.scope()
                    phc.__enter__()
                    wck = em.sbs(phc, [128, 32, 128], BF16, "wck")
                    wcv = em.sbs(phc, [128, 32, 128], BF16, "wcv")
                    em.dma("pool", wck[:, :, :], wcmp_in[l, 0].rearrange("p (a b) -> p a b", a=32), key=wck)
                    em.dma("pool", wcv[:, :, :], wcmp_in[l, 1].rearrange("p (a b) -> p a b", a=32), key=wcv)
                    em.dma("sp", coverf[:], cover_in[:, :], key=coverf)
                    em.dma("sp", keep[:], keep_in.rearrange("p (a b) -> p a b", a=16), key=keep)
                    em.dma("sp", addm[:], addm_in.rearrange("p (a b) -> p a b", a=16), key=addm)
                    em.dma("pool", wex[:, :], wex_in[:, :], key=wex)
                    kv16 = kT[:, 0, 0:T].re("p (n s) -> p n s", s=16)
                    vv16 = vcT[:, 0:T].re("p (n s) -> p n s", s=16)
                    em.mm(PS[0][:, 0:127], [(wck[:, ms, :], kv16[:, ms // 16:ms // 16 + 127, ms % 16]) for ms in range(32)])
                    em.copy(kcmpT[:, 0:127], PS[0][:, 0:127], e="act")
                    em.mm(PS[1][0:127, 0:128], [(vv16[:, ms // 16:ms // 16 + 127, ms % 16], wcv[:, ms, :]) for ms in range(32)])
                    em.copy(rhsc[0:127, :, 0:64], PS[1][0:127, 0:128].re("p (g e) -> p g e", g=2), e="act")
                    em.memset(rhsc[:, :, 64:65], 1.0, e="dve")
                    for g in range(2):
                        em.copy(rhsc[:, g, 65:97], coverf[:, :])
                    phc.__exit__(None, None, None)
                    oa = em.sbs(ph, [128, 4, 512], F32, "oa")
                    imp = em.sbs(ph, [128, 4, 2, 32], F32, "imp")
                    penT = em.sbs(ph, [32, 2, 512], BF16, "penT")
                    msel2 = [em.sbs(ph, [128, 2432], F32, f"msel{i}") for i in range(1)] * 2
                    mwin2 = [em.sbs(ph, [128, 1408], F32, f"mwin{i}") for i in range(1)] * 2
                    mcb2 = [em.sbs(ph, [128, 512], F32, f"mcb{i}") for i in range(2)]
                    sbt = [em.sbs(ph, [128, 512], F32, f"sbt{i}") for i in range(2)]
                    Pt = [em.sbs(ph, [128, 512], BF16, f"Pt{i}") for i in range(2)]
                    rden = em.sbs(ph, [128, 4], F32, "rden")
                    coef = em.sbs(ph, [128, 4], F32, "coef")
                    tmpo = em.sbs(ph, [128, 4, 64], F32, "tmpo")
                    tmpi = em.sbs(ph, [128, 4, 32], F32, "tmpi")
                    sc = em.sbs(ph, [128, 32], F32, "sc")
                    sc2 = em.sbs(ph, [128, 32], F32, "sc2")
                    m8 = em.sbs(ph, [128, 8], F32, "m8")
                    azs = em.sbs(ph, [128, 4, 512], F32, "azs")
                    brA = em.sbs(ph, [128, 4, 512], BF16, "brA")
                    tcount = [0]
                    for qb in range(4):
                        q0 = 512 * qb
                        for j in range(4):
                            wz_ = load_w(l, "az", j)
                            pb = fm_proj(wz_, j, q0, 512)
                            em.act(azs[:, j, :], pb[:, 0:512], AF.Silu)
                        for h in range(8):
                            g, r = h // 4, h % 4
                            hs = slice(64 * g, 64 * g + 64)
                            mcb = mcb2[h % 2]
                            em.dma("sp", mcb[:, :], mc_in[h, :, q0:q0 + 512], key=mcb)
                            k = tcount[0] % 2
                            tcount[0] += 1
                            ps = PS[k]
                            em.mm(ps[0:127, 0:512], [(kcmpT[hs, 0:127], qT[hs, r, q0:q0 + 512])])
                            em.tt(sbt[k][0:127, :], ps[0:127, 0:512], mcb[0:127, :], ALU.add)
                            em.act(Pt[k][0:127, :], sbt[k][0:127, :], AF.Exp)
                            po = PS[2 + k]
                            pov = po[:, 0:388].re("p (s c) -> p s c", c=97)
                            for sub in range(4):
                                em.mm(pov[:, sub, :], [(Pt[k][0:127, sub * 128:(sub + 1) * 128], rhsc[0:127, g, :])])
                            em.ts(rden[:, :], pov[:, :, 64], 1e-30, None, op0=ALU.max)
                            em.recip(rden[:, :], rden[:, :])
                            em.tt(coef[:, :], rden[:, :], gates[:, 4 * qb:4 * qb + 4, 3 * h], ALU.mult)
                            em.tt(oa[:, :, 64 * h:64 * h + 64], pov[:, :, 0:64], bc(coef[:, :], [128, 4, 64]), ALU.mult)
                            if r == 0:
                                em.tt(imp[:, :, g, :], pov[:, :, 65:97], bc(rden[:, :], [128, 4, 32]), ALU.mult)
                            else:
                                em.tt(tmpi[:, :, :], pov[:, :, 65:97], bc(rden[:, :], [128, 4, 32]), ALU.mult)
                                em.tt(imp[:, :, g, :], imp[:, :, g, :], tmpi[:, :, :], ALU.add, e="pool")
                        for sub in range(4):
                            for g in range(2):
                                em.tt(sc[:, :], imp[:, sub, g, :], keep[:, 4 * qb + sub, :], ALU.mult)
                                em.tt(sc[:, :], sc[:, :], addm[:, 4 * qb + sub, :], ALU.add)
                                em.top8(m8[:, :], sc[:, :])
                                em.mrep(sc2[:, :], m8[:, :], sc[:, :], -3.0e38)
                                em.top8(m8[:, :], sc2[:, :])
                                em.ts(sc2[:, :], sc[:, :], m8[:, 7:8], -NEGM, op0=ALU.is_ge, op1=ALU.mult)
                                em.ts(sc2[:, :], sc2[:, :], NEGM, None, op0=ALU.add)
                                pt = PS[4 + (2 * sub + g) % 2]
                                em.mm(pt[0:32, 0:128], [(sc2[:, :], ident[:, :])])
                                em.copy(penT[:, g, sub * 128:(sub + 1) * 128], pt[0:32, 0:128], e="act")
                        for h in range(8):
                            g, r = h // 4, h % 4
                            hs = slice(64 * g, 64 * g + 64)
                            wsel = 512 * qb + 896
                            msel, mwin = msel2[h % 2], mwin2[h % 2]
                            em.dma("sp", msel[:, 0:wsel], msel_in[h, :, 0:wsel], key=msel)
                            em.dma("sp", mwin[:, :], mwin_in[h, :, :], key=mwin)
                            for br_, b_, tab, kts in ((1, 1, msel, range(0, 4 * qb + 4)), (2, 2, mwin, range(max(0, 4 * qb - 4), 4 * qb + 4))):
                                kts = list(kts)

                                def issue_scores(kt, br_=br_, b_=b_, tab=tab):
                                    dl = 512 * qb - 128 * kt
                                    k = tcount[0] % 2
                                    tcount[0] += 1
                                    ps = PS[k]
                                    pairs = [(kT[hs, b_, kt * 128:(kt + 1) * 128], qT[hs, r, q0:q0 + 512])]
                                    if br_ == 1:
                                        pairs.append((wex[:, kt * 128:(kt + 1) * 128], penT[:, g, :]))
                                    em.mm(ps[:, 0:512], pairs)
                                    em.tt(sbt[k][:, :], ps[:, 0:512], tab[:, dl + 384:dl + 896], ALU.add)
                                    em.act(Pt[k][:, :], sbt[k][:, :], AF.Exp)
                                    return k
                                nxt = issue_scores(kts[0])
                                for ix, kt in enumerate(kts):
                                    k = nxt
                                    if ix + 1 < len(kts):
                                        nxt = issue_scores(kts[ix + 1])
                                    for sub in range(4):
                                        em.mm1(PS[4 + sub][:, 0:65], Pt[k][:, sub * 128:(sub + 1) * 128], vaug[:, kt, 2 * b_ + g, :],
                                               start=(kt == kts[0]), stop=(kt == kts[-1]))
                                for sub in range(4):
                                    em.recip(rden[:, sub:sub + 1], PS[4 + sub][:, 64:65])
                                em.tt(coef[:, :], rden[:, :], gates[:, 4 * qb:4 * qb + 4, 3 * h + br_], ALU.mult)
                                for sub in range(4):
                                    em.ts(tmpo[:, sub, :], PS[4 + sub][:, 0:64], coef[:, sub:sub + 1], None, op0=ALU.mult)
                                em.tt(oa[:, :, 64 * h:64 * h + 64], oa[:, :, 64 * h:64 * h + 64], tmpo[:, :, :], ALU.add, e="pool")
                        for sub in range(4):
                            for j in range(4):
                                pt = PS[4 + j % 2]
                                em.mm(pt[:, 0:128], [(oa[:, sub, 128 * j:128 * j + 128], ident[:, :])])
                                em.tt(brA[:, j, sub * 128:(sub + 1) * 128], pt[:, 0:128], azs[:, j, sub * 128:(sub + 1) * 128], ALU.mult)
                        em.dma("sp", BR[0:4, :, q0:q0 + 512].re("a p t -> p a t"), brA[:, :, :], key=BR)
                    em.barrier()
                if "SAMP" in S:
                  with em.scope() as ph:
                    F_, B_ = F32, BF16
                    wag = em.sbs(ph, [128, 8, 24], B_, "wag")
                    selg = em.sbs(ph, [24, 4, 3, 128], F_, "selg")
                    gsm = em.sbs(ph, [24, 16], F_, "gsm")
                    gfm = em.sbs(ph, [128, 4, 3, 16], F_, "gfm")
                    qs = em.sbs(ph, [128, 4, 16], B_, "qs")
                    oaT = em.sbs(ph, [128, 4, 16], F_, "oaT")
                    mcs = em.sbs(ph, [128, 4, 2, 16], F_, "mcs")
                    coversf = em.sbs(ph, [128, 4, 129], F_, "coversf")
                    rhscs = em.sbs(ph, [128, 4, 2, 194], B_, "rhscs")
                    msels = em.sbs(ph, [128, 64, 2, 16], F_, "msels")
                    mwins = em.sbs(ph, [128, 4, 2, 16], F_, "mwins")
                    msnew = em.sbs(ph, [16, 4, 2, 16], F_, "msnew")
                    keeps = em.sbs(ph, [4, 129], F_, "keeps")
                    adds = em.sbs(ph, [4, 129], F_, "adds")
                    qsel = em.sbs(ph, [16, 4], F_, "qsel")
                    maskc = em.sbs(ph, [16, 2, 2], F_, "maskc")
                    lsel = em.sbs(ph, [4, 2, 128], F_, "lsel")
                    iota = em.sbs(ph, [128, 1], F_, "iota")
                    pti = em.sbs(ph, [128, 256], I32, "pti")
                    ptf = em.sbs(ph, [128, 256], F_, "ptf")
                    idx = em.sbs(ph, [128, 4, 64], I32, "idx")
                    X = em.sbs(ph, [128, 8208], B_, "X")
                    wc = em.sbs(ph, [128, 32, 128], B_, "wc")
                    kcs = em.sbs(ph, [128, 512], B_, "kcs")
                    pgb = [em.sbs(ph, [128, 4, 256], B_, f"pgb{i}") for i in range(2)]
                    KsT = [em.sbs(ph, [128, 512], B_, f"KsT{i}") for i in range(2)]
                    Es = [em.sbs(ph, [128, 4, 2, 16], B_, f"Es{i}") for i in range(2)]
                    sb_ = [em.sbs(ph, [128, 4, 2, 16], F_, f"ssb{i}") for i in range(2)]
                    sbn = em.sbs(ph, [16, 2, 16], F_, "sbn")
                    En = em.sbs(ph, [16, 2, 16], B_, "En")
                    penTf = em.sbs(ph, [128, 2, 64, 4], F_, "penTf")
                    sc = em.sbs(ph, [4, 129], F_, "ssc")
                    sc2 = em.sbs(ph, [4, 129], F_, "ssc2")
                    m8 = em.sbs(ph, [4, 8], F_, "sm8")
                    Xe = em.sbs(ph, [4, 64, 4], F_, "Xe")
                    Xo = em.sbs(ph, [4, 64, 4], F_, "Xo")
                    on = em.sbs(ph, [16, 64], F_, "on")
                    Om = em.sbs(ph, [16, 2, 2, 64], F_, "Om")
                    rd16 = em.sbs(ph, [16, 1], F_, "rd16")
                    impn = em.sbs(ph, [16, 129], F_, "impn")
                    tmp4 = em.sbs(ph, [128, 4], F_, "tmp4")
                    onesb = em.sbs(ph, [128, 1], B_, "onesb")
                    azsS = em.sbs(ph, [128, 4, 16], F_, "azsS")
                    brS = em.sbs(ph, [128, 4, 16], B_, "brS")
                    em.dma("pool", wag[:, :, :], wag_in[l].rearrange("p (a b) -> p a b", a=8), key=wag)
                    em.dma("sp", selg[:], selg_in.rearrange("p (a b c) -> p a b c", a=4, b=3), key=selg)
                    em.dma("sp", mcs[:], mcs_in.rearrange("p (a b c) -> p a b c", a=4, b=2), key=mcs)
                    em.dma("sp", coversf[:], covers_in.rearrange("p (a b) -> p a b", a=4), key=coversf)
                    em.dma("sp", msels[:], msels_in.rearrange("p (a b c) -> p a b c", a=64, b=2), key=msels)
                    em.dma("sp", mwins[:], mwins_in.rearrange("p (a b c) -> p a b c", a=4, b=2), key=mwins)
                    em.dma("sp", msnew[:], msnew_in.rearrange("p (a b c) -> p a b c", a=4, b=2), key=msnew)
                    em.dma("sp", keeps[:], keeps_in[:, :], key=keeps)
                    em.dma("sp", adds[:], adds_in[:, :], key=adds)
                    em.dma("sp", qsel[:], qsel_in[:, :], key=qsel)
                    em.dma("sp", maskc[:], maskc_in.rearrange("p (a b) -> p a b", a=2), key=maskc)
                    em.dma("sp", lsel[:], lsel_in.rearrange("p (a b) -> p a b", a=2), key=lsel)
                    em.dma("sp", iota[:], iota_in[:, :], key=iota)
                    em.dma("sp", pti[:, :], ptab_in[0:1, :].broadcast_to([128, 256]), key=pti)
                    em.copy(ptf[:, :], pti[:, :])
                    em.ts(ptf[:, :], ptf[:, :], 128.0, iota[:, 0:1], op0=ALU.mult, op1=ALU.add)
                    em.copy(idx[:, :, :].re("p s i -> p (s i)"), ptf[:, :])
                    em.memset(onesb[:, :], 1.0, e="dve")
                    em.memset(rhscs[:, :, :, 64:65], 1.0, e="dve")
                    for g in range(2):
                        em.copy(rhscs[:, :, g, 65:194], coversf[:, :, :])
                    em.mm(PS[0][0:24, 0:16], [(wag[:, kc, :], hT[:, kc, T:TT]) for kc in range(8)])
                    em.act(gsm[:, :], PS[0][0:24, 0:16], AF.Sigmoid)
                    for j in range(4):
                        for br_ in range(3):
                            em.mm(PS[1][:, (3 * j + br_) * 16:(3 * j + br_) * 16 + 16], [(selg[:, j, br_, :], gsm[:, :])])
                    em.copy(gfm[:, :, :, :].re("p a b c -> p (a b c)"), PS[1][:, 0:192])
                    em.copy(qs[:, :, :].re("p s (c q) -> p s c q", q=4), qT[:, :, T:TT].re("p c (s q) -> p s c q", q=4))
                    first = {}

                    def branch_out(pnum, pden, g, br_, s):
                        em.ts(rd16[:, :], pden, 1e-30, None, op0=ALU.max)
                        em.recip(rd16[:, :], rd16[:, :])
                        em.ts(on[:, :], pnum, rd16[:, 0:1], None, op0=ALU.mult)
                        onv = View(on, on.t[:, :].unsqueeze(1).unsqueeze(1).broadcast_to([16, 2, 2, 64]))
                        em.tt(Om[:, :, :, :], onv, bc(maskc[:, :, :], [16, 2, 2, 64]), ALU.mult)
                        for jj in range(2):
                            j = 2 * g + jj
                            pj = PS[jj]
                            em.mm(pj[:, 0:4], [(Om[:, jj, :, :].re("p e d -> p (e d)"), qsel[:, :])])
                            if (s, j) not in first:
                                first[(s, j)] = 1
                                em.tt(oaT[:, j, 4 * s:4 * s + 4], pj[:, 0:4], gfm[:, j, br_, 4 * s:4 * s + 4], ALU.mult)
                            else:
                                em.tt(tmp4[:, :], pj[:, 0:4], gfm[:, j, br_, 4 * s:4 * s + 4], ALU.mult)
                                em.tt(oaT[:, j, 4 * s:4 * s + 4], oaT[:, j, 4 * s:4 * s + 4], tmp4[:, :], ALU.add)

                    def gather_group(pool_ap, s, pg, kcol, dst):
                        pb_ = pgb[pg % 2]
                        for pi in range(4):
                            em.idma(pb_[:, pi, :], pool_ap, idx[:, s, 4 * pg + pi:4 * pg + pi + 1], key=pb_)
                        pt = PS[pg % 2]
                        for pi in range(4):
                            em.mm(pt[:, pi * 128:(pi + 1) * 128], [(pb_[:, pi, kcol:kcol + 128], identb[:, :])])
                        em.copy(dst, pt[:, 0:512], e="act" if pg % 2 else "dve")
                        return pb_

                    def score_group(KT_, qsl, tab4, pen4, k):
                        for g in range(2):
                            hs = slice(64 * g, 64 * g + 64)
                            for pi in range(4):
                                em.mm(PS[2 + g][:, pi * 16:pi * 16 + 16], [(KT_[hs, pi * 128:(pi + 1) * 128], qsl[hs, :])])
                        for g in range(2):
                            em.tt(sb_[k][:, :, g, :], PS[2 + g][:, 0:64].re("p (t x) -> p t x", x=16), tab4[:, :, g, :], ALU.add)
                        fl = lambda v: v.re("p a b c -> p (a b c)")
                        if pen4 is not None:
                            for g in range(2):
                                v_ = sb_[k][:, :, g, :].re("p t (c q) -> p t c q", q=4)
                                p4 = pen4(g)
                                em.tt(v_, v_, View(p4.buf, p4.ap.unsqueeze(2).broadcast_to([128, 4, 4, 4])), ALU.add)
                        em.act(fl(Es[k][:, :, :, :]), fl(sb_[k][:, :, :, :]), AF.Exp)

                    import os
                    CUT = float(os.environ.get("SAMP_CUT", "99"))

                    class StopSamp(Exception):
                        pass

                    def cut(n):
                        if CUT <= n:
                            raise StopSamp()
                    for s in range(4 if CUT >= 99 else 1):
                      try:
                        cut(1)
                        qsl = qs[:, s, :]
                        for kv_ in range(2):
                            em.dma("pool", wc[:, :, :], wcmp_in[l, kv_].rearrange("p (a b) -> p a b", a=32), key=wc)
                            em.memset(X[:, 8192:8208], 0.0, e="dve")
                            em.copy(X[:, 8192:8196], kT[:, 0, T + 4 * s:T + 4 * s + 4] if kv_ == 0 else vcT[:, T + 4 * s:T + 4 * s + 4])
                            for pg in range(16):
                                gather_group(cmp_pool[l], s, pg, 128 * kv_, X[:, 512 * pg:512 * pg + 512])
                                cut(2)
                            X16 = X[:, :].re("p (n s) -> p n s", s=16)
                            if kv_ == 0:
                                em.mm(PS[2][:, 0:512], [(wc[:, ms, :], X16[:, ms // 16:ms // 16 + 512, ms % 16]) for ms in range(32)])
                                em.copy(kcs[:, :], PS[2][:, 0:512], e="act")
                            else:
                                for nt in range(4):
                                    pz = PS[2 + nt % 2]
                                    em.mm(pz[:, 0:128], [(X16[:, ms // 16 + 128 * nt:ms // 16 + 128 * nt + 128, ms % 16], wc[:, ms, :]) for ms in range(32)])
                                    em.copy(rhscs[:, nt, :, 0:64], pz[:, 0:128].re("p (g e) -> p g e", g=2), e="act")
                        cut(3)
                        score_group(kcs, qsl, mcs[:, :, :, :], None, 0)
                        cut(3.1)
                        for g in range(2):
                            pc = PS[4 + g]
                            em.mm(pc[0:16, 0:194], [(Es[0][:, nt, g, :], rhscs[:, nt, g, :]) for nt in range(4)])
                            cut(3.2)
                            branch_out(pc[0:16, 0:64], pc[0:16, 64:65], g, 0, s)
                            cut(3.3)
                            em.ts(impn[:, :], pc[0:16, 65:194], rd16[:, 0:1], None, op0=ALU.mult)
                            pi_ = PS[3]
                            em.mm(pi_[0:4, 0:129], [(qsel[:, :], impn[:, :])])
                            em.tt(sc[:, :], pi_[0:4, 0:129], keeps[:, :], ALU.mult)
                            em.tt(sc[:, :], sc[:, :], adds[:, :], ALU.add)
                            cut(3.4)
                            em.top8(m8[:, :], sc[:, :])
                            em.mrep(sc2[:, :], m8[:, :], sc[:, :], -3.0e38)
                            em.top8(m8[:, :], sc2[:, :])
                            em.ts(sc2[:, :], sc[:, :], m8[:, 7:8], -NEGM, op0=ALU.is_ge, op1=ALU.mult)
                            em.ts(sc2[:, :], sc2[:, :], NEGM, None, op0=ALU.add)
                            cut(3.5)
                            pe2 = sc2[:, 0:128].re("p (i two) -> p i two", two=2)
                            i4 = View(ident, ident.t[0:4, 0:4].unsqueeze(1).broadcast_to([4, 64, 4]))
                            em.tt(Xe[:, :, :], bc(pe2[:, :, 0], [4, 64, 4]), i4, ALU.mult)
                            em.tt(Xo[:, :, :], bc(pe2[:, :, 1], [4, 64, 4]), i4, ALU.mult)
                            cut(3.6)
                            pp = PS[3]
                            em.mm(pp[:, 256:512], [(lsel[:, 0, :], Xe[:, :, :].re("p i q -> p (i q)")), (lsel[:, 1, :], Xo[:, :, :].re("p i q -> p (i q)"))])
                            em.copy(penTf[:, g, :, :].re("p i q -> p (i q)"), pp[:, 256:512], e="act")
                        cut(4)
                        for (br_, bidx) in ((1, 1), (2, 2)):
                            ngrp = 16 if br_ == 1 else 1
                            for pg in range(ngrp):
                                k = pg % 2
                                if br_ == 1:
                                    pb_ = gather_group(sel_pool[l], s, pg, 0, KsT[k][:, :])
                                    tab4 = msels[:, 4 * pg:4 * pg + 4, :, :]
                                    pen4 = (lambda g, pg=pg: penTf[:, g, 4 * pg:4 * pg + 4, :])
                                else:
                                    pb_ = pgb[0]
                                    em.dma("pool", pb_[:, :, :], cwin[l, s].rearrange("(t p) c -> p t c", p=128), key=pb_)
                                    pt = PS[0]
                                    for pi in range(4):
                                        em.mm(pt[:, pi * 128:(pi + 1) * 128], [(pb_[:, pi, 0:128], identb[:, :])])
                                    em.copy(KsT[k][:, :], pt[:, 0:512])
                                    tab4 = mwins[:, :, :, :]
                                    pen4 = None
                                score_group(KsT[k], qsl, tab4, pen4, k)
                                for pi in range(4):
                                    for g in range(2):
                                        st_ = (pg == 0 and pi == 0)
                                        em.mm1(PS[4 + g][0:16, 0:64], Es[k][:, pi, g, :], pb_[:, pi, 128 + 64 * g:128 + 64 * g + 64], start=st_, stop=False)
                                        em.mm1(PS[6 + g][0:16, 0:1], Es[k][:, pi, g, :], onesb[:, 0:1], start=st_, stop=False)
                            for g in range(2):
                                hs = slice(64 * g, 64 * g + 64)
                                em.mm(PS[2 + g][0:16, 0:16], [(kT[hs, bidx, T:TT], qsl[hs, :])])
                            for g in range(2):
                                em.tt(sbn[:, g, :], PS[2 + g][0:16, 0:16], msnew[:, s, g, :], ALU.add)
                            em.act(En[:, :, :], sbn[:, :, :], AF.Exp)
                            for g in range(2):
                                em.mm1(PS[4 + g][0:16, 0:64], En[:, g, :], vaug[0:16, 16, 2 * bidx + g, 0:64], start=False, stop=True)
                                em.mm1(PS[6 + g][0:16, 0:1], En[:, g, :], vaug[0:16, 16, 2 * bidx + g, 64:65], start=False, stop=True)
                            for g in range(2):
                                branch_out(PS[4 + g][0:16, 0:64], PS[6 + g][0:16, 0:1], g, br_, s)
                            cut(4 + br_)
                      except StopSamp:
                        pass
                    for j in range(4):
                        wz_ = load_w(l, "az", j)
                        pb = fm_proj(wz_, j, T, TS)
                        em.act(azsS[:, j, :], pb[:, 0:TS], AF.Silu)
                    em.tt(brS[:, :, :], oaT[:, :, :], azsS[:, :, :], ALU.mult)
                    em.dma("sp", BR[0:4, :, T:TT].re("a p t -> p a t"), brS[:, :, :], key=BR)
                em.barrier()

            if "M" in S:
                with em.scope() as ph:
                    cols = em.sbs(ph, [64, NT64, 8], F32, "mcols")
                    decb = em.sbs(ph, [128, 4, NT64], F32, "decb")
                    mhn = em.sbs(ph, [64, 128], F32, "mhn")
                    wsmb = em.sbs(ph, [128, 8, 16], BF16, "wsmb")
                    em.dma("sp", mhn[:], mhn_in[l], key=mhn)
                    em.dma("pool", wsmb[:, :, :], wsm[l].rearrange("p (a b) -> p a b", a=8), key=wsmb)
                    with em.scope() as ph2:
                        rA = em.sbs(ph2, [4, TT], F32, "rA")
                        rB = em.sbs(ph2, [4, TT], F32, "rB")
                        rC = em.sbs(ph2, [4, TT], F32, "rC")
                        rD = em.sbs(ph2, [4, TT], F32, "rD")
                        hc = em.sbs(ph2, [4, 4], F32, "hc")
                        nbf = em.sbs(ph2, [4, 1], F32, "nbf")
                        m0c = em.sbs(ph2, [4, 4], F32, "m0c")
                        nm0 = em.sbs(ph2, [4, 4], F32, "nm0")
                        Mp = em.sbs(ph2, [4, 32], F32, "Mp")
                        dec = em.sbs(ph2, [4, NT64], F32, "dec")
                        mfin = em.sbs(ph2, [4, 5], F32, "mfin")
                        em.dma("sp", hc[:], gcol_in[l], key=hc)
                        em.dma("sp", m0c[:], mm0[l], key=m0c)
                        em.ts(nbf[:], hc[:, 1:2], -1.0, None, op0=ALU.mult)
                        em.ts(nm0[:], m0c[:], -1.0, None, op0=ALU.mult)
                        for bi, (t0, n) in enumerate(BLOCKS):
                            for gi, dst in ((0, rA), (1, rB)):
                                pb = PS[(2 * bi + gi) % 4]
                                em.mm(pb[0:4, 0:n], [(wsmb[:, kc, 4 * gi:4 * gi + 4], hT[:, kc, t0:t0 + n]) for kc in range(8)])
                                if gi == 0:
                                    em.ts(dst[:, t0:t0 + n], pb[0:4, 0:n], hc[:, 0:1], None, op0=ALU.add)
                                else:
                                    em.act(dst[:, t0:t0 + n], pb[0:4, 0:n], AF.Softplus, scale=-1.0, bias=nbf[:, 0:1])
                        em.ts(rB[:], rB[:], -1.0, None, op0=ALU.mult)
                        em.scan(rD[:, 0:T], rB[:, 0:T], rB[:, 0:T], 0.0, ALU.add, ALU.bypass)
                        em.tt(rA[:, 0:T], rA[:, 0:T], rD[:, 0:T], ALU.subtract)
                        em.scan(rC[:, 0:T], rA[:, 0:T], rA[:, 0:T], 0.0, ALU.max, ALU.max)
                        em.memset(Mp[:, 0:1], 0.0, e="dve")
                        Rv = rC[:, 0:T].re("p (c t) -> p c t", t=64)
                        em.copy(Mp[:, 1:32], Rv[:, 0:31, 63])
                        em.tt(dec[:, 0:32], Mp[:, 0:32], Rv[:, :, 63], ALU.subtract)
                        em.tt(mfin[:, 0:1], rC[:, T - 1:T], rD[:, T - 1:T], ALU.add)
                        Mpb = bc(Mp[:, 0:32], [4, 32, 64])
                        em.tt(rA[:, 0:T].re("p (c t) -> p c t", t=64), rA[:, 0:T].re("p (c t) -> p c t", t=64), Mpb, ALU.subtract)
                        em.tt(rD[:, 0:T].re("p (c t) -> p c t", t=64), rD[:, 0:T].re("p (c t) -> p c t", t=64), Mpb, ALU.add)
                        for s in range(4):
                            c0 = T + 4 * s
                            em.scan(rD[:, c0:c0 + 4], rB[:, c0:c0 + 4], rB[:, c0:c0 + 4], 0.0, ALU.add, ALU.bypass)
                            em.tt(rA[:, c0:c0 + 4], rA[:, c0:c0 + 4], rD[:, c0:c0 + 4], ALU.subtract)
                            em.scan(rC[:, c0:c0 + 4], rA[:, c0:c0 + 4], rA[:, c0:c0 + 4], m0c[:, s:s + 1], ALU.max, ALU.max)
                            em.tt(dec[:, 32 + s:33 + s], m0c[:, s:s + 1], rC[:, c0 + 3:c0 + 4], ALU.subtract)
                            em.tt(mfin[:, 1 + s:2 + s], rC[:, c0 + 3:c0 + 4], rD[:, c0 + 3:c0 + 4], ALU.add)
                            em.ts(rA[:, c0:c0 + 4], rA[:, c0:c0 + 4], m0c[:, s:s + 1], None, op0=ALU.subtract)
                            em.ts(rD[:, c0:c0 + 4], rD[:, c0:c0 + 4], m0c[:, s:s + 1], None, op0=ALU.add)
                        em.act(rA[:], rA[:], AF.Exp)
                        em.act(rD[:], rD[:], AF.Exp, scale=-1.0)
                        em.act(dec[:], dec[:], AF.Exp)
                        store("sp", o_mm_p[l].rearrange("(h o) -> h o", o=1), mfin[:, 0:1])
                        store("sp", o_mm_s[l], mfin[:, 1:5])
                        pc = PS[4]
                        for i in range(NT64):
                            t0, n = t64(i)
                            em.mm(pc[0:n, 8 * i:8 * i + 4], [(rA[:, t0:t0 + n], ident[0:4, 0:4])])
                            em.mm(pc[0:n, 8 * i + 4:8 * i + 8], [(rD[:, t0:t0 + n], ident[0:4, 0:4])])
                        em.copy(cols[:, 0:32, :], pc[0:64, 0:256].re("p (i c) -> p i c", c=8))
                        em.copy(cols[0:4, 32:36, :], pc[0:4, 256:288].re("p (i c) -> p i c", c=8))
                        pd = PS[5]
                        for h in range(4):
                            em.mm(pd[:, h * NT64:(h + 1) * NT64], [(onehot[:, h, :], dec[:, :])])
                        em.copy(decb[:], pd[:, 0:4 * NT64].re("p (h c) -> p h c", h=4))
                        em.barrier()
                    mqT = em.sbs(ph, [128, TT], BF16, "mqT")
                    mkT = em.sbs(ph, [128, TT], BF16, "mkT")
                    gmT = em.sbs(ph, [128, TT], F32, "gmT")
                    gtmp = em.sbs(ph, [128, 512], F32, "gtmp")
                    mktm = em.sbs(ph, [64, NT64, 128], BF16, "mktm")
                    vb = em.sbs(ph, [64, NT64, 129], BF16, "vb")
                    wtmb = em.sbs(ph, [128, 8, 256], BF16, "wtmmb")
                    Cst = em.sbs(ph, [128, 129], F32, "Cst")
                    Cb = em.sbs(ph, [128, 129], BF16, "Cb")
                    At = [em.sbs(ph, [64, 64], BF16, f"At{i}") for i in range(2)]
                    dn = [em.sbs(ph, [64, 2], F32, f"dn{i}") for i in range(2)]
                    hm = [em.sbs(ph, [64, 128], F32, f"hm{i}") for i in range(2)]
                    hsq = em.sbs(ph, [64, 128], F32, "hsq")
                    hnb = [em.sbs(ph, [64, 128], BF16, f"hnb{i}") for i in range(2)]
                    brh = em.sbs(ph, [128, TT], BF16, "brh")
                    for h in range(4):
                        em.dma("pool", wtmb[:, :, :], wtmm[l, h].rearrange("p (a b) -> p a b", a=8), key=wtmb)
                        wq_, wk_, wo_, wz_ = (load_w(l, nm, h) for nm in ("mq", "mk", "mo", "mz"))
                        for bi, (t0, n) in enumerate(BLOCKS):
                            pb = fm_proj(wq_, 0, t0, n)
                            em.copy(mqT[:, t0:t0 + n], pb[:, 0:n], e="act")
                            pb = fm_proj(wk_, 1, t0, n)
                            em.ts(mkT[:, t0:t0 + n], pb[:, 0:n], 128.0 ** -0.5, None, op0=ALU.mult)
                            pb = fm_proj(wo_, 0, t0, n)
                            em.act(gmT[:, t0:t0 + n], pb[:, 0:n], AF.Sigmoid)
                            pb = fm_proj(wz_, 1, t0, n)
                            em.act(gtmp[:, 0:n], pb[:, 0:n], AF.Silu)
                            em.tt(gmT[:, t0:t0 + n], gmT[:, t0:t0 + n], gtmp[:, 0:n], ALU.mult)
                        for i in range(NT64):
                            t0, n = t64(i)
                            pb = PS[2 + i % 2]
                            em.mm(pb[0:n, 0:256], [(hT[:, kc, t0:t0 + n], wtmb[:, kc, :]) for kc in range(8)])
                            em.ts(mktm[0:n, i, :], pb[0:n, 0:128], 128.0 ** -0.5, None, op0=ALU.mult)
                            em.ts(vb[0:n, i, 0:128], pb[0:n, 128:256], cols[0:n, i, h:h + 1], None, op0=ALU.mult)
                            em.copy(vb[0:n, i, 128:129], cols[0:n, i, h:h + 1], e="act")

                        pend = []

                        def flush():
                            while pend:
                                pend.pop(0)()

                        def chunk(i):
                            t0, n = t64(i)
                            k = i % 2
                            pa = PS[4 + k]
                            em.mm(pa[0:n, 0:n], [(mkT[:, t0:t0 + n], mqT[:, t0:t0 + n])])
                            em.tt(At[k][0:n, 0:n], pa[0:n, 0:n], masku[0:n, 0:n], ALU.mult)
                            po = PS[6 + k]
                            em.mm(po[0:n, 0:129], [(mqT[:, t0:t0 + n], Cb[:, :]), (At[k][0:n, 0:n], vb[0:n, i, :])])
                            pk = PS[k]
                            em.mm(pk[:, 0:129], [(mktm[0:n, i, :], vb[0:n, i, :])])
                            flush()
                            em.ts(Cst[:, :], Cst[:, :], decb[:, h, i:i + 1], None, op0=ALU.mult)
                            em.stt(Cst[:, :], pk[:, 0:129], decb[:, h, i:i + 1], Cst[:, :], ALU.mult, ALU.add)
                            em.copy(Cb[:, :], Cst[:, :], e="act")
                            em.act(dn[k][0:n, 0:1], po[0:n, 128:129], AF.Abs)
                            em.tt(dn[k][0:n, 0:1], dn[k][0:n, 0:1], cols[0:n, i, 4 + h:5 + h], ALU.max)
                            em.recip(dn[k][0:n, 0:1], dn[k][0:n, 0:1])
                            em.ts(hm[k][0:n, :], po[0:n, 0:128], dn[k][0:n, 0:1], None, op0=ALU.mult)

                            def tail():
                                em.act(hsq[0:n, :], hm[k][0:n, :], AF.Square, accum_out=dn[k][0:n, 1:2])
                                em.act(dn[k][0:n, 1:2], dn[k][0:n, 1:2], AF.Sqrt, scale=1.0 / 128, bias=EPS)
                                em.recip(dn[k][0:n, 1:2], dn[k][0:n, 1:2])
                                em.stt(hnb[k][0:n, :], hm[k][0:n, :], dn[k][0:n, 1:2], mhn[0:n, :], ALU.mult, ALU.mult)
                                pt = PS[2 + k]
                                em.mm(pt[:, 0:n], [(hnb[k][0:n, :], identb[0:n, 0:n])])
                                em.tt(brh[:, t0:t0 + n], pt[:, 0:n], gmT[:, t0:t0 + n], ALU.mult)
                            pend.append(tail)

                        em.memset(Cst[:, :], 0.0, e="dve")
                        em.memset(Cb[:, :], 0.0, e="dve")
                        for i in range(32):
                            chunk(i)
                        flush()
                        store("sp", o_mC_p[l, h], Cst[:, 0:128])
                        store("sp", o_mn_p[l, h].rearrange("(p o) -> p o", o=1), Cst[:, 128:129])
                        for s in range(4):
                            em.dma("sp", Cst[:, 0:128], mC0[l, s, h], key=Cst)
                            em.dma("sp", Cst[:, 128:129], mn0[l, s, h].rearrange("(p o) -> p o", o=1), key=Cst)
                            em.copy(Cb[:, :], Cst[:, :], e="act")
                            chunk(32 + s)
                            flush()
                            store("sp", o_mC_s[l, s, h], Cst[:, 0:128])
                            store("sp", o_mn_s[l, s, h].rearrange("(p o) -> p o", o=1), Cst[:, 128:129])
                        em.dma("sp", BR[4 + h, :, :], brh[:, :], key=BR)
                    em.barrier()

            if "G" in S:
                with em.scope() as ph:
                    ghn = em.sbs(ph, [64, 128], F32, "ghn")
                    wsmb = em.sbs(ph, [128, 8, 16], BF16, "wsmb")
                    gcw = em.sbs(ph, [128, 12, 4], F32, "gcw")
                    Gr = em.sbs(ph, [4, TT], F32, "Gr")
                    Bg = em.sbs(ph, [4, TT], F32, "Bg")
                    gct = em.sbs(ph, [64, NT64, 16], F32, "gct")
                    eGLb = em.sbs(ph, [128, 4, NT64], F32, "eGLb")
                    em.dma("sp", ghn[:], ghn_in[l], key=ghn)
                    em.dma("sp", gcw[:], gcw_in[l].rearrange("j p t -> p j t"), key=gcw)
                    em.dma("pool", wsmb[:, :, :], wsm[l].rearrange("p (a b) -> p a b", a=8), key=wsmb)
                    with em.scope() as ph2:
                        rA = em.sbs(ph2, [4, TT], F32, "grA")
                        rK = em.sbs(ph2, [4, TT], F32, "grK")
                        rDf = em.sbs(ph2, [4, TT], F32, "grD")
                        rm = em.sbs(ph2, [4, TT], F32, "grm")
                        hc = em.sbs(ph2, [4, 4], F32, "ghc")
                        nA = em.sbs(ph2, [4, 1], F32, "gnA")
                        GL = em.sbs(ph2, [4, NT64], F32, "GL")
                        em.dma("sp", hc[:], gcol_in[l], key=hc)
                        em.dma("sp", rm[:], rmask_in[:, :], key=rm)
                        em.act(nA[:], hc[:, 2:3], AF.Exp)
                        em.ts(nA[:], nA[:], -1.0, None, op0=ALU.mult)
                        for bi, (t0, n) in enumerate(BLOCKS):
                            for gi in (2, 3):
                                pb = PS[(2 * bi + gi) % 4]
                                em.mm(pb[0:4, 0:n], [(wsmb[:, kc, 4 * gi:4 * gi + 4], hT[:, kc, t0:t0 + n]) for kc in range(8)])
                                if gi == 2:
                                    em.act(rA[:, t0:t0 + n], pb[0:4, 0:n], AF.Softplus, bias=hc[:, 3:4])
                                else:
                                    em.act(Bg[:, t0:t0 + n], pb[0:4, 0:n], AF.Sigmoid)
                        em.ts(rA[:], rA[:], nA[:, 0:1], None, op0=ALU.mult)
                        em.scan(Gr[:], rm[:], rA[:], 0.0, ALU.mult, ALU.add)
                        em.act(rK[:], Gr[:], AF.Exp)
                        em.tt(rK[:], rK[:], Bg[:], ALU.mult)
                        Gp = Gr[:, 0:T].re("p (c t) -> p c t", t=64)
                        Gs = Gr[:, T:TT].re("p (c t) -> p c t", t=4)
                        em.copy(GL[:, 0:32], Gp[:, :, 63])
                        em.copy(GL[:, 32:36], Gs[:, :, 3])
                        em.tt(rDf[:, 0:T].re("p (c t) -> p c t", t=64), bc(GL[:, 0:32], [4, 32, 64]), Gp, ALU.subtract)
                        em.tt(rDf[:, T:TT].re("p (c t) -> p c t", t=4), bc(GL[:, 32:36], [4, 4, 4]), Gs, ALU.subtract)
                        em.act(rDf[:], rDf[:], AF.Exp)
                        em.act(GL[:], GL[:], AF.Exp)
                        em.ts(rA[:], Gr[:], -1.0, None, op0=ALU.mult)
                        for i in range(NT64):
                            t0, n = t64(i)
                            pc = PS[4 + (i // 18)]
                            o = 16 * (i % 18)
                            for q, row in enumerate((Bg, rK, rDf, rA)):
                                em.mm(pc[0:n, o + 4 * q:o + 4 * q + 4], [(row[:, t0:t0 + n], ident[0:4, 0:4])])
                        em.copy(gct[:, 0:18, :], PS[4][0:64, 0:288].re("p (i c) -> p i c", c=16))
                        em.copy(gct[:, 18:32, :], PS[5][0:64, 0:224].re("p (i c) -> p i c", c=16))
                        em.copy(gct[0:4, 32:36, :], PS[5][0:4, 224:288].re("p (i c) -> p i c", c=16))
                        pd = PS[6]
                        for h in range(4):
                            em.mm(pd[:, h * NT64:(h + 1) * NT64], [(onehot[:, h, :], GL[:, :])])
                        em.copy(eGLb[:], pd[:, 0:4 * NT64].re("p (h c) -> p h c", h=4))
                        em.barrier()
                    gqT = em.sbs(ph, [128, TT], BF16, "gqT")
                    gkT = em.sbs(ph, [128, TT], BF16, "gkT")
                    gkbT = em.sbs(ph, [128, TT], BF16, "gkbT")
                    gqGT = em.sbs(ph, [128, TT], BF16, "gqGT")
                    kbG = em.sbs(ph, [64, NT64, 128], BF16, "kbG")
                    kdec = em.sbs(ph, [64, NT64, 128], BF16, "kdec")
                    vbt = em.sbs(ph, [64, NT64, 128], BF16, "vbt")
                    attnT = em.sbs(ph, [64, NT64, 64], BF16, "attnT")
                    gg = em.sbs(ph, [128, TT], BF16, "gg")
                    brh = em.sbs(ph, [128, TT], BF16, "gbrh")
                    Ttb = em.sbs(ph, [64, 32, 64], BF16, "Ttb")
                    Ttb4 = em.sbs(ph, [4, 4, 4], BF16, "Ttb4")
                    Sst = em.sbs(ph, [128, 128], F32, "Sst")
                    Sb = em.sbs(ph, [128, 128], BF16, "Sb")
                    SP0 = 2051
                    for h in range(4):
                        with em.scope() as ph2:
                            wkA = em.sbs(ph2, [128, 2080], F32, "wkA")
                            wkB = em.sbs(ph2, [128, TT], F32, "wkB")
                            sqb = em.sbs(ph2, [128, 512], F32, "sqb")
                            qf = em.sbs(ph2, [128, 512], F32, "qf")
                            dtm = [em.sbs(ph2, [64, 64], F32, f"dtm{i}") for i in range(2)]
                            atf = [em.sbs(ph2, [64, 64], F32, f"atf{i}") for i in range(2)]
                            wz_ = load_w(l, "gz", h)
                            for bi, (t0, n) in enumerate(BLOCKS):
                                pb = fm_proj(wz_, bi, t0, n)
                                em.act(gg[:, t0:t0 + n], pb[:, 0:n], AF.Silu)
                            sview = wkA[:, SP0:SP0 + 28].re("p (s c) -> p s c", c=7)
                            for comp in range(3):
                                j = 4 * comp + h
                                w_ = load_w(l, "gqkv", j)
                                em.memset(wkA[:, 0:3], 0.0, e="dve")
                                for s_ in range(4):
                                    em.dma("sp", wkA[:, SP0 + 7 * s_:SP0 + 7 * s_ + 3], gcv0[l, s_, j], key=wkA)
                                for bi, (t0, n) in enumerate(BLOCKS):
                                    pb = fm_proj(w_, bi, t0, n)
                                    if bi < 4:
                                        em.copy(wkA[:, 3 + t0:3 + t0 + n], pb[:, 0:n], e="act")
                                    else:
                                        em.copy(sview[:, :, 3:7], pb[:, 0:16].re("p (s c) -> p s c", c=4), e="act")
                                store("sp", o_gc_p[l, j], wkA[:, T:T + 3])
                                for s_ in range(4):
                                    store("sp", o_gc_s[l, s_, j], wkA[:, SP0 + 7 * s_ + 4:SP0 + 7 * s_ + 7])
                                for (src_, dst_) in ((lambda k: wkA[:, k:k + T], wkB[:, 0:T]),
                                                     (lambda k: sview[:, :, k:k + 4], wkB[:, T:TT].re("p (s c) -> p s c", c=4))):
                                    em.ts(dst_, src_(0), gcw[:, j, 0:1], None, op0=ALU.mult)
                                    for k in range(1, 4):
                                        em.stt(dst_, src_(k), gcw[:, j, k:k + 1], dst_, ALU.mult, ALU.add)
                                em.act(wkB[:], wkB[:], AF.Silu)
                                if comp < 2:
                                    for bi, (t0, n) in enumerate(BLOCKS):
                                        em.act(sqb[:, 0:n], wkB[:, t0:t0 + n], AF.Square)
                                        pss = PS[2 + bi % 2]
                                        em.mm(pss[:, 0:n], [(ones128[:, :], sqb[:, 0:n])])
                                        em.act(sqb[:, 0:n], pss[:, 0:n], AF.Sqrt, bias=EPS)
                                        em.recip(sqb[:, 0:n], sqb[:, 0:n])
                                        pbc = PS[4 + bi % 2]
                                        if comp == 0:
                                            em.stt(qf[:, 0:n], wkB[:, t0:t0 + n], 128.0 ** -0.5, sqb[:, 0:n], ALU.mult, ALU.mult)
                                            em.copy(gqT[:, t0:t0 + n], qf[:, 0:n], e="act")
                                            em.mm(pbc[:, 0:n], [(onehot[:, h, :], Gr[:, t0:t0 + n])])
                                            em.act(sqb[:, 0:n], pbc[:, 0:n], AF.Exp)
                                            em.tt(gqGT[:, t0:t0 + n], qf[:, 0:n], sqb[:, 0:n], ALU.mult)
                                        else:
                                            em.tt(qf[:, 0:n], wkB[:, t0:t0 + n], sqb[:, 0:n], ALU.mult)
                                            em.copy(gkT[:, t0:t0 + n], qf[:, 0:n], e="act")
                                            em.mm(pbc[:, 0:n], [(onehot[:, h, :], Bg[:, t0:t0 + n])])
                                            em.tt(gkbT[:, t0:t0 + n], qf[:, 0:n], pbc[:, 0:n], ALU.mult)
                                    if comp == 1:
                                        for i in range(NT64):
                                            t0, n = t64(i)
                                            pt = PS[6 + i % 2]
                                            em.mm(pt[0:n, 0:128], [(gkT[:, t0:t0 + n], identb[:, :])])
                                            em.ts(kbG[0:n, i, :], pt[0:n, 0:128], gct[0:n, i, 4 + h:5 + h], None, op0=ALU.mult)
                                            em.ts(kdec[0:n, i, :], pt[0:n, 0:128], gct[0:n, i, 8 + h:9 + h], None, op0=ALU.mult)
                                else:
                                    for i in range(NT64):
                                        t0, n = t64(i)
                                        pt = PS[6 + i % 2]
                                        em.mm(pt[0:n, 0:128], [(wkB[:, t0:t0 + n], ident[:, :])])
                                        em.ts(vbt[0:n, i, :], pt[0:n, 0:128], gct[0:n, i, h:h + 1], None, op0=ALU.mult)
                            for i in range(NT64):
                                t0, n = t64(i)
                                k = i % 2
                                pg = PS[k]
                                em.mm(pg[0:n, 0:n], [(onehot[:, h, 0:n], Gr[:, t0:t0 + n])])
                                em.stt(dtm[k][0:n, 0:n], pg[0:n, 0:n], gct[0:n, i, 12 + h:13 + h], maskb[0:n, 0:n], ALU.add, ALU.add)
                                em.act(dtm[k][0:n, 0:n], dtm[k][0:n, 0:n], AF.Exp)
                                pkk = PS[2 + k]
                                em.mm(pkk[0:n, 0:n], [(gkT[:, t0:t0 + n], gkbT[:, t0:t0 + n])])
                                em.tt(atf[k][0:n, 0:n], pkk[0:n, 0:n], dtm[k][0:n, 0:n], ALU.mult)
                                em.tt(atf[k][0:n, 0:n], atf[k][0:n, 0:n], strict[0:n, 0:n], ALU.mult)
                                if i < 32:
                                    em.dma("sp", SA[i, :].re("(j x) -> j x", x=64), atf[k][0:64, 0:64], key=SA)
                                else:
                                    em.dma("sp", SA4[i - 32, :].re("(j x) -> j x", x=4), atf[k][0:4, 0:4], key=SA4)
                                pkq = PS[4 + k]
                                em.mm(pkq[0:n, 0:n], [(gkT[:, t0:t0 + n], gqT[:, t0:t0 + n])])
                                em.tt(attnT[0:n, i, 0:n], pkq[0:n, 0:n], dtm[k][0:n, 0:n], ALU.mult)
                            em.barrier()
                        with em.scope() as ph2:
                            Atp = em.sbs(ph2, [128, 4096], F32, "Atp")
                            Tt = em.sbs(ph2, [128, 64, 16], F32, "Tt")
                            tmp = em.sbs(ph2, [128, 16, 64], F32, "tmpS")
                            red = em.sbs(ph2, [128, 16], F32, "red")
                            At4 = em.sbs(ph2, [4, 16], F32, "At4")
                            Tt4 = em.sbs(ph2, [4, 4, 4], F32, "Tt4")
                            for rb in range(4):
                                em.dma("sp", Atp[32 * rb:32 * rb + 32, :], SA[:, :], key=Atp)
                            em.dma("sp", Tt[:, :, :], ttinit_in.rearrange("p (j r) -> p j r", r=16), key=Tt)
                            em.dma("sp", At4[:, :], SA4[:, :], key=At4)
                            em.dma("sp", Tt4[:, :, :], ttinit4_in.rearrange("p (j r) -> p j r", r=4), key=Tt4)
                            for (A_, T_, L_, R_, np_) in ((Atp, Tt, 64, 16, 128), (At4, Tt4, 4, 4, 4)):
                                for j in range(L_ - 2, -1, -1):
                                    cnt = L_ - 1 - j
                                    in0 = T_[0:np_, j + 1:L_, :].re("p i r -> p r i")
                                    a1 = A_[0:np_, j * L_ + j + 1:j * L_ + L_]
                                    in1 = View(a1.buf, a1.ap.unsqueeze(1).broadcast_to([np_, R_, cnt]))
                                    em.tt(tmp[0:np_, 0:R_, 0:cnt], in0, in1, ALU.mult)
                                    em.reduce(red[0:np_, 0:R_], tmp[0:np_, 0:R_, 0:cnt])
                                    em.tt(T_[0:np_, j, :], T_[0:np_, j, :], red[0:np_, 0:R_], ALU.subtract)
                            for rb in range(4):
                                em.dma("sp", STT[:, :, 16 * rb:16 * rb + 16], Tt[32 * rb:32 * rb + 32, :, :], key=STT)
                            em.dma("sp", ST4[:, :], Tt4[:, :, :].re("p j r -> p (j r)"), key=ST4)
                            em.dma("pool", Ttb[:, :, :], STT[:, :, :].re("c j r -> j c r"), key=Ttb)
                            em.dma("pool", Ttb4[:, :, :], ST4[:, :].re("s (j r) -> j s r", r=4), key=Ttb4)
                            em.barrier()
                        with em.scope() as ph2:
                            usb = [em.sbs(ph2, [64, 128], F32, f"usb{i}") for i in range(2)]
                            wT = [em.sbs(ph2, [128, 64], BF16, f"wT{i}") for i in range(2)]
                            vnew = [em.sbs(ph2, [64, 128], BF16, f"vnew{i}") for i in range(2)]
                            og = [em.sbs(ph2, [64, 128], F32, f"og{i}") for i in range(2)]
                            osq = em.sbs(ph2, [64, 128], F32, "osq")
                            onb = [em.sbs(ph2, [64, 128], BF16, f"onb{i}") for i in range(2)]
                            rs2 = [em.sbs(ph2, [64, 1], F32, f"rs2{i}") for i in range(2)]

                            gpend = []

                            def gflush():
                                while gpend:
                                    gpend.pop(0)()

                            def gindep(i):
                                t0, n = t64(i)
                                k = i % 2
                                Ttv = Ttb[0:n, i, 0:n] if i < 32 else Ttb4[0:n, i - 32, 0:n]
                                pu = PS[k]
                                em.mm(pu[0:n, 0:128], [(Ttv, vbt[0:n, i, :])])
                                em.copy(usb[k][0:n, :], pu[0:n, 0:128], e="act")
                                pw = PS[2 + k]
                                em.mm(pw[:, 0:n], [(kbG[0:n, i, :], Ttv)])
                                em.copy(wT[k][:, 0:n], pw[:, 0:n], e="act")

                            def gchunk(i, nxt=None):
                                t0, n = t64(i)
                                k = i % 2
                                pws = PS[4 + k]
                                em.mm(pws[0:n, 0:128], [(wT[k][:, 0:n], Sb[:, :])])
                                em.tt(vnew[k][0:n, :], usb[k][0:n, :], pws[0:n, 0:128], ALU.subtract)
                                po = PS[6 + k]
                                em.mm(po[0:n, 0:128], [(gqGT[:, t0:t0 + n], Sb[:, :]), (attnT[0:n, i, 0:n], vnew[k][0:n, :])])
                                pS_ = PS[k]
                                em.mm(pS_[:, 0:128], [(kdec[0:n, i, :], vnew[k][0:n, :])])
                                em.stt(Sst[:, :], Sst[:, :], eGLb[:, h, i:i + 1], pS_[:, 0:128], ALU.mult, ALU.add)
                                em.copy(Sb[:, :], Sst[:, :], e="act")
                                if nxt is not None:
                                    gindep(nxt)
                                gflush()
                                em.copy(og[k][0:n, :], po[0:n, 0:128])

                                def tail():
                                    em.act(osq[0:n, :], og[k][0:n, :], AF.Square, accum_out=rs2[k][0:n, :])
                                    em.act(rs2[k][0:n, :], rs2[k][0:n, :], AF.Sqrt, scale=1.0 / 128, bias=EPS)
                                    em.recip(rs2[k][0:n, :], rs2[k][0:n, :])
                                    em.stt(onb[k][0:n, :], og[k][0:n, :], rs2[k][0:n, :], ghn[0:n, :], ALU.mult, ALU.mult)
                                    pt = PS[4 + k]
                                    em.mm(pt[:, 0:n], [(onb[k][0:n, :], identb[0:n, 0:n])])
                                    em.tt(brh[:, t0:t0 + n], pt[:, 0:n], gg[:, t0:t0 + n], ALU.mult)
                                gpend.append(tail)

                            em.memset(Sst[:, :], 0.0, e="dve")
                            em.memset(Sb[:, :], 0.0, e="dve")
                            gindep(0)
                            for i in range(32):
                                gchunk(i, i + 1 if i < 31 else None)
                            gflush()
                            store("sp", o_gS_p[l, h], Sst[:, :])
                            for s_ in range(4):
                                em.dma("sp", Sst[:, :], gS0[l, s_, h], key=Sst)
                                em.copy(Sb[:, :], Sst[:, :], e="act")
                                gindep(32 + s_)
                                gchunk(32 + s_)
                                gflush()
                                store("sp", o_gS_s[l, s_, h], Sst[:, :])
                            em.dma("sp", BR[8 + h, :, :], brh[:, :], key=BR)
                            em.barrier()
                    em.barrier()
            if "OUT" in S:
                with em.scope() as ph:
                    wbrb = em.sbs(ph, [128, 12, 1024], BF16, "wbrb")
                    woutb = em.sbs(ph, [128, 8, 1024], BF16, "woutb")
                    brb = em.sbs(ph, [128, 12, 512], BF16, "brb")
                    yT = em.sbs(ph, [128, 8, 512], BF16, "yT")
                    sg = [em.sbs(ph, [128, 512], F32, f"sg{i}") for i in range(3)]
                    acc = em.sbs(ph, [128, 512], F32, "macc")
                    tmp = em.sbs(ph, [128, 512], F32, "mtmp")
                    xt = [em.sbs(ph, [128, D], F32, f"oxt{i}") for i in range(2)]
                    for a in range(12):
                        em.dma("pool", wbrb[:, a, :], wbr_in[l, :, a * 1024:(a + 1) * 1024], key=wbrb)
                    for a in range(8):
                        em.dma("pool", woutb[:, a, :], wout_in[l, :, a * 1024:(a + 1) * 1024], key=woutb)
                    xcnt = 0
                    for bi, (t0, n) in enumerate(BLOCKS):
                        em.dma("sp", brb[:, :, 0:n], BR[:, :, t0:t0 + n].re("a p t -> p a t"), key=brb)
                        for dmc in range(8):
                            for i in range(3):
                                em.mm(PS[i][:, 0:n], [(wbrb[:, 4 * i + e, dmc * 128:(dmc + 1) * 128], brb[:, 4 * i + e, 0:n]) for e in range(4)])
                                w_ = load_w(l, "merge", i * 8 + dmc)
                                em.mm(PS[3 + i][:, 0:n], [(w_[:, kc, :], hT[:, kc, t0:t0 + n]) for kc in range(8)])
                                em.act(sg[i][:, 0:n], PS[3 + i][:, 0:n], AF.Sigmoid)
                            em.tt(acc[:, 0:n], sg[0][:, 0:n], PS[0][:, 0:n], ALU.mult)
                            em.tt(tmp[:, 0:n], sg[1][:, 0:n], PS[1][:, 0:n], ALU.mult)
                            em.tt(acc[:, 0:n], acc[:, 0:n], tmp[:, 0:n], ALU.add, e="pool")
                            em.tt(tmp[:, 0:n], sg[2][:, 0:n], PS[2][:, 0:n], ALU.mult)
                            em.tt(yT[:, dmc, 0:n], acc[:, 0:n], tmp[:, 0:n], ALU.add)
                        for sub in range((n + 127) // 128):
                            nn = min(128, n - 128 * sub)
                            r0 = t0 + 128 * sub
                            xb = xt[xcnt % 2]
                            xcnt += 1
                            if bi < 4:
                                src = xp[r0:r0 + nn, :] if l == 0 else YP[r0:r0 + nn, :]
                                dst = YP[r0:r0 + nn, :]
                            else:
                                src = xs[:, :] if l == 0 else YS[:, :]
                                dst = YS[:, :]
                            em.dma("sp", xb[0:nn, :], src, key=xb)
                            for half in range(2):
                                po = PS[6 + half]
                                em.mm(po[0:nn, 0:512], [(yT[:, dmc, 128 * sub:128 * sub + nn], woutb[:, dmc, 512 * half:512 * half + 512]) for dmc in range(8)])
                                em.tt(xb[0:nn, 512 * half:512 * half + 512], xb[0:nn, 512 * half:512 * half + 512], po[0:nn, 0:512], ALU.add)
                            em.dma("sp", dst, xb[0:nn, :], key=xb, final=True)
                    em.barrier()
        em.barrier()
        em.finish()
        stats = dict(peak_sbuf=em.peak_bytes, peak_at=getattr(em, 'peak_at', ''), ninst=em.ninst, nwaits=em.nwaits, nsem=em.nsem, cnt=dict(em.cnt))
    return nc, stats, in_names


def _prep_inputs(inp, stages):
    f = np.float32
    w_in = np.asarray(inp["w_in"], f)

    def kcl(W):
        C = W.shape[1]
        return np.ascontiguousarray(W.reshape(8, 128, C).transpose(1, 0, 2).reshape(128, 8 * C))
    wfm = np.stack([np.stack([kcl(w_in[l][:, idx]) for (_, _, idx) in FM]) for l in range(2)])
    wtma = np.stack([kcl(w_in[l][:, TMA_COLS]) for l in range(2)])
    wtmm = np.stack([np.stack([kcl(w_in[l][:, TMM_COLS[h]]) for h in range(4)]) for l in range(2)])
    wsm = np.stack([kcl(w_in[l][:, SM_COLS]) for l in range(2)])
    wbr = np.asarray(inp["w_branch"], f).reshape(2, 3, 4, 128, 1024).transpose(0, 3, 1, 2, 4).reshape(2, 128, 12 * 1024)
    wout = np.asarray(inp["w_out"], f).reshape(2, 8, 128, 1024).transpose(0, 2, 1, 3).reshape(2, 128, 8 * 1024)
    gcols = np.stack([np.asarray(inp[k], f) for k in ("m_bi", "m_bf", "g_A_log", "g_dt_bias")], axis=2)
    gcw = np.asarray(inp["g_conv"], f).reshape(2, 4, 12, 128).transpose(0, 2, 3, 1)
    onehot = np.zeros((4, 4, 128), f)
    for h in range(4):
        onehot[h, h, :] = 1.0
    ii = np.arange(64)
    masku = (ii[None, :] >= ii[:, None]).astype(f)
    maskb = np.where(ii[:, None] <= ii[None, :], 0.0, NEGM).astype(f)
    strict = (ii[:, None] < ii[None, :]).astype(f)
    rmask = np.ones((4, TT), f)
    rmask[:, 0:T:64] = 0.0
    rmask[:, T::4] = 0.0
    ttinit = np.zeros((128, 64, 16), f)
    for p in range(128):
        rb = p // 32
        for r_ in range(16):
            ttinit[p, 16 * rb + r_, r_] = 1.0
    shared = {
        "wfm": wfm, "wtma": wtma, "wtmm": wtmm, "wsm": wsm,
        "wbr": np.ascontiguousarray(wbr), "wout": np.ascontiguousarray(wout),
        "normg": np.ascontiguousarray(np.broadcast_to(np.asarray(inp["norm_g"], f)[:, None, :], (2, 128, D))),
        "aqn": np.ascontiguousarray(np.asarray(inp["a_qn"], f)[:, np.arange(128) % 64][:, :, None]),
        "akn": np.ascontiguousarray(np.asarray(inp["a_kn"], f)[:, :, np.arange(128) % 64][:, :, :, None]),
        "mhn": np.ascontiguousarray(np.broadcast_to(np.asarray(inp["m_hn"], f)[:, None, :], (2, 64, 128))),
        "ghn": np.ascontiguousarray(np.broadcast_to(np.asarray(inp["g_hn"], f)[:, None, :], (2, 64, 128))),
        "gcols": np.ascontiguousarray(gcols), "gcw": np.ascontiguousarray(gcw),
        "ident": np.eye(128, dtype=f),
        "bd64": np.kron(np.eye(2, dtype=f), np.ones((64, 64), f)),
        "onehot": onehot.reshape(4, 512), "masku": masku, "maskb": maskb, "strict": strict, "rmask": rmask,
        "ttinit": ttinit.reshape(128, 1024), "ttinit4": np.tile(np.eye(4, dtype=f).reshape(1, 16), (4, 1)),
    }
    if "ATT" in stages:
        rb_ = np.asarray(inp["rel_bias"], f)

        def bucket(dist):
            n = np.maximum(dist, 0)
            nf = np.maximum(n, 16).astype(np.float32)
            large = 16 + (np.log(nf / np.float32(16)) / np.float32(math.log(2048 / 16)) * np.float32(16)).astype(np.int32)
            return np.where(n < 16, n, np.minimum(large, 31))

        def table(dist, valid):
            bk = bucket(dist)
            out = np.empty((8,) + dist.shape, f)
            for hh in range(8):
                out[hh] = np.where(valid, rb_[bk, hh], np.float32(NEGM))
            return out
        kk = np.arange(128)[:, None]
        d1 = np.arange(2432)[None, :] - kk - 384
        shared["msel"] = table(d1, d1 >= 0)
        d2 = np.arange(1408)[None, :] - kk - 384
        shared["mwin"] = table(d2, (d2 >= 0) & (d2 < 512))
        d3 = np.arange(2048)[None, :] - 16 * kk - 31
        shared["mc"] = table(d3, d3 >= 0)
        nn_ = np.arange(128)[:, None]
        jj = np.arange(32)[None, :]
        cover = ((16 * nn_ < 64 * jj + 64) & (64 * jj <= 16 * nn_ + 31)).astype(f)
        cover[127] = 0.0
        shared["cover"] = cover
        tt_ = (np.arange(16)[None, :, None] * 128 + np.arange(128)[:, None, None])
        j3 = np.arange(32)[None, None, :]
        cur = tt_ // 64
        future = 64 * j3 > tt_
        forced = ((j3 == 0) | (j3 == cur) | (j3 == cur - 1)) & ~future
        big = (np.float32(1e30) * (1.0 + j3 / 64.0)).astype(f) * np.ones_like(tt_, dtype=f)
        shared["keep"] = np.ascontiguousarray((~future & ~forced).astype(f).reshape(128, 512))
        shared["addm"] = np.ascontiguousarray(np.where(future, -big, np.where(forced, big, 0.0)).astype(f).reshape(128, 512))
        shared["wex"] = (np.arange(2048)[None, :] // 64 == np.arange(32)[:, None]).astype(f)
        wc = np.zeros((2, 2, 128, 32, 128), f)
        for kv_, nm in enumerate(("a_cmp_wk", "a_cmp_wv")):
            w = np.asarray(inp[nm], f)
            for g in range(2):
                wc[:, kv_, 64 * g:64 * g + 64, :, 64 * g:64 * g + 64] = w[:, g].transpose(0, 2, 1, 3)
        shared["wcmp"] = wc.reshape(2, 2, 128, 32 * 128)
    if "SAMP" in stages:
        if "ATT" not in stages:
            raise ValueError("SAMP needs ATT")
        P0 = 8192
        cq = np.arange(16)
        c_, q_ = cq // 4, cq % 4

        def table_s(dist_q, valid_q):
            bk = bucket(dist_q)
            out = np.empty(dist_q.shape[:-1] + (2, 16), f)
            for g in range(2):
                for x in range(16):
                    hh = 4 * g + c_[x]
                    out[..., g, x] = np.where(valid_q[..., q_[x]], rb_[bk[..., q_[x]], hh], np.float32(NEGM))
            return out
        n_abs = (np.arange(4)[None, :, None] * 128 + np.arange(128)[:, None, None])
        qq = np.arange(4)[None, None, :]
        d = P0 + qq - 16 * n_abs - 31
        shared["mcs"] = table_s(d, d >= 0).reshape(128, 128)
        j129 = np.arange(129)[None, None, :]
        covers = ((16 * n_abs < 64 * j129 + 64) & (64 * j129 <= 16 * n_abs + 31)).astype(f)
        shared["covers"] = np.ascontiguousarray(covers.reshape(128, 4 * 129))
        kpos = (np.arange(64)[None, :, None] * 128 + np.arange(128)[:, None, None])
        d = P0 + qq - kpos
        shared["msels"] = table_s(d, d >= 0).reshape(128, 64 * 32)
        wpos = P0 - 512 + n_abs
        d = P0 + qq - wpos
        shared["mwins"] = table_s(d, (d >= 0) & (d < 512)).reshape(128, 128)
        k16 = np.arange(16)[:, None, None]
        s4 = np.arange(4)[None, :, None]
        d = qq - (k16 % 4) + 0 * s4
        valid = ((k16 // 4) == s4) & (d >= 0)
        shared["msnew"] = table_s(d, valid).reshape(16, 128)
        forced = np.zeros(129, bool)
        forced[[0, 127, 128]] = True
        bigs = (np.float32(1e30) * (1.0 + np.arange(129) / 256.0)).astype(f)
        shared["keeps"] = np.tile((~forced).astype(f)[None, :], (4, 1))
        shared["adds"] = np.tile(np.where(forced, bigs, 0.0).astype(f)[None, :], (4, 1))
        shared["qsel"] = (q_[:, None] == np.arange(4)[None, :]).astype(f)
        shared["maskc"] = (c_[:, None, None] == (2 * np.arange(2)[None, :, None] + np.arange(2)[None, None, :])).astype(f).reshape(16, 4)
        ls = np.zeros((4, 2, 128), f)
        ls[:, 0, 0:64] = 1.0
        ls[:, 1, 64:128] = 1.0
        shared["lsel"] = ls.reshape(4, 256)
        sg_ = np.zeros((24, 4, 3, 128), f)
        for j in range(4):
            for br_ in range(3):
                for m_ in range(128):
                    sg_[3 * (2 * j + m_ // 64) + br_, j, br_, m_] = 1.0
        shared["selg"] = sg_.reshape(24, 12 * 128)
        shared["iota"] = np.arange(128, dtype=f)[:, None]
        shared["wag"] = np.stack([kcl(w_in[l][:, O_AG:O_AG + 24]) for l in range(2)])
        cp_ = np.asarray(inp["cache_cmp_kv"], f).reshape(2, 2560 * 128, 256)
        sp_ = np.asarray(inp["cache_sel_kv"], f).reshape(2, 2560 * 128, 256)
        for i in range(2):
            shared[f"cmp_pool{i}"] = cp_[i]
            shared[f"sel_pool{i}"] = sp_[i]
    maps = []
    xp = np.asarray(inp["x_prompt"], f)
    xs = np.asarray(inp["x_sample"], f)
    cw = np.asarray(inp["cache_win_kv"], f).reshape(2, 32, 512, 256)
    mC = np.asarray(inp["state_mlstm_C"], f)
    mn = np.asarray(inp["state_mlstm_n"], f)
    mm = np.asarray(inp["state_mlstm_m"], f)
    gS = np.asarray(inp["state_gdn_S"], f)
    gc = np.asarray(inp["state_gdn_conv"], f)
    for c in range(NCORES):
        m = dict(shared)
        sl = slice(4 * c, 4 * c + 4)
        m["xp"] = np.ascontiguousarray(xp[c])
        m["xs"] = np.ascontiguousarray(xs[sl].reshape(TS, D))
        m["cwin"] = np.ascontiguousarray(cw[:, sl])
        m["mC0"] = np.ascontiguousarray(mC[:, sl])
        m["mn0"] = np.ascontiguousarray(mn[:, sl])
        m["mm0"] = np.ascontiguousarray(mm[:, sl].transpose(0, 2, 1))
        m["gS0"] = np.ascontiguousarray(gS[:, sl])
        if "SAMP" in stages:
            m["ptab"] = np.ascontiguousarray(np.asarray(inp["page_table"], np.int32)[sl].reshape(1, 256))
        m["gcv0"] = np.ascontiguousarray(gc[:, sl].reshape(2, 4, 3, 12, 128).transpose(0, 1, 3, 4, 2))
        maps.append(m)
    return maps


def _assemble(res):
    R = res

    def cat(name):
        return [np.asarray(r[name]) for r in R]
    y_p = np.stack(cat("y_p"))
    y_s = np.concatenate([a.reshape(4, 4, D) for a in cat("y_s")])

    def kvp(name, tlen):
        return np.stack(cat(name), axis=1).reshape(2, 8, tlen, 2, 2, 64)

    def kvs(name):
        return np.concatenate([a.reshape(2, 4, 4, 256) for a in cat(name)], axis=1).reshape(2, 32, 4, 2, 2, 64)
    gc_p = np.stack([a.transpose(0, 3, 1, 2).reshape(2, 3, 1536) for a in cat("gc_p")], axis=1)
    gc_s = np.concatenate([a.transpose(0, 1, 4, 2, 3).reshape(2, 4, 3, 1536) for a in cat("gc_s")], axis=1)
    outs = [
        y_p, y_s,
        kvp("cmp_p", T), kvs("cmp_s"), kvp("sel_p", T), kvs("sel_s"),
        kvp("win_p", 512), np.concatenate(cat("win_s"), axis=1).reshape(2, 32, 512, 2, 2, 64),
        np.stack(cat("mC_p"), axis=1), np.concatenate(cat("mC_s"), axis=1),
        np.stack(cat("mn_p"), axis=1), np.concatenate(cat("mn_s"), axis=1),
        np.stack(cat("mm_p"), axis=1), np.concatenate([a.transpose(0, 2, 1) for a in cat("mm_s")], axis=1),
        np.stack(cat("gS_p"), axis=1), np.concatenate(cat("gS_s"), axis=1),
        gc_p, gc_s,
    ]
    return tuple(np.ascontiguousarray(o, dtype=np.float32) for o in outs)


STAGES = ("B1", "M", "G", "ATT", "SAMP", "OUT")
LAST = {}


def kernel(**inputs):
    nc, stats, in_names = build_program(STAGES)
    maps = _prep_inputs(inputs, STAGES)
    names = set(in_names)
    maps = [{k: v for k, v in m.items() if k in names} for m in maps]
    res = run_bass_kernel_spmd(nc, maps, core_ids=list(range(NCORES)))
    LAST["res"] = res.results
    return _assemble(res.results)
```
